# Optimizing a Trainium2 kernel written in Bass

```python
import math
import jax, jax.numpy as jnp
from jax import lax
import numpy as np

D_MODEL = 4096
BATCH = 2
SEQ = 4096
DEPTH = 2

MIX_WIDTH = D_MODEL
ATTN_WIDTH = MIX_WIDTH // 2
GMLP_WIDTH = MIX_WIDTH - ATTN_WIDTH
DIFF_HEAD_DIM = 128
DIFF_V_DIM = 2 * DIFF_HEAD_DIM
N_DIFF_HEADS = ATTN_WIDTH // DIFF_V_DIM
N_GMLP_GROUPS = 8
GMLP_GROUP_DIM = GMLP_WIDTH // N_GMLP_GROUPS
CHUNK = 128
Q_BLOCK = 128
D_FF = 4 * D_MODEL
N_BUCKETS = 32
MAX_DISTANCE = 128
NORM_EPS = 1e-6
N_MOD = 6
IN_COLS = 3 * ATTN_WIDTH + 2 * GMLP_WIDTH

kernel_name = "hybrid_diffattn_gmlp_parallel_block"


def rms_norm(x, g):
    xf = x.astype(jnp.float32)
    y = xf * lax.rsqrt(jnp.mean(xf * xf, axis=-1, keepdims=True) + NORM_EPS)
    return (y * g.astype(jnp.float32)).astype(x.dtype)


def layer_norm(x, g, b):
    xf = x.astype(jnp.float32)
    mu = jnp.mean(xf, axis=-1, keepdims=True)
    var = jnp.mean(jnp.square(xf - mu), axis=-1, keepdims=True)
    y = (xf - mu) * lax.rsqrt(var + NORM_EPS)
    return (y * g.astype(jnp.float32) + b.astype(jnp.float32)).astype(x.dtype)


def t5_bucket(rel):
    n = jnp.maximum(-rel, 0)
    max_exact = N_BUCKETS // 2
    is_small = n < max_exact
    nf = jnp.maximum(n, 1).astype(jnp.float32)
    large = max_exact + (jnp.log(nf / max_exact) / math.log(MAX_DISTANCE / max_exact)
                         * (N_BUCKETS - max_exact)).astype(jnp.int32)
    large = jnp.minimum(large, N_BUCKETS - 1)
    return jnp.where(is_small, n, large)


def diff_attention(q, k, v, rel_bias, lam):
    B, S = q.shape[0], q.shape[1]
    nb = S // Q_BLOCK
    qb_all = q.reshape(B, nb, Q_BLOCK, N_DIFF_HEADS, 2, DIFF_HEAD_DIM).transpose(1, 0, 3, 4, 2, 5)
    kt = k.transpose(0, 2, 3, 1, 4)
    vt = v.transpose(0, 2, 1, 3)
    k_pos = jnp.arange(S)
    scale = DIFF_HEAD_DIM ** -0.5

    def block(args):
        qb, bi = args
        q_pos = bi * Q_BLOCK + jnp.arange(Q_BLOCK)
        rel = k_pos[None, :] - q_pos[:, None]
        bias = jnp.transpose(rel_bias[t5_bucket(rel)], (2, 0, 1)).astype(jnp.float32)
        s = jnp.einsum('bhcqd,bhckd->bhcqk', qb, kt).astype(jnp.float32) * scale
        s = s + bias[None, :, None]
        s = jnp.where((rel <= 0)[None, None, None], s, -jnp.inf)
        p = jax.nn.softmax(s, axis=-1)
        a = p[:, :, 0] - lam * p[:, :, 1]
        return jnp.einsum('bhqk,bhkv->bhqv', a.astype(vt.dtype), vt)

    out = lax.map(block, (qb_all, jnp.arange(nb)))
    return out.transpose(1, 0, 3, 2, 4).reshape(B, S, N_DIFF_HEADS, DIFF_V_DIM)


def spatial_gating(z, w_s, b_s, v_g, v_b):
    B, S = z.shape[0], z.shape[1]
    u, v = jnp.split(z, 2, axis=-1)
    v = layer_norm(v, v_g, v_b)
    nc = S // CHUNK
    vg = v.reshape(B, nc, CHUNK, N_GMLP_GROUPS, GMLP_GROUP_DIM)
    ws = w_s * jnp.tril(jnp.ones((CHUNK, CHUNK), dtype=w_s.dtype))[None]
    mixed = jnp.einsum('gts,bnsgc->bntgc', ws, vg) + b_s.T[None, None, :, :, None]
    return u * mixed.reshape(B, S, GMLP_WIDTH)


def setup_inputs(seed: int = 0) -> dict:
    key = jax.random.key(seed)
    ks = jax.random.split(key, 24)
    f32 = jnp.float32
    L, D = DEPTH, D_MODEL

    def nrm(k, shape, s):
        return jax.random.normal(k, shape, f32) * s

    def gain(k, shape):
        return 1.0 + 0.05 * jax.random.normal(k, shape, f32)

    return {
        "x": jax.random.normal(ks[0], (BATCH, SEQ, D), f32),
        "c": jax.random.normal(ks[1], (BATCH, D), f32),
        "rel_bias": nrm(ks[2], (N_BUCKETS, N_DIFF_HEADS), 0.5),
        "w_ada": nrm(ks[3], (L, D, N_MOD * D), 0.5 * D ** -0.5),
        "b_ada": nrm(ks[4], (L, N_MOD * D), 0.01),
        "pre_mix_g": gain(ks[5], (L, D)),
        "w_in": nrm(ks[6], (L, D, IN_COLS), D ** -0.5),
        "lambda_q1": nrm(ks[7], (L, DIFF_HEAD_DIM), 0.1),
        "lambda_k1": nrm(ks[8], (L, DIFF_HEAD_DIM), 0.1),
        "lambda_q2": nrm(ks[9], (L, DIFF_HEAD_DIM), 0.1),
        "lambda_k2": nrm(ks[10], (L, DIFF_HEAD_DIM), 0.1),
        "subln_g": gain(ks[11], (L, DIFF_V_DIM)),
        "v_norm_g": gain(ks[12], (L, GMLP_WIDTH)),
        "v_norm_b": nrm(ks[13], (L, GMLP_WIDTH), 0.02),
        "w_s": nrm(ks[14], (L, N_GMLP_GROUPS, CHUNK, CHUNK), CHUNK ** -0.5),
        "b_s": gain(ks[15], (L, N_GMLP_GROUPS, CHUNK)),
        "w_out": nrm(ks[16], (L, MIX_WIDTH, D), MIX_WIDTH ** -0.5),
        "post_mix_g": gain(ks[17], (L, D)),
        "pre_mlp_g": gain(ks[18], (L, D)),
        "w_1": nrm(ks[19], (L, D, D_FF), D ** -0.5),
        "w_2": nrm(ks[20], (L, D_FF, D), D_FF ** -0.5),
        "post_mlp_g": gain(ks[21], (L, D)),
    }


def reference(x, c, rel_bias, w_ada, b_ada, pre_mix_g, w_in, lambda_q1, lambda_k1,
              lambda_q2, lambda_k2, subln_g, v_norm_g, v_norm_b, w_s, b_s, w_out,
              post_mix_g, pre_mlp_g, w_1, w_2, post_mlp_g):
    B, S = x.shape[0], x.shape[1]
    c_act = jax.nn.silu(c)
    for l in range(DEPTH):
        lambda_init = 0.8 - 0.6 * math.exp(-0.3 * l)
        mod = c_act @ w_ada[l] + b_ada[l]
        sh_a, sc_a, g_a, sh_m, sc_m, g_m = [m[:, None, :] for m in jnp.split(mod, N_MOD, axis=-1)]

        h = rms_norm(x, pre_mix_g[l]) * (1 + sc_a) + sh_a
        proj = h @ w_in[l]
        q, k, v, z = jnp.split(proj, [ATTN_WIDTH, 2 * ATTN_WIDTH, 3 * ATTN_WIDTH], axis=-1)
        q = q.reshape(B, S, N_DIFF_HEADS, 2, DIFF_HEAD_DIM)
        k = k.reshape(B, S, N_DIFF_HEADS, 2, DIFF_HEAD_DIM)
        v = v.reshape(B, S, N_DIFF_HEADS, DIFF_V_DIM)
        lam = (jnp.exp(jnp.sum(lambda_q1[l].astype(jnp.float32) * lambda_k1[l].astype(jnp.float32)))
               - jnp.exp(jnp.sum(lambda_q2[l].astype(jnp.float32) * lambda_k2[l].astype(jnp.float32)))
               + lambda_init)
        attn = diff_attention(q, k, v, rel_bias, lam)
        attn = (rms_norm(attn, subln_g[l]) * (1 - lambda_init)).reshape(B, S, ATTN_WIDTH)
        gm = spatial_gating(jax.nn.gelu(z, approximate=False), w_s[l], b_s[l],
                            v_norm_g[l], v_norm_b[l])
        y = jnp.concatenate([attn, gm], axis=-1) @ w_out[l]
        x = x + g_a * rms_norm(y, post_mix_g[l])

        h = rms_norm(x, pre_mlp_g[l]) * (1 + sc_m) + sh_m
        y = jnp.square(jax.nn.relu(h @ w_1[l])) @ w_2[l]
        x = x + g_m * rms_norm(y, post_mlp_g[l])
    return x
```

```python
import math
from contextlib import ExitStack

import numpy as np
import concourse.bass as bass
import concourse.mybir as mybir
from concourse.bass_utils import run_bass_kernel_spmd

F32 = mybir.dt.float32
BF16 = mybir.dt.bfloat16
ALU = mybir.AluOpType
AF = mybir.ActivationFunctionType
AX = mybir.AxisListType

DM = 4096
DEPTH = 2
NCORE = 8
T = 1024
NS = 8
DFF = 16384
EPS = 1e-6
SCALE = 128 ** -0.5
MASKV = -30000.0 / SCALE
ENGS = ("pe", "act", "dve", "pool", "sp")


class Prog:
    def __init__(self, nc, cap=30000, same_engine_sync=("act", "dve", "pool")):
        self.nc = nc
        self.cap = cap
        self.q = {e: [] for e in ENGS}
        self.bufs = {}
        self.dcount = {}
        self.ginc = {}
        self.ses = set(same_engine_sync)

    def _b(self, k):
        b = self.bufs.get(k)
        if b is None:
            b = self.bufs[k] = {"w": {}, "r": {}}
        return b

    def _collect(self, eng, reads, writes):
        deps = {}

        def add(ev):
            dom = ev[:2]
            if dom == ("e", eng) and eng not in self.ses:
                return
            if dom not in deps or deps[dom][2] < ev[2]:
                deps[dom] = ev

        for k in reads:
            for ev in self._b(k)["w"].values():
                add(ev)
        for k in writes:
            b = self._b(k)
            for ev in b["w"].values():
                add(ev)
            for ev in b["r"].values():
                add(ev)
        return list(deps.values())

    def _record(self, ev, reads, writes):
        dom = ev[:2]
        for k in reads:
            self._b(k)["r"][dom] = ev
        for k in writes:
            b = self._b(k)
            b["w"] = {dom: ev}
            b["r"] = {}

    def op(self, eng, emit, reads=(), writes=()):
        reads = tuple(reads)
        writes = tuple(writes)
        deps = self._collect(eng, reads, writes)
        idx = len(self.q[eng])
        self.q[eng].append({"waits": deps, "emit": emit, "kind": "op", "ms": False})
        self._record(("e", eng, idx), reads, writes)

    def dma(self, eng, group, emit, reads=(), writes=(), inc=16):
        reads = tuple(reads)
        writes = tuple(writes)
        deps = self._collect(eng, reads, writes)
        n = self.dcount.get(group, 0) + 1
        self.dcount[group] = n
        self.ginc[group] = inc
        self.q[eng].append({"waits": deps, "emit": emit, "kind": "dma", "group": group})
        self._record(("d", group, n), reads, writes)

    def barrier(self, skip_queues=("pool",), skip_groups=()):
        evs = []
        for e in ENGS:
            for i in range(len(self.q[e]) - 1, -1, -1):
                if self.q[e][i]["kind"] == "op":
                    evs.append(("e", e, i))
                    break
        for g, n in self.dcount.items():
            if g not in skip_groups:
                evs.append(("d", g, n))
        for e in ENGS:
            if e in skip_queues:
                continue
            w = [ev for ev in evs if not (ev[:2] == ("e", e) and e not in self.ses)]
            self.q[e].append({"waits": w, "emit": None, "kind": "nop"})

    def emit(self):
        nc = self.nc
        for e in ENGS:
            for ent in self.q[e]:
                for ev in ent["waits"]:
                    if ev[0] == "e":
                        self.q[ev[1]][ev[2]]["ms"] = True
        msnum = {}
        nms = {}
        for e in ENGS:
            c = 0
            for i, ent in enumerate(self.q[e]):
                if ent["kind"] == "op" and ent["ms"]:
                    msnum[(e, i)] = c
                    c += 1
            nms[e] = c
        with ExitStack() as st:
            esems = {}
            for e in ENGS:
                k = (nms[e] + self.cap - 1) // self.cap
                esems[e] = [st.enter_context(nc.semaphore(f"s_{e}{j}")) for j in range(k)]
            dsems = {g: st.enter_context(nc.semaphore(f"d_{g}")) for g in self.dcount}
            block = st.enter_context(nc.Block())

            def resolve(ev):
                if ev[0] == "e":
                    m = msnum[(ev[1], ev[2])]
                    return esems[ev[1]][m // self.cap], (m % self.cap) + 1
                return dsems[ev[1]], self.ginc[ev[1]] * ev[2]

            def run(ename, eobj):
                waited = {}
                for i, ent in enumerate(self.q[ename]):
                    for ev in ent["waits"]:
                        dom = ev[:2]
                        if waited.get(dom, -1) >= ev[2]:
                            continue
                        waited[dom] = ev[2]
                        s, v = resolve(ev)
                        eobj.wait_ge(s, v)
                    if ent["kind"] == "nop":
                        continue
                    ins = ent["emit"](eobj)
                    if ent["kind"] == "dma":
                        ins.then_inc(dsems[ent["group"]], self.ginc[ent["group"]])
                    elif ent["ms"]:
                        m = msnum[(ename, i)]
                        ins.then_inc(esems[ename][m // self.cap], 1)

            if self.q["pe"]:
                block.tensor(lambda e: run("pe", e))
            if self.q["act"]:
                block.scalar(lambda e: run("act", e))
            if self.q["dve"]:
                block.vector(lambda e: run("dve", e))
            if self.q["pool"]:
                block.gpsimd(lambda e: run("pool", e))
            if self.q["sp"]:
                block.sync(lambda e: run("sp", e))


def qblock(j, s):
    m = s // 2
    return 8 * m + j if s % 2 == 0 else 8 * m + 7 - j


def nkeys(s):
    m = s // 2
    return 8 * m + 4 if s % 2 == 0 else 8 * m + 8


def t5_bucket_np(n):
    n = np.maximum(n, 0)
    nf = np.maximum(n, 1).astype(np.float32)
    large = 16 + (np.log(nf / np.float32(16)) / np.float32(math.log(8.0)) * np.float32(16)).astype(np.int32)
    large = np.minimum(large, 31)
    return np.where(n < 16, n, large)


SB0 = 16512
SLAB_OFF = SB0
BIG_OFF = SB0 + 65536
CONST_OFF = BIG_OFF + 65536
LOCAL_OFF = CONST_OFF + 4096
LOCAL_END = 229344


class Builder:
    def __init__(self, ext_in, ext_out):
        self.nc = bass.Bass("TRN2", target_bir_lowering=False)
        self.P = Prog(self.nc)
        self.ext_in = set(ext_in)
        self.ext_out = set(ext_out)
        self.drams = {}
        self.used_in = []
        self.used_out = []
        self.nsb = 0
        self.slab_i = 0
        nc = self.nc
        self.slab = [self.sb([128, 32, 512], BF16, SLAB_OFF + i * 32768) for i in range(2)]
        self.big = self.sb([128, 32, 1024], BF16, BIG_OFF)
        self.ident_b = self.sb([128, 128], BF16, CONST_OFF)
        self.ones_b = self.sb([128, 128], BF16, CONST_OFF + 256)
        self.ident_f = self.sb([128, 128], F32, CONST_OFF + 512)
        self.small = self.sb([128, 768], F32, CONST_OFF + 1024)
        self.ps = nc.alloc_psum_tensor("ps", [128, 4096], F32)
        self.loc = LOCAL_OFF
        self.dmai = 0
        idb = self.dram("ident_bf", [128, 128], BF16)
        self.ld(self.ident_b[:], idb, "c_ident", [("c", "ident")])
        self.P.op("dve", lambda e: e.memset(self.ones_b[:], 1.0), writes=[("c", "ones")])
        self.epsc = self.sb([128, 8], F32, CONST_OFF + 512)
        self.P.op("dve", lambda e: e.memset(self.epsc[:], EPS), writes=[("c", "eps")])

    def sb(self, shape, dtype, off):
        self.nsb += 1
        return self.nc.alloc_sbuf_tensor_at(f"sb{self.nsb}", list(shape), dtype, offset=off)

    def local_reset(self):
        self.loc = LOCAL_OFF

    def lsb(self, shape, dtype):
        n = int(np.prod(shape[1:])) * (4 if dtype == F32 else 2)
        n = (n + 63) // 64 * 64
        off = self.loc
        self.loc += n
        assert self.loc <= LOCAL_END, ("local sbuf overflow", self.loc - LOCAL_END)
        self.last_off = off
        return self.sb(shape, dtype, off)

    def dram(self, name, shape, dtype):
        if name in self.drams:
            return self.drams[name]
        if name in self.ext_in:
            t = self.nc.dram_tensor(name, list(shape), dtype, kind="ExternalInput")
            self.used_in.append(name)
        elif name in self.ext_out:
            t = self.nc.dram_tensor(name, list(shape), dtype, kind="ExternalOutput")
            self.used_out.append(name)
        else:
            t = self.nc.dram_tensor(name, list(shape), dtype)
        self.drams[name] = t.ap()
        return self.drams[name]

    def ld(self, out, in_, group, writes, reads=(), eng="sp"):
        self.P.dma(eng, group, lambda e, o=out, i=in_: e.dma_start(out=o, in_=i), reads=reads, writes=writes)

    def st(self, out, in_, group, reads, writes=()):
        self.P.dma("sp", group, lambda e, o=out, i=in_: e.dma_start(out=o, in_=i), reads=reads, writes=writes)

    def mm(self, out, lhsT, rhs, start, stop, reads, writes):
        self.P.op("pe", lambda e, o=out, l=lhsT, r=rhs, a=start, b=stop: e.matmul(o, l, r, start=a, stop=b),
                  reads=reads, writes=writes)

    def tr(self, out, in_, ident, reads, writes):
        self.P.op("pe", lambda e, o=out, i=in_, d=ident: e.transpose(o, i, d), reads=reads, writes=writes)

    def act(self, out, in_, func, reads, writes, **kw):
        self.P.op("act", lambda e, o=out, i=in_, f=func, kw=kw: e.activation(out=o, in_=i, func=f, **kw),
                  reads=reads, writes=writes)

    def ts(self, out, in0, s1, s2, op0, op1, reads, writes, eng="dve"):
        if op1 is None:
            self.P.op(eng, lambda e, o=out, i=in0, a=s1, p=op0: e.tensor_scalar(out=o, in0=i, scalar1=a, scalar2=None, op0=p),
                      reads=reads, writes=writes)
        else:
            self.P.op(eng, lambda e, o=out, i=in0, a=s1, b=s2, p=op0, q=op1: e.tensor_scalar(out=o, in0=i, scalar1=a, scalar2=b, op0=p, op1=q),
                      reads=reads, writes=writes)

    def tt(self, out, in0, in1, op, reads, writes, eng="dve"):
        self.P.op(eng, lambda e, o=out, a=in0, b=in1, p=op: e.tensor_tensor(out=o, in0=a, in1=b, op=p), reads=reads, writes=writes)

    def stt(self, out, in0, scalar, in1, op0, op1, reads, writes, eng="dve"):
        self.P.op(eng, lambda e, o=out, a=in0, s=scalar, b=in1, p=op0, q=op1: e.scalar_tensor_tensor(out=o, in0=a, scalar=s, in1=b, op0=p, op1=q),
                  reads=reads, writes=writes)

    def red(self, out, in_, op, reads, writes):
        self.P.op("dve", lambda e, o=out, i=in_, p=op: e.tensor_reduce(out=o, in_=i, axis=AX.X, op=p), reads=reads, writes=writes)

    def cp(self, out, in_, reads, writes, eng="dve"):
        if eng == "act":
            self.act(out, in_, AF.Copy, reads, writes)
        else:
            self.P.op(eng, lambda e, o=out, i=in_: e.tensor_copy(out=o, in_=i), reads=reads, writes=writes)

    def rstd(self, out, ssq, n, key):
        self.act(out, ssq, AF.Sqrt, [key, ("c", "eps")], [key + ("r",)], scale=1.0 / n, bias=self.epsc[:, 0:1])
        self.P.op("dve", lambda e, o=out: e.reciprocal(out=o, in_=o), reads=[key + ("r",)], writes=[key + ("r",)])

    def bank(self, b, n=512, off=0):
        return self.ps[:, b * 512 + off:b * 512 + off + n]

    def bankb(self, b):
        return self.ps[:, b * 512:(b + 1) * 512].bitcast(BF16)

    def barrier(self):
        self.P.barrier(skip_groups=tuple(f"slab{i}_{q}" for i in range(2) for q in range(4)))

    def slab_load(self, W, r0, c0):
        i = self.slab_i % 2
        self.slab_i += 1
        buf = self.slab[i]
        src = W[r0:r0 + 4096, c0:c0 + 512].rearrange("(c p) n -> p c n", p=128)
        for q in range(4):
            self.P.dma("pool", f"slab{i}_{q}",
                       lambda e, o=buf[:, 8 * q:8 * q + 8, :], s=src[:, 8 * q:8 * q + 8, :]: e.dma_start(out=o, in_=s),
                       writes=[("slab", i, q)])
        return buf, (lambda kc, i=i: ("slab", i, kc // 8))

    def gemm_fm(self, slab, skey, actT, akey):
        for cc in range(4):
            for th in range(2):
                b = cc * 2 + th
                for kc in range(32):
                    self.mm(self.bank(b), slab[:, kc, cc * 128:(cc + 1) * 128], actT[:, kc, th * 512:(th + 1) * 512],
                            kc == 0, kc == 31, [skey(kc), akey(kc)], [("ps", b)])

    def gemm_tm(self, slab, skey, actT, akey, kcs=range(32), first=True, last=True):
        kcs = list(kcs)
        for kc in kcs:
            for tb in range(8):
                self.mm(self.bank(tb), actT[:, kc, tb * 128:(tb + 1) * 128], slab[:, kc, :],
                        first and kc == kcs[0], last and kc == kcs[-1], [skey(kc), akey(kc)], [("ps", tb)])

    def stage_modshard(self):
        self.local_reset()
        c2_d = self.dram("c_pc2", [128, 64], F32)
        wS = self.dram("w_adaS", [DM, 6144], F32)
        bS = self.dram("browS", [6144], F32)
        modS = self.dram("modS", [2, 6144], F32)
        csb = self.lsb([128, 64], F32)
        cact = self.lsb([128, 64], F32)
        rep = self.lsb([128, 32, 128], BF16)
        brow = self.lsb([128, 6144], F32)
        trow = self.lsb([128, 6144], F32)
        self.ld(csb[:], c2_d, "m_c", [("m", "csb")])
        self.ld(brow[:], bS.partition_broadcast(128), "m_br", [("m", "brow")])
        self.act(cact[:], csb[:], AF.Silu, [("m", "csb")], [("m", "cact")])
        for kc in range(32):
            for b in range(2):
                self.ts(rep[:, kc, b * 64:(b + 1) * 64], self.ones_b[:, 0:64], cact[:, kc * 2 + b:kc * 2 + b + 1], None, ALU.mult, None,
                        [("m", "cact"), ("c", "ones")], [("m", "rep", kc)])
        for j in range(12):
            slab, skey = self.slab_load(wS, 0, j * 512)
            for kc in range(32):
                self.mm(self.bank(j % 8), rep[:, kc, :], slab[:, kc, :], kc == 0, kc == 31, [skey(kc), ("m", "rep", kc)], [("ps", j % 8)])
            sl = slice(j * 512, (j + 1) * 512)
            self.tt(trow[:, sl], self.bank(j % 8), brow[:, sl], ALU.add, [("ps", j % 8), ("m", "brow")], [("m", "trow", j)])
        for b in range(2):
            self.st(modS[b:b + 1, :], trow[b * 64:b * 64 + 1, :], f"m_st{b}", [("m", "trow", j) for j in range(12)], writes=[("modS", b)])
        self.barrier()

    def stage_norm(self, l, kind):
        self.local_reset()
        has_y = kind != "pre"
        final = kind == "end" and l == DEPTH - 1
        has_next = not final
        if kind == "pre":
            xsrc = self.dram("x_c", [T, DM], F32); xsrc_key = None
        elif kind == "mid":
            xsrc = self.dram("x_c", [T, DM], F32) if l == 0 else self.dram(f"xb{l - 1}", [T, DM], F32)
            xsrc_key = None if l == 0 else f"xb{l - 1}"
            xdst = self.dram(f"xa{l}", [T, DM], F32); xdst_key = f"xa{l}"
        else:
            xsrc = self.dram(f"xa{l}", [T, DM], F32); xsrc_key = f"xa{l}"
            xdst = self.dram("out", [T, DM], F32) if final else self.dram(f"xb{l}", [T, DM], F32)
            xdst_key = "out" if final else f"xb{l}"
        if has_y:
            y_d = self.dram("y_d", [T, DM], F32)
            mr_d = self.dram(f"modraw{l}", [6, DM], F32)
            gr_d = self.dram(f"grow{l}", [4, DM], F32)
            gi = 2 if kind == "mid" else 5
            gate = self.lsb([128, DM], F32)
            yt = [self.lsb([128, DM], F32)] * 2
            self.ld(gate[:], mr_d[gi, :].partition_broadcast(128), "n_g", [("n", "gate")])
            self.ld(yt[0][:], gr_d[1 if kind == "mid" else 3, :].partition_broadcast(128), "n_y0", [("n", "y", 0)])
            self.tt(gate[:], gate[:], yt[0][:], ALU.mult, [("n", "gate"), ("n", "y", 0)], [("n", "gate")])
        xt = [self.lsb([128, DM], F32) for _ in range(2)]
        if has_next:
            nl, pi = (l, 0) if kind == "pre" else ((l, 2) if kind == "mid" else (l + 1, 0))
            mrn_d = self.dram(f"modraw{nl}", [6, DM], F32)
            grn_d = self.dram(f"grow{nl}", [4, DM], F32)
            mc = self.lsb([128, 4, 32], F32)
            rv = 0 if pi == 0 else 3
            srcs = [(pi, mrn_d[rv, :]), ((pi + 2) % 4, mrn_d[rv + 1, :]), ((pi + 3) % 4, grn_d[0 if pi == 0 else 2, :])]
            for i2, (slot, src) in enumerate(srcs):
                self.P.dma("sp", f"n_mc{i2}", lambda e, o=mc[:, slot, :], i=src.rearrange("(c p) -> p c", p=128):
                           e.dma_start(out=o, in_=i, allow_slow_non_contiguous=True), writes=[("n", "mcl", i2)])
            self.stt(mc[:, pi + 1, :], mc[:, (pi + 2) % 4, :], 1.0, mc[:, (pi + 3) % 4, :], ALU.add, ALU.mult,
                     [("n", "mcl", 1), ("n", "mcl", 2)], [("n", "mc")])
            self.ts(mc[:, pi, :], mc[:, pi, :], 1.0, None, ALU.mult, None, [("n", "mcl", 0), ("n", "mc")], [("n", "mc")])
            xn = self.lsb([128, DM], BF16)
        junk = xn if has_next else self.lsb([128, DM], BF16)
        sm = self.small
        for s in range(NS):
            x_ = xt[s % 2]; kx = ("n", "x", s % 2)
            self.ld(x_[:], xsrc[s * 128:(s + 1) * 128, :], f"n_x{s % 2}", [kx],
                    reads=[(xsrc_key, s)] if xsrc_key else [])
            if has_y:
                y_ = yt[0]; ky = ("n", "y", 0)
                self.ld(y_[:], y_d[s * 128:(s + 1) * 128, :], "n_y0", [ky], reads=[("y_d", s, j) for j in range(8)])
                k1 = ("n", "ssq1", s % 2)
                self.act(junk[:], y_[:], AF.Square, [ky], [("n", "xn"), k1], accum_out=sm[:, s % 2:s % 2 + 1])
                self.rstd(sm[:, 2 + s % 2:3 + s % 2], sm[:, s % 2:s % 2 + 1], DM, k1)
                self.stt(y_[:], y_[:], sm[:, 2 + s % 2:3 + s % 2], gate[:], ALU.mult, ALU.mult,
                         [ky, k1 + ("r",), ("n", "gate")], [ky])
                self.tt(x_[:], x_[:], y_[:], ALU.add, [kx, ky], [kx])
                self.st(xdst[s * 128:(s + 1) * 128, :], x_[:], f"n_xs{s % 2}", [kx], writes=[(xdst_key, s)])
            if has_next:
                k2 = ("n", "ssq2", s % 2)
                self.act(junk[:], x_[:], AF.Square, [kx], [("n", "xn"), k2], accum_out=sm[:, 4 + s % 2:5 + s % 2])
                self.rstd(sm[:, 6 + s % 2:7 + s % 2], sm[:, 4 + s % 2:5 + s % 2], DM, k2)
                self.ts(xn[:], x_[:], sm[:, 6 + s % 2:7 + s % 2], None, ALU.mult, None, [kx, k2 + ("r",)], [("n", "xn")])
                for g in range(4):
                    b = (s % 2) * 4 + g
                    pb = self.bankb(b)
                    for jj in range(8):
                        kc = g * 8 + jj
                        self.tr(pb[:, jj * 128:(jj + 1) * 128], xn[:, kc * 128:(kc + 1) * 128], self.ident_b[:],
                                [("n", "xn"), ("c", "ident")], [("ps", b)])
                    for jj in range(8):
                        kc = g * 8 + jj
                        o = self.big[:, kc, s * 128:(s + 1) * 128]
                        i_ = pb[:, jj * 128:(jj + 1) * 128]
                        if g % 2 == 0:
                            self.act(o, i_, AF.Identity, [("ps", b), ("n", "mc")], [("big", kc, s)],
                                     scale=mc[:, pi + 1, kc:kc + 1], bias=mc[:, pi, kc:kc + 1])
                        else:
                            self.ts(o, i_, mc[:, pi + 1, kc:kc + 1], mc[:, pi, kc:kc + 1], ALU.mult, ALU.add,
                                    [("ps", b), ("n", "mc")], [("big", kc, s)])
        self.barrier()

    def bigkey(self, kc):
        return ("bigall",)

    def stage_inproj(self, l):
        self.local_reset()
        P = self.P
        w_in = self.dram(f"w_in{l}", [DM, 10240], F32)
        wsT_d = self.dram(f"wsT{l}", [128, 8, 128], F32)
        tril_d = self.dram("trilT", [128, 128], F32)
        bs_d = self.dram(f"bs{l}", [1024], F32)
        vn_d = self.dram(f"vncol{l}", [128, 2, 16], F32)
        qT_d = self.dram(f"qT{l}", [2048, T], BF16)
        kT_d = self.dram(f"kT{l}", [2048, T], BF16)
        v_d = self.dram(f"v{l}", [T, 2048], BF16)
        gmT_d = self.dram(f"gmT{l}", [2048, T], BF16)

        vt = self.lsb([128, 8, 2048], BF16)
        uT = self.lsb([128, 4, 1024], BF16)
        stg = [self.lsb([128, 4096], BF16) for _ in range(2)]
        wsf = self.sb([128, 8, 128], F32, self.last_off)
        bsb = self.sb([128, 1024], F32, self.last_off + 4096)
        E = self.lsb([128, 16, 128], F32)
        wsb = self.lsb([128, 8, 128], BF16)
        tril = self.lsb([128, 128], F32)
        vn = self.lsb([128, 2, 16], F32)
        tmp = self.lsb([128, 512], F32)
        junk = self.lsb([128, 512], BF16)
        st1 = self.lsb([128, 32], F32)
        st2 = self.lsb([128, 32], F32)
        sv = self.lsb([128, 48], F32)
        bk = lambda kc: ("bigall",)

        self.ld(wsf[:], wsT_d, "i_ws", [("i", "wsf")])
        self.ld(tril[:], tril_d, "i_tr", [("i", "tril")])
        self.ld(bsb[:], bs_d.partition_broadcast(128), "i_bs", [("i", "bsb")])
        self.ld(vn[:], vn_d, "i_vn", [("i", "vn")])
        for g in range(8):
            self.tt(wsb[:, g, :], wsf[:, g, :], tril[:], ALU.mult, [("i", "wsf"), ("i", "tril")], [("i", "wsb", g)])
        for g in range(8):
            self.mm(self.bank(g // 4, 128, (g % 4) * 128), self.ones_b[:], wsb[:, g, :], True, True,
                    [("c", "ones"), ("i", "wsb", g)], [("ps", g // 4)])
        for cg in range(16):
            g = cg // 2
            self.stt(E[:, cg, :], self.bank(g // 4, 128, (g % 4) * 128), vn[:, 1, cg:cg + 1], bsb[:, g * 128:(g + 1) * 128],
                     ALU.mult, ALU.add, [("ps", g // 4), ("i", "vn"), ("i", "bsb")], [("i", "E", cg)])

        for j in range(4):
            slab, skey = self.slab_load(w_in, 0, 8192 + j * 512)
            self.gemm_tm(slab, skey, self.big, bk)
            for tb in range(8):
                self.act(vt[:, tb, j * 512:(j + 1) * 512], self.bank(tb), AF.Gelu, [("ps", tb)], [("i", "vt", tb, j), ("i", "st1", tb * 4 + j)],
                         accum_out=st1[:, tb * 4 + j:tb * 4 + j + 1])
                self.act(junk[:], vt[:, tb, j * 512:(j + 1) * 512], AF.Square, [("i", "vt", tb, j)], [("i", "junk"), ("i", "st2", tb * 4 + j)],
                         accum_out=st2[:, tb * 4 + j:tb * 4 + j + 1])
        k1 = [("i", "st1", i) for i in range(32)]
        k2 = [("i", "st2", i) for i in range(32)]
        ks = ("i", "sv")
        self.red(sv[:, 0:8], st1[:].rearrange("p (a b) -> p a b", b=4), ALU.add, k1, [ks])
        self.red(sv[:, 8:16], st2[:].rearrange("p (a b) -> p a b", b=4), ALU.add, k2, [ks])
        self.ts(sv[:, 0:8], sv[:, 0:8], 1.0 / 2048, None, ALU.mult, None, [ks], [ks])
        self.tt(sv[:, 16:24], sv[:, 0:8], sv[:, 0:8], ALU.mult, [ks], [ks])
        self.stt(sv[:, 8:16], sv[:, 8:16], 1.0 / 2048, sv[:, 16:24], ALU.mult, ALU.subtract, [ks], [ks])
        self.act(sv[:, 8:16], sv[:, 8:16], AF.Sqrt, [ks, ("c", "eps")], [ks], scale=1.0, bias=self.epsc[:, 0:1])
        self.P.op("dve", lambda e, o=sv[:, 8:16]: e.reciprocal(out=o, in_=o), reads=[ks], writes=[ks])
        for tb in range(8):
            self.ts(vt[:, tb, :], vt[:, tb, :], sv[:, tb:tb + 1], sv[:, 8 + tb:9 + tb], ALU.subtract, ALU.mult,
                    [ks] + [("i", "vt", tb, j) for j in range(4)], [("i", "vtn", tb)] + [("i", "vt", tb, j) for j in range(4)])

        nst = 0
        for j in range(4):
            slab, skey = self.slab_load(w_in, 0, 6144 + j * 512)
            self.gemm_fm(slab, skey, self.big, bk)
            for cc in range(4):
                for th in range(2):
                    b = cc * 2 + th
                    self.act(uT[:, cc, th * 512:(th + 1) * 512], self.bank(b), AF.Gelu, [("ps", b)], [("i", "uT", cc, th)])
            sg_ = stg[nst % 2]; ksg = ("i", "stg", nst % 2); nst += 1
            for cc in range(4):
                cg = j * 4 + cc
                g = cg // 2
                for tb in range(8):
                    b = cc * 2 + tb // 4
                    self.mm(self.bank(b, 128, (tb % 4) * 128), vt[:, tb, cg * 128:(cg + 1) * 128], wsb[:, g, :], True, True,
                            [("i", "vtn", tb), ("i", "wsb", g)], [("ps", b)])
                for tb in range(8):
                    b = cc * 2 + tb // 4
                    self.stt(tmp[:, 0:128], self.bank(b, 128, (tb % 4) * 128), vn[:, 0, cg:cg + 1], E[:, cg, :], ALU.mult, ALU.add,
                             [("ps", b), ("i", "vn"), ("i", "E", cg)], [("i", "tmp")])
                    self.tt(sg_[:, cc * 1024 + tb * 128:cc * 1024 + (tb + 1) * 128], tmp[:, 0:128], uT[:, cc, tb * 128:(tb + 1) * 128], ALU.mult,
                            [("i", "tmp"), ("i", "uT", cc, tb // 4)], [ksg])
            self.st(gmT_d[j * 512:(j + 1) * 512, :].rearrange("(c p) t -> p c t", p=128), sg_[:].rearrange("p (c t) -> p c t", c=4),
                    f"i_st{(nst - 1) % 2}", [ksg], writes=[(f"gmT{l}", j)])
        for nm, c0, dst in (("qT", 0, qT_d), ("kT", 2048, kT_d)):
            for j in range(4):
                slab, skey = self.slab_load(w_in, 0, c0 + j * 512)
                self.gemm_fm(slab, skey, self.big, bk)
                sg_ = stg[nst % 2]; ksg = ("i", "stg", nst % 2); nst += 1
                for cc in range(4):
                    for th in range(2):
                        b = cc * 2 + th
                        self.cp(sg_[:, cc * 1024 + th * 512:cc * 1024 + (th + 1) * 512], self.bank(b), [("ps", b)], [ksg],
                                eng="act" if b % 2 == 0 else "dve")
                self.st(dst[j * 512:(j + 1) * 512, :].rearrange("(c p) t -> p c t", p=128), sg_[:].rearrange("p (c t) -> p c t", c=4),
                        f"i_st{(nst - 1) % 2}", [ksg], writes=[(f"{nm}{l}", j)])
        for j in range(4):
            slab, skey = self.slab_load(w_in, 0, 4096 + j * 512)
            self.gemm_tm(slab, skey, self.big, bk)
            sg_ = stg[nst % 2]; ksg = ("i", "stg", nst % 2); nst += 1
            for tb in range(8):
                self.cp(sg_[:, tb * 512:(tb + 1) * 512], self.bank(tb), [("ps", tb)], [ksg], eng="act" if tb % 2 == 0 else "dve")
            self.st(v_d[:, j * 512:(j + 1) * 512].rearrange("(b p) n -> p b n", p=128), sg_[:].rearrange("p (b n) -> p b n", b=8),
                    f"i_st{(nst - 1) % 2}", [ksg], writes=[(f"v{l}", j)])
        self.barrier()

    def stage_attn(self, l):
        self.local_reset()
        lam_init = 0.8 - 0.6 * math.exp(-0.3 * l)
        qT_d = self.dram(f"qT{l}", [2048, T], BF16)
        kT_f = self.dram(f"kTf{l}", [2048, 4096], BF16)
        v_f = self.dram(f"vf{l}", [4096, 2048], BF16)
        braw_d = self.dram("braw", [8, NS, 128, 640], F32)
        mask_d = self.dram("maskc", [NS, 128, 640], F32)
        b31_d = self.dram("b31", [8], F32)
        lamv_d = self.dram(f"lamv{l}", [4, 128], F32)
        sg_d = self.dram(f"subg{l}", [256], F32)

        KT = [self.sb([128, 2, 4096], BF16, BIG_OFF + 32768), self.lsb([128, 2, 4096], BF16)]
        V = self.sb([128, 32, 256], BF16, BIG_OFF + 32768 + 16384)
        Q = [self.lsb([128, 2, 1024], BF16) for _ in range(2)]
        p = [[self.lsb([128, 4096], BF16) for _ in range(2)] for _ in range(2)]
        tch = [self.lsb([128, 512], BF16) for _ in range(2)]
        ach = [self.lsb([128, 512], BF16) for _ in range(2)]
        aT = [self.lsb([128, 512], BF16) for _ in range(2)]
        braw = [self.lsb([128, 640], F32)] * 2
        mask = [self.lsb([128, 640], F32)] * 2
        bp32 = self.lsb([128, 640], F32)
        lv = self.sb([128, 512], F32, self.last_off)
        bpb = [self.lsb([128, 640], BF16) for _ in range(2)]
        sgr = self.lsb([128, 256], F32)
        b31 = self.lsb([128, 8], F32)
        on = self.lsb([128, 256], BF16)
        junk = self.lsb([128, 256], BF16)
        osb = self.lsb([128, 256], F32)
        smc = self.lsb([128, 16], F32)
        sms = [self.lsb([128, 64], F32) for _ in range(2)]
        self.ld(lv[:], lamv_d.rearrange("a n -> (a n)").partition_broadcast(128), "a_lv", [("a", "lv")])
        self.ld(sgr[:], sg_d.partition_broadcast(128), "a_sg", [("a", "sgr")])
        self.ld(b31[:], b31_d.partition_broadcast(128), "a_b31", [("a", "b31")])
        kl = ("a", "lam")
        self.tt(lv[:, 0:128], lv[:, 0:128], lv[:, 128:256], ALU.mult, [("a", "lv")], [("a", "lv")])
        self.tt(lv[:, 256:384], lv[:, 256:384], lv[:, 384:512], ALU.mult, [("a", "lv")], [("a", "lv")])
        self.red(smc[:, 0:1], lv[:, 0:128], ALU.add, [("a", "lv")], [kl])
        self.red(smc[:, 1:2], lv[:, 256:384], ALU.add, [("a", "lv")], [kl])
        self.act(smc[:, 2:4], smc[:, 0:2], AF.Exp, [kl], [kl])
        self.tt(smc[:, 4:5], smc[:, 3:4], smc[:, 2:3], ALU.subtract, [kl], [kl])
        self.ts(smc[:, 4:5], smc[:, 4:5], -lam_init, None, ALU.add, None, [kl], [kl])
        self.ts(sgr[:], sgr[:], 1.0 - lam_init, None, ALU.mult, None, [("a", "sgr")], [("a", "sgr")])
        neglam = smc[:, 4:5]
        cnt = {"sb": 0, "tb": 0, "at": 0, "ch": 0}

        def load_qk(h):
            hp = h % 2
            self.ld(KT[hp][:], kT_f[h * 256:(h + 1) * 256, :].rearrange("(c p) k -> p c k", p=128), f"a_k{hp}", [("a", "KT", hp)])
            self.ld(Q[hp][:], qT_d[h * 256:(h + 1) * 256, :].rearrange("(c p) t -> p c t", p=128), f"a_q{hp}", [("a", "Q", hp)],
                    reads=[(f"qT{l}", j) for j in range(4)])

        def gen_ab(ti, h, s):
            par = ti % 2
            hp = h % 2
            sm = sms[par]
            if ti == 0:
                load_qk(0)
            if s == 4 and h + 1 < 8:
                load_qk(h + 1)
            nk = nkeys(s); w = min(5, nk); w0 = nk - w; nch = nk // 4
            self.ld(braw[par][:], braw_d[h, s], "a_br", [("a", "braw", 0)])
            self.ld(mask[par][:], mask_d[s], "a_mk", [("a", "mask", 0)])
            self.ts(bp32[:], braw[par][:], b31[:, h:h + 1], 1.0 / SCALE, ALU.subtract, ALU.mult, [("a", "braw", 0), ("a", "b31")], [("a", "bp32")])
            self.tt(bpb[par][:], bp32[:], mask[par][:], ALU.add, [("a", "bp32"), ("a", "mask", 0)], [("a", "bpb", par)])
            yield
            for pas in range(2):
                for c in range(2):
                    for ch in range(nch):
                        b = cnt["sb"] % 4
                        cnt["sb"] += 1
                        lo, hi = max(ch * 512, w0 * 128), (ch + 1) * 512
                        hasb = hi > lo
                        self.mm(self.bank(b), Q[hp][:, c, s * 128:(s + 1) * 128], KT[hp][:, c, ch * 512:(ch + 1) * 512], True, not hasb,
                                [("a", "Q", hp), ("a", "KT", hp)], [("ps", b)])
                        if hasb:
                            self.mm(self.bank(b, hi - lo, lo - ch * 512), self.ident_b[:], bpb[par][:, lo - w0 * 128:hi - w0 * 128], False, True,
                                    [("c", "ident"), ("a", "bpb", par)], [("ps", b)])
                        if pas == 0:
                            self.red(sm[:, c * 8 + ch:c * 8 + ch + 1], self.bank(b), ALU.max, [("ps", b)], [("a", "mx", par, c, ch)])
                        else:
                            self.act(p[par][c][:, ch * 512:(ch + 1) * 512], self.bank(b), AF.Exp, [("ps", b), ("a", "negm", par, c)],
                                     [("a", "p", par, c, ch), ("a", "lp", par, c, ch)], scale=SCALE, bias=sm[:, 18 + c:19 + c],
                                     accum_out=sm[:, 20 + c * 8 + ch:21 + c * 8 + ch])
                        yield
                    if pas == 0:
                        self.red(sm[:, 16 + c:17 + c], sm[:, c * 8:c * 8 + nch], ALU.max, [("a", "mx", par, c, ch) for ch in range(nch)], [("a", "m", par, c)])
                        self.ts(sm[:, 18 + c:19 + c], sm[:, 16 + c:17 + c], -SCALE, None, ALU.mult, None, [("a", "m", par, c)], [("a", "negm", par, c)])
                    else:
                        self.red(sm[:, 36 + c:37 + c], sm[:, 20 + c * 8:20 + c * 8 + nch], ALU.add, [("a", "lp", par, c, ch) for ch in range(nch)], [("a", "l", par, c)])
            kr = ("a", "r", par)
            self.P.op("dve", lambda e, o=sm[:, 38:40], i=sm[:, 36:38]: e.reciprocal(out=o, in_=i), reads=[("a", "l", par, 0), ("a", "l", par, 1)], writes=[kr])
            self.stt(sm[:, 39:40], sm[:, 39:40], neglam, sm[:, 36:37], ALU.mult, ALU.mult, [kr, kl, ("a", "l", par, 0)], [kr])
            yield

        def gen_c(ti, h, s):
            par = ti % 2
            sm = sms[par]
            kr = ("a", "r", par)
            if s == 0:
                self.ld(V[:], v_f[:, h * 256:(h + 1) * 256].rearrange("(b p) d -> p b d", p=128), "a_v", [("a", "V")])
            nk = nkeys(s); nch = nk // 4
            ob = 6 + ti % 2
            pend = None

            def pv(g, ai):
                for jj in range(4):
                    kb = g * 4 + jj
                    self.mm(self.bank(ob, 256), aT[ai][:, jj * 128:(jj + 1) * 128], V[:, kb, :], kb == 0, kb == nk - 1,
                            [("a", "aT", ai), ("a", "V")], [("ps", ob)])

            for g in range(nch):
                i = cnt["ch"] % 2
                cnt["ch"] += 1
                sl = slice(g * 512, (g + 1) * 512)
                self.stt(ach[i][:], p[par][1][:, sl], sm[:, 39:40], p[par][0][:, sl], ALU.mult, ALU.add,
                         [kr, ("a", "p", par, 0, g), ("a", "p", par, 1, g)], [("a", "ach", i)])
                tb = 4 + cnt["tb"] % 2
                cnt["tb"] += 1
                pb = self.bankb(tb)
                for jj in range(4):
                    self.tr(pb[:, jj * 128:(jj + 1) * 128], ach[i][:, jj * 128:(jj + 1) * 128], self.ident_b[:], [("a", "ach", i), ("c", "ident")], [("ps", tb)])
                ai = cnt["at"] % 2
                cnt["at"] += 1
                self.cp(aT[ai][:, 0:512], pb[:, 0:512], [("ps", tb)], [("a", "aT", ai)], eng="act")
                if pend is not None:
                    pv(*pend)
                pend = (g, ai)
                yield
            pv(*pend)
            yield
            ko = ("a", "o", par)
            self.ts(osb[:], self.bank(ob, 256), sm[:, 38:39], None, ALU.mult, None, [("ps", ob), kr], [("a", "osb")])
            self.P.op("dve", lambda e, o=junk[:], a_=osb[:], acc=sm[:, 48:49]: e.scalar_tensor_tensor(out=o, in0=a_, scalar=1.0, in1=a_, op0=ALU.mult, op1=ALU.mult, accum_out=acc),
                      reads=[("a", "osb")], writes=[("a", "junk"), ko])
            self.act(sm[:, 49:50], sm[:, 48:49], AF.Ln, [ko, ("c", "eps")], [ko + ("r",)], scale=1.0 / 256, bias=self.epsc[:, 0:1])
            self.act(sm[:, 49:50], sm[:, 49:50], AF.Exp, [ko + ("r",)], [ko + ("r",)], scale=-0.5)
            self.stt(on[:], osb[:], sm[:, 49:50], sgr[:], ALU.mult, ALU.mult, [("a", "osb"), ko + ("r",), ("a", "sgr")], [("a", "on")])
            tb = 4 + cnt["tb"] % 2
            cnt["tb"] += 1
            pb = self.bankb(tb)
            for jj in range(2):
                self.tr(pb[:, jj * 128:(jj + 1) * 128], on[:, jj * 128:(jj + 1) * 128], self.ident_b[:], [("a", "on"), ("c", "ident")], [("ps", tb)])
            for jj in range(2):
                self.cp(self.big[:, h * 2 + jj, s * 128:(s + 1) * 128], pb[:, jj * 128:(jj + 1) * 128], [("ps", tb)], [("bigw", h, jj, s)], eng="act")
            yield

        tasks = [(h, s) for h in range(8) for s in range(NS)]
        for _ in gen_ab(0, *tasks[0]):
            pass
        for ti, (h, s) in enumerate(tasks):
            gc = gen_c(ti, h, s)
            n_c = nkeys(s) // 4 + 2
            if ti + 1 < len(tasks):
                ga = gen_ab(ti + 1, *tasks[ti + 1])
                n_a = 4 * (nkeys(tasks[ti + 1][1]) // 4) + 2
            else:
                ga, n_a = iter(()), 0
            steps = max(n_a, n_c)
            ia = ic = 0
            for k in range(1, steps + 1):
                while ia < (k * n_a) // steps:
                    next(ga, None); ia += 1
                while ic < (k * n_c) // steps:
                    next(gc, None); ic += 1
            for _ in ga:
                pass
            for _ in gc:
                pass
        self.barrier()

    def evac_y(self, j, nst):
        y_d = self.dram("y_d", [T, DM], F32)
        sg_ = self.ystg[nst % 2]; ksg = ("o", "ystg", nst % 2)
        for tb in range(8):
            self.cp(sg_[:, tb * 512:(tb + 1) * 512], self.bank(tb), [("ps", tb)], [ksg], eng="act" if tb % 2 == 0 else "dve")
        self.st(y_d[:, j * 512:(j + 1) * 512].rearrange("(b p) n -> p b n", p=128), sg_[:].rearrange("p (b n) -> p b n", b=8),
                f"o_st{nst % 2}", [ksg], writes=[("y_d", s, j) for s in range(8)])

    def stage_outproj(self, l):
        self.local_reset()
        w_out = self.dram(f"w_out{l}", [DM, DM], F32)
        gmT_d = self.dram(f"gmT{l}", [2048, T], BF16)
        self.ystg = [self.lsb([128, 4096], F32) for _ in range(2)]
        for hf in range(2):
            self.ld(self.big[:, 16 + hf * 8:24 + hf * 8, :], gmT_d[hf * 1024:(hf + 1) * 1024, :].rearrange("(c p) t -> p c t", p=128),
                    f"o_gm{hf}", [("o", "gm", hf)], reads=[(f"gmT{l}", j) for j in range(4)])
        ak = lambda kc: ("bigall",) if kc < 16 else ("o", "gm", (kc - 16) // 8)
        for j in range(8):
            slab, skey = self.slab_load(w_out, 0, j * 512)
            self.gemm_tm(slab, skey, self.big, ak)
            self.evac_y(j, j)
        self.barrier()

    def stage_mlp1(self, l):
        self.local_reset()
        w1 = self.dram(f"w_1{l}", [DM, DFF], F32)
        hid_d = self.dram("hid_d", [DFF, T], BF16)
        stg = [self.lsb([128, 4096], BF16) for _ in range(2)]
        r = [self.lsb([128, 512], BF16) for _ in range(2)]
        bk = lambda kc: ("bigall",)
        ri = 0
        for j in range(32):
            slab, skey = self.slab_load(w1, 0, j * 512)
            self.gemm_fm(slab, skey, self.big, bk)
            sg_ = stg[j % 2]; ksg = ("h", "stg", j % 2)
            for cc in range(4):
                for th in range(2):
                    b = cc * 2 + th
                    r_ = r[ri % 2]; kr = ("h", "r", ri % 2); ri += 1
                    self.act(r_[:], self.bank(b), AF.Relu, [("ps", b)], [kr])
                    self.tt(sg_[:, cc * 1024 + th * 512:cc * 1024 + (th + 1) * 512], self.bank(b), r_[:], ALU.mult, [("ps", b), kr], [ksg])
            self.st(hid_d[j * 512:(j + 1) * 512, :].rearrange("(c p) t -> p c t", p=128), sg_[:].rearrange("p (c t) -> p c t", c=4),
                    f"h_st{j % 2}", [ksg], writes=[("hid", j)])
        self.barrier()

    def stage_mlp2(self, l):
        self.local_reset()
        w2 = self.dram(f"w_2{l}", [DFF, DM], F32)
        hid_d = self.dram("hid_d", [DFF, T], BF16)
        self.ystg = [self.lsb([128, 4096], F32) for _ in range(2)]
        for cg in range(8):
            for ks in range(4):
                slab, skey = self.slab_load(w2, ks * 4096, cg * 512)
                for hf in range(2):
                    r0 = ks * 4096 + hf * 2048
                    self.ld(self.big[:, hf * 16:(hf + 1) * 16, :], hid_d[r0:r0 + 2048, :].rearrange("(c p) t -> p c t", p=128),
                            f"g_h{hf}", [("g", "hid", hf)], reads=[("hid", r0 // 512 + i) for i in range(4)])
                    self.gemm_tm(slab, skey, self.big, lambda kc, hf=hf: ("g", "hid", hf), kcs=range(hf * 16, hf * 16 + 16),
                                 first=(ks == 0 and hf == 0), last=(ks == 3 and hf == 1))
            self.evac_y(cg, cg)
        self.barrier()

    def finish(self):
        self.P.barrier(skip_queues=())
        self.P.emit()
        return self.nc


def _col(v, n):
    return np.ascontiguousarray(v.reshape(n, 128).T)


_LAUNCHES = [
    dict(stages=[("modshard",)], ins=["ident_bf", "c_pc2", "w_adaS", "browS"], outs=["modS"]),
    dict(stages=[("norm", 0, "pre"), ("inproj", 0)],
         ins=["ident_bf", "modraw0", "grow0", "x_c", "w_in0", "wsT0", "trilT", "bs0", "vncol0"],
         outs=["qT0", "kT0", "v0", "gmT0"]),
    dict(stages=[("attn", 0), ("outproj", 0), ("norm", 0, "mid"), ("mlp1", 0), ("mlp2", 0), ("norm", 0, "end"), ("inproj", 1)],
         ins=["ident_bf", "qT0", "kTf0", "vf0", "braw", "maskc", "b31", "lamv0", "subg0", "w_out0", "gmT0", "modraw0", "grow0", "x_c",
              "w_10", "w_20", "modraw1", "grow1", "w_in1", "wsT1", "trilT", "bs1", "vncol1"],
         outs=["xb0", "qT1", "kT1", "v1", "gmT1"]),
    dict(stages=[("attn", 1), ("outproj", 1), ("norm", 1, "mid"), ("mlp1", 1), ("mlp2", 1), ("norm", 1, "end")],
         ins=["ident_bf", "qT1", "kTf1", "vf1", "braw", "maskc", "b31", "lamv1", "subg1", "w_out1", "gmT1", "modraw1", "grow1", "xb0",
              "w_11", "w_21"],
         outs=["out"]),
]


def _build(cfg):
    B = Builder(cfg["ins"], cfg["outs"])
    for stg in cfg["stages"]:
        getattr(B, "stage_" + stg[0])(*stg[1:])
    nc = B.finish()
    return nc, B.used_in, B.used_out


def kernel(x, c, rel_bias, w_ada, b_ada, pre_mix_g, w_in, lambda_q1, lambda_k1, lambda_q2, lambda_k2,
           subln_g, v_norm_g, v_norm_b, w_s, b_s, w_out, post_mix_g, pre_mlp_g, w_1, w_2, post_mlp_g):
    import ml_dtypes
    f = lambda a: np.asarray(a, dtype=np.float32)
    x, c, rel_bias, w_ada, b_ada, w_in, w_out, w_1, w_2 = map(f, (x, c, rel_bias, w_ada, b_ada, w_in, w_out, w_1, w_2))
    pre_mix_g, post_mix_g, pre_mlp_g, post_mlp_g = map(f, (pre_mix_g, post_mix_g, pre_mlp_g, post_mlp_g))
    lambda_q1, lambda_k1, lambda_q2, lambda_k2, subln_g, v_norm_g, v_norm_b, w_s, b_s = map(
        f, (lambda_q1, lambda_k1, lambda_q2, lambda_k2, subln_g, v_norm_g, v_norm_b, w_s, b_s))

    cores = list(range(NCORE))
    bj = [(cid // 4, cid % 4) for cid in cores]
    shared = {"ident_bf": np.eye(128, dtype=np.float32).astype(ml_dtypes.bfloat16),
              "trilT": np.ascontiguousarray(np.triu(np.ones((128, 128), np.float32))),
              "b31": np.ascontiguousarray(rel_bias[31, :])}
    for l in range(DEPTH):
        shared[f"grow{l}"] = np.ascontiguousarray(np.stack([pre_mix_g[l], post_mix_g[l], pre_mlp_g[l], post_mlp_g[l]], axis=0))
        shared[f"w_in{l}"] = w_in[l]
        shared[f"wsT{l}"] = np.ascontiguousarray(w_s[l].transpose(2, 0, 1))
        shared[f"bs{l}"] = np.ascontiguousarray(b_s[l].reshape(-1))
        shared[f"vncol{l}"] = np.ascontiguousarray(np.stack([_col(v_norm_g[l], 16), _col(v_norm_b[l], 16)], axis=1))
        shared[f"lamv{l}"] = np.ascontiguousarray(np.stack([lambda_q1[l], lambda_k1[l], lambda_q2[l], lambda_k2[l]], axis=0))
        shared[f"subg{l}"] = subln_g[l]
        shared[f"w_out{l}"] = w_out[l]
        shared[f"w_1{l}"] = w_1[l]
        shared[f"w_2{l}"] = w_2[l]
    percore = [dict() for _ in cores]
    for cid, (b, j) in zip(cores, bj):
        d = percore[cid]
        d["x_c"] = np.ascontiguousarray(np.concatenate([x[b, qblock(j, s) * 128:(qblock(j, s) + 1) * 128, :] for s in range(NS)], axis=0))
        d["c_pc2"] = np.ascontiguousarray(np.stack([_col(c[0], 32), _col(c[1], 32)], axis=2).reshape(128, 64))
        sl_ = slice(cid * 3072, (cid + 1) * 3072)
        d["w_adaS"] = np.ascontiguousarray(np.concatenate([w_ada[0][:, sl_], w_ada[1][:, sl_]], axis=1))
        d["browS"] = np.ascontiguousarray(np.concatenate([b_ada[0][sl_], b_ada[1][sl_]]))
        braw = np.zeros((8, NS, 128, 640), np.float32)
        mask = np.zeros((NS, 128, 640), np.float32)
        for s in range(NS):
            nk = nkeys(s); w = min(5, nk); w0 = nk - w
            qpos = qblock(j, s) * 128 + np.arange(128)[:, None]
            kpos = w0 * 128 + np.arange(w * 128)[None, :]
            n = qpos - kpos
            bucket = t5_bucket_np(n)
            mask[s, :, :w * 128] = np.where(n < 0, np.float32(MASKV), np.float32(0))
            braw[:, s, :, :w * 128] = rel_bias[bucket, :].transpose(2, 0, 1)
        d["braw"] = braw
        d["maskc"] = mask

    def run(cfg, extra):
        nc, used_in, used_out = _build(cfg)
        in_maps = []
        for cid in cores:
            m = {}
            for name in used_in:
                if name in extra[cid]:
                    m[name] = extra[cid][name]
                elif name in percore[cid]:
                    m[name] = percore[cid][name]
                else:
                    m[name] = shared[name]
            in_maps.append(m)
        res = run_bass_kernel_spmd(nc, in_maps, core_ids=cores)
        return [r for r in res.results]

    def gather_kv(res, l):
        ext = [dict() for _ in cores]
        for b in range(2):
            kT = np.zeros((2048, 4096), ml_dtypes.bfloat16)
            vv = np.zeros((4096, 2048), ml_dtypes.bfloat16)
            for j in range(4):
                r = res[b * 4 + j]
                for s in range(NS):
                    qb = qblock(j, s)
                    kT[:, qb * 128:(qb + 1) * 128] = r[f"kT{l}"][:, s * 128:(s + 1) * 128]
                    vv[qb * 128:(qb + 1) * 128, :] = r[f"v{l}"][s * 128:(s + 1) * 128, :]
            for j in range(4):
                ext[b * 4 + j][f"kTf{l}"] = kT
                ext[b * 4 + j][f"vf{l}"] = vv
        return ext

    r0 = run(_LAUNCHES[0], [dict() for _ in cores])
    modraw = np.zeros((2, DEPTH, 6 * DM), np.float32)
    for cid in cores:
        for b in range(2):
            for l in range(DEPTH):
                modraw[b, l, cid * 3072:(cid + 1) * 3072] = r0[cid]["modS"][b, l * 3072:(l + 1) * 3072]
    base = [dict() for _ in cores]
    for cid, (b, j) in zip(cores, bj):
        for l in range(DEPTH):
            base[cid][f"modraw{l}"] = np.ascontiguousarray(modraw[b, l].reshape(6, DM))
    r1 = run(_LAUNCHES[1], base)
    ext = gather_kv(r1, 0)
    for cid in cores:
        ext[cid].update(base[cid])
        for k in ("qT0", "gmT0"):
            ext[cid][k] = r1[cid][k]
    r2 = run(_LAUNCHES[2], ext)
    ext = gather_kv(r2, 1)
    for cid in cores:
        ext[cid].update(base[cid])
        for k in ("qT1", "gmT1", "xb0"):
            ext[cid][k] = r2[cid][k]
    r3 = run(_LAUNCHES[3], ext)
    out = np.zeros((2, 4096, DM), np.float32)
    for cid, (b, j) in zip(cores, bj):
        for s in range(NS):
            qb = qblock(j, s)
            out[b, qb * 128:(qb + 1) * 128, :] = r3[cid]["out"][s * 128:(s + 1) * 128, :]
    return out
```

```python
import math
from contextlib import ExitStack

import numpy as np
import concourse.bass as bass
import concourse.mybir as mybir
from concourse.bass_utils import run_bass_kernel_spmd

F32 = mybir.dt.float32
BF16 = mybir.dt.bfloat16
ALU = mybir.AluOpType
AF = mybir.ActivationFunctionType
AX = mybir.AxisListType

DM = 4096
DEPTH = 2
NCORE = 8
T = 1024
NS = 8
DFF = 16384
EPS = 1e-6
SCALE = 128 ** -0.5
MASKV = -30000.0 / SCALE
ENGS = ("pe", "act", "dve", "pool", "sp")


class Prog:
    def __init__(self, nc, cap=30000, same_engine_sync=("act", "dve", "pool")):
        self.nc = nc
        self.cap = cap
        self.q = {e: [] for e in ENGS}
        self.bufs = {}
        self.dcount = {}
        self.ginc = {}
        self.ses = set(same_engine_sync)

    def _b(self, k):
        b = self.bufs.get(k)
        if b is None:
            b = self.bufs[k] = {"w": {}, "r": {}}
        return b

    def _collect(self, eng, reads, writes):
        deps = {}

        def add(ev):
            dom = ev[:2]
            if dom == ("e", eng) and eng not in self.ses:
                return
            if dom not in deps or deps[dom][2] < ev[2]:
                deps[dom] = ev

        for k in reads:
            for ev in self._b(k)["w"].values():
                add(ev)
        for k in writes:
            b = self._b(k)
            for ev in b["w"].values():
                add(ev)
            for ev in b["r"].values():
                add(ev)
        return list(deps.values())

    def _record(self, ev, reads, writes):
        dom = ev[:2]
        for k in reads:
            self._b(k)["r"][dom] = ev
        for k in writes:
            b = self._b(k)
            b["w"] = {dom: ev}
            b["r"] = {}

    def op(self, eng, emit, reads=(), writes=()):
        reads = tuple(reads)
        writes = tuple(writes)
        deps = self._collect(eng, reads, writes)
        idx = len(self.q[eng])
        self.q[eng].append({"waits": deps, "emit": emit, "kind": "op", "ms": False})
        self._record(("e", eng, idx), reads, writes)

    def dma(self, eng, group, emit, reads=(), writes=(), inc=16):
        reads = tuple(reads)
        writes = tuple(writes)
        deps = self._collect(eng, reads, writes)
        n = self.dcount.get(group, 0) + 1
        self.dcount[group] = n
        self.ginc[group] = inc
        self.q[eng].append({"waits": deps, "emit": emit, "kind": "dma", "group": group})
        self._record(("d", group, n), reads, writes)

    def barrier(self, skip_queues=("pool",), skip_groups=()):
        evs = []
        for e in ENGS:
            for i in range(len(self.q[e]) - 1, -1, -1):
                if self.q[e][i]["kind"] == "op":
                    evs.append(("e", e, i))
                    break
        for g, n in self.dcount.items():
            if g not in skip_groups:
                evs.append(("d", g, n))
        for e in ENGS:
            if e in skip_queues:
                continue
            w = [ev for ev in evs if not (ev[:2] == ("e", e) and e not in self.ses)]
            self.q[e].append({"waits": w, "emit": None, "kind": "nop"})

    def emit(self):
        nc = self.nc
        for e in ENGS:
            for ent in self.q[e]:
                for ev in ent["waits"]:
                    if ev[0] == "e":
                        self.q[ev[1]][ev[2]]["ms"] = True
        msnum = {}
        nms = {}
        for e in ENGS:
            c = 0
            for i, ent in enumerate(self.q[e]):
                if ent["kind"] == "op" and ent["ms"]:
                    msnum[(e, i)] = c
                    c += 1
            nms[e] = c
        with ExitStack() as st:
            esems = {}
            for e in ENGS:
                k = (nms[e] + self.cap - 1) // self.cap
                esems[e] = [st.enter_context(nc.semaphore(f"s_{e}{j}")) for j in range(k)]
            dsems = {g: st.enter_context(nc.semaphore(f"d_{g}")) for g in self.dcount}
            block = st.enter_context(nc.Block())

            def resolve(ev):
                if ev[0] == "e":
                    m = msnum[(ev[1], ev[2])]
                    return esems[ev[1]][m // self.cap], (m % self.cap) + 1
                return dsems[ev[1]], self.ginc[ev[1]] * ev[2]

            def run(ename, eobj):
                waited = {}
                for i, ent in enumerate(self.q[ename]):
                    for ev in ent["waits"]:
                        dom = ev[:2]
                        if waited.get(dom, -1) >= ev[2]:
                            continue
                        waited[dom] = ev[2]
                        s, v = resolve(ev)
                        eobj.wait_ge(s, v)
                    if ent["kind"] == "nop":
                        continue
                    ins = ent["emit"](eobj)
                    if ent["kind"] == "dma":
                        ins.then_inc(dsems[ent["group"]], self.ginc[ent["group"]])
                    elif ent["ms"]:
                        m = msnum[(ename, i)]
                        ins.then_inc(esems[ename][m // self.cap], 1)

            if self.q["pe"]:
                block.tensor(lambda e: run("pe", e))
            if self.q["act"]:
                block.scalar(lambda e: run("act", e))
            if self.q["dve"]:
                block.vector(lambda e: run("dve", e))
            if self.q["pool"]:
                block.gpsimd(lambda e: run("pool", e))
            if self.q["sp"]:
                block.sync(lambda e: run("sp", e))


def qblock(j, s):
    m = s // 2
    return 8 * m + j if s % 2 == 0 else 8 * m + 7 - j


def nkeys(s):
    m = s // 2
    return 8 * m + 4 if s % 2 == 0 else 8 * m + 8


def t5_bucket_np(n):
    n = np.maximum(n, 0)
    nf = np.maximum(n, 1).astype(np.float32)
    large = 16 + (np.log(nf / np.float32(16)) / np.float32(math.log(8.0)) * np.float32(16)).astype(np.int32)
    large = np.minimum(large, 31)
    return np.where(n < 16, n, large)


SB0 = 16512
SLAB_OFF = SB0
BIG_OFF = SB0 + 65536
CONST_OFF = BIG_OFF + 65536
LOCAL_OFF = CONST_OFF + 4096
LOCAL_END = 229344


class Builder:
    def __init__(self, ext_in, ext_out):
        self.nc = bass.Bass("TRN2", target_bir_lowering=False)
        self.P = Prog(self.nc)
        self.ext_in = set(ext_in)
        self.ext_out = set(ext_out)
        self.drams = {}
        self.used_in = []
        self.used_out = []
        self.nsb = 0
        self.slab_i = 0
        nc = self.nc
        self.slab = [self.sb([128, 32, 512], BF16, SLAB_OFF + i * 32768) for i in range(2)]
        self.big = self.sb([128, 32, 1024], BF16, BIG_OFF)
        self.ident_b = self.sb([128, 128], BF16, CONST_OFF)
        self.ones_b = self.sb([128, 128], BF16, CONST_OFF + 256)
        self.ident_f = self.sb([128, 128], F32, CONST_OFF + 512)
        self.small = self.sb([128, 768], F32, CONST_OFF + 1024)
        self.ps = nc.alloc_psum_tensor("ps", [128, 4096], F32)
        self.loc = LOCAL_OFF
        self.dmai = 0
        idb = self.dram("ident_bf", [128, 128], BF16)
        self.ld(self.ident_b[:], idb, "c_ident", [("c", "ident")])
        self.P.op("dve", lambda e: e.memset(self.ones_b[:], 1.0), writes=[("c", "ones")])
        self.epsc = self.sb([128, 8], F32, CONST_OFF + 512)
        self.P.op("dve", lambda e: e.memset(self.epsc[:], EPS), writes=[("c", "eps")])

    def sb(self, shape, dtype, off):
        self.nsb += 1
        return self.nc.alloc_sbuf_tensor_at(f"sb{self.nsb}", list(shape), dtype, offset=off)

    def local_reset(self):
        self.loc = LOCAL_OFF

    def lsb(self, shape, dtype):
        n = int(np.prod(shape[1:])) * (4 if dtype == F32 else 2)
        n = (n + 63) // 64 * 64
        off = self.loc
        self.loc += n
        assert self.loc <= LOCAL_END, ("local sbuf overflow", self.loc - LOCAL_END)
        self.last_off = off
        return self.sb(shape, dtype, off)

    def dram(self, name, shape, dtype):
        if name in self.drams:
            return self.drams[name]
        if name in self.ext_in:
            t = self.nc.dram_tensor(name, list(shape), dtype, kind="ExternalInput")
            self.used_in.append(name)
        elif name in self.ext_out:
            t = self.nc.dram_tensor(name, list(shape), dtype, kind="ExternalOutput")
            self.used_out.append(name)
        else:
            t = self.nc.dram_tensor(name, list(shape), dtype)
        self.drams[name] = t.ap()
        return self.drams[name]

    def ld(self, out, in_, group, writes, reads=(), eng="sp"):
        self.P.dma(eng, group, lambda e, o=out, i=in_: e.dma_start(out=o, in_=i), reads=reads, writes=writes)

    def st(self, out, in_, group, reads, writes=()):
        self.P.dma("sp", group, lambda e, o=out, i=in_: e.dma_start(out=o, in_=i), reads=reads, writes=writes)

    def mm(self, out, lhsT, rhs, start, stop, reads, writes):
        self.P.op("pe", lambda e, o=out, l=lhsT, r=rhs, a=start, b=stop: e.matmul(o, l, r, start=a, stop=b),
                  reads=reads, writes=writes)

    def tr(self, out, in_, ident, reads, writes):
        self.P.op("pe", lambda e, o=out, i=in_, d=ident: e.transpose(o, i, d), reads=reads, writes=writes)

    def act(self, out, in_, func, reads, writes, **kw):
        self.P.op("act", lambda e, o=out, i=in_, f=func, kw=kw: e.activation(out=o, in_=i, func=f, **kw),
                  reads=reads, writes=writes)

    def ts(self, out, in0, s1, s2, op0, op1, reads, writes, eng="dve"):
        if op1 is None:
            self.P.op(eng, lambda e, o=out, i=in0, a=s1, p=op0: e.tensor_scalar(out=o, in0=i, scalar1=a, scalar2=None, op0=p),
                      reads=reads, writes=writes)
        else:
            self.P.op(eng, lambda e, o=out, i=in0, a=s1, b=s2, p=op0, q=op1: e.tensor_scalar(out=o, in0=i, scalar1=a, scalar2=b, op0=p, op1=q),
                      reads=reads, writes=writes)

    def tt(self, out, in0, in1, op, reads, writes, eng="dve"):
        self.P.op(eng, lambda e, o=out, a=in0, b=in1, p=op: e.tensor_tensor(out=o, in0=a, in1=b, op=p), reads=reads, writes=writes)

    def stt(self, out, in0, scalar, in1, op0, op1, reads, writes, eng="dve"):
        self.P.op(eng, lambda e, o=out, a=in0, s=scalar, b=in1, p=op0, q=op1: e.scalar_tensor_tensor(out=o, in0=a, scalar=s, in1=b, op0=p, op1=q),
                  reads=reads, writes=writes)

    def red(self, out, in_, op, reads, writes):
        self.P.op("dve", lambda e, o=out, i=in_, p=op: e.tensor_reduce(out=o, in_=i, axis=AX.X, op=p), reads=reads, writes=writes)

    def cp(self, out, in_, reads, writes, eng="dve"):
        if eng == "act":
            self.act(out, in_, AF.Copy, reads, writes)
        else:
            self.P.op(eng, lambda e, o=out, i=in_: e.tensor_copy(out=o, in_=i), reads=reads, writes=writes)

    def rstd(self, out, ssq, n, key):
        self.act(out, ssq, AF.Sqrt, [key, ("c", "eps")], [key + ("r",)], scale=1.0 / n, bias=self.epsc[:, 0:1])
        self.P.op("dve", lambda e, o=out: e.reciprocal(out=o, in_=o), reads=[key + ("r",)], writes=[key + ("r",)])

    def bank(self, b, n=512, off=0):
        return self.ps[:, b * 512 + off:b * 512 + off + n]

    def bankb(self, b):
        return self.ps[:, b * 512:(b + 1) * 512].bitcast(BF16)

    def barrier(self):
        self.P.barrier(skip_groups=tuple(f"slab{i}_{q}" for i in range(2) for q in range(4)))

    def slab_load(self, W, r0, c0):
        i = self.slab_i % 2
        self.slab_i += 1
        buf = self.slab[i]
        src = W[r0:r0 + 4096, c0:c0 + 512].rearrange("(c p) n -> p c n", p=128)
        for q in range(4):
            self.P.dma("pool", f"slab{i}_{q}",
                       lambda e, o=buf[:, 8 * q:8 * q + 8, :], s=src[:, 8 * q:8 * q + 8, :]: e.dma_start(out=o, in_=s),
                       writes=[("slab", i, q)])
        return buf, (lambda kc, i=i: ("slab", i, kc // 8))

    def gemm_fm(self, slab, skey, actT, akey):
        for cc in range(4):
            for th in range(2):
                b = cc * 2 + th
                for kc in range(32):
                    self.mm(self.bank(b), slab[:, kc, cc * 128:(cc + 1) * 128], actT[:, kc, th * 512:(th + 1) * 512],
                            kc == 0, kc == 31, [skey(kc), akey(kc)], [("ps", b)])

    def gemm_tm(self, slab, skey, actT, akey, kcs=range(32), first=True, last=True):
        kcs = list(kcs)
        for kc in kcs:
            for tb in range(8):
                self.mm(self.bank(tb), actT[:, kc, tb * 128:(tb + 1) * 128], slab[:, kc, :],
                        first and kc == kcs[0], last and kc == kcs[-1], [skey(kc), akey(kc)], [("ps", tb)])

    def stage_modshard(self):
        self.local_reset()
        c2_d = self.dram("c_pc2", [128, 64], F32)
        wS = self.dram("w_adaS", [DM, 6144], F32)
        bS = self.dram("browS", [6144], F32)
        modS = self.dram("modS", [2, 6144], F32)
        csb = self.lsb([128, 64], F32)
        cact = self.lsb([128, 64], F32)
        rep = self.lsb([128, 32, 128], BF16)
        brow = self.lsb([128, 6144], F32)
        trow = self.lsb([128, 6144], F32)
        self.ld(csb[:], c2_d, "m_c", [("m", "csb")])
        self.ld(brow[:], bS.partition_broadcast(128), "m_br", [("m", "brow")])
        self.act(cact[:], csb[:], AF.Silu, [("m", "csb")], [("m", "cact")])
        for kc in range(32):
            for b in range(2):
                self.ts(rep[:, kc, b * 64:(b + 1) * 64], self.ones_b[:, 0:64], cact[:, kc * 2 + b:kc * 2 + b + 1], None, ALU.mult, None,
                        [("m", "cact"), ("c", "ones")], [("m", "rep", kc)])
        for j in range(12):
            slab, skey = self.slab_load(wS, 0, j * 512)
            for kc in range(32):
                self.mm(self.bank(j % 8), rep[:, kc, :], slab[:, kc, :], kc == 0, kc == 31, [skey(kc), ("m", "rep", kc)], [("ps", j % 8)])
            sl = slice(j * 512, (j + 1) * 512)
            self.tt(trow[:, sl], self.bank(j % 8), brow[:, sl], ALU.add, [("ps", j % 8), ("m", "brow")], [("m", "trow", j)])
        for b in range(2):
            self.st(modS[b:b + 1, :], trow[b * 64:b * 64 + 1, :], f"m_st{b}", [("m", "trow", j) for j in range(12)], writes=[("modS", b)])
        self.barrier()

    def stage_norm(self, l, kind):
        self.local_reset()
        has_y = kind != "pre"
        final = kind == "end" and l == DEPTH - 1
        has_next = not final
        if kind == "pre":
            xsrc = self.dram("x_c", [T, DM], F32); xsrc_key = None
        elif kind == "mid":
            xsrc = self.dram("x_c", [T, DM], F32) if l == 0 else self.dram(f"xb{l - 1}", [T, DM], F32)
            xsrc_key = None if l == 0 else f"xb{l - 1}"
            xdst = self.dram(f"xa{l}", [T, DM], F32); xdst_key = f"xa{l}"
        else:
            xsrc = self.dram(f"xa{l}", [T, DM], F32); xsrc_key = f"xa{l}"
            xdst = self.dram("out", [T, DM], F32) if final else self.dram(f"xb{l}", [T, DM], F32)
            xdst_key = "out" if final else f"xb{l}"
        if has_y:
            y_d = self.dram("y_d", [T, DM], F32)
            mr_d = self.dram(f"modraw{l}", [6, DM], F32)
            gr_d = self.dram(f"grow{l}", [4, DM], F32)
            gi = 2 if kind == "mid" else 5
            gate = self.lsb([128, DM], F32)
            yt = [self.lsb([128, DM], F32)] * 2
            self.ld(gate[:], mr_d[gi, :].partition_broadcast(128), "n_g", [("n", "gate")])
            self.ld(yt[0][:], gr_d[1 if kind == "mid" else 3, :].partition_broadcast(128), "n_y0", [("n", "y", 0)])
            self.tt(gate[:], gate[:], yt[0][:], ALU.mult, [("n", "gate"), ("n", "y", 0)], [("n", "gate")])
        xt = [self.lsb([128, DM], F32) for _ in range(2)]
        if has_next:
            nl, pi = (l, 0) if kind == "pre" else ((l, 2) if kind == "mid" else (l + 1, 0))
            mrn_d = self.dram(f"modraw{nl}", [6, DM], F32)
            grn_d = self.dram(f"grow{nl}", [4, DM], F32)
            mc = self.lsb([128, 4, 32], F32)
            rv = 0 if pi == 0 else 3
            srcs = [(pi, mrn_d[rv, :]), ((pi + 2) % 4, mrn_d[rv + 1, :]), ((pi + 3) % 4, grn_d[0 if pi == 0 else 2, :])]
            for i2, (slot, src) in enumerate(srcs):
                self.P.dma("sp", f"n_mc{i2}", lambda e, o=mc[:, slot, :], i=src.rearrange("(c p) -> p c", p=128):
                           e.dma_start(out=o, in_=i, allow_slow_non_contiguous=True), writes=[("n", "mcl", i2)])
            self.stt(mc[:, pi + 1, :], mc[:, (pi + 2) % 4, :], 1.0, mc[:, (pi + 3) % 4, :], ALU.add, ALU.mult,
                     [("n", "mcl", 1), ("n", "mcl", 2)], [("n", "mc")])
            self.ts(mc[:, pi, :], mc[:, pi, :], 1.0, None, ALU.mult, None, [("n", "mcl", 0), ("n", "mc")], [("n", "mc")])
            xn = self.lsb([128, DM], BF16)
        junk = xn if has_next else self.lsb([128, DM], BF16)
        sm = self.small
        for s in range(NS):
            x_ = xt[s % 2]; kx = ("n", "x", s % 2)
            self.ld(x_[:], xsrc[s * 128:(s + 1) * 128, :], f"n_x{s % 2}", [kx],
                    reads=[(xsrc_key, s)] if xsrc_key else [])
            if has_y:
                y_ = yt[0]; ky = ("n", "y", 0)
                self.ld(y_[:], y_d[s * 128:(s + 1) * 128, :], "n_y0", [ky], reads=[("y_d", s, j) for j in range(8)])
                k1 = ("n", "ssq1", s % 2)
                self.act(junk[:], y_[:], AF.Square, [ky], [("n", "xn"), k1], accum_out=sm[:, s % 2:s % 2 + 1])
                self.rstd(sm[:, 2 + s % 2:3 + s % 2], sm[:, s % 2:s % 2 + 1], DM, k1)
                self.stt(y_[:], y_[:], sm[:, 2 + s % 2:3 + s % 2], gate[:], ALU.mult, ALU.mult,
                         [ky, k1 + ("r",), ("n", "gate")], [ky])
                self.tt(x_[:], x_[:], y_[:], ALU.add, [kx, ky], [kx])
                self.st(xdst[s * 128:(s + 1) * 128, :], x_[:], f"n_xs{s % 2}", [kx], writes=[(xdst_key, s)])
            if has_next:
                k2 = ("n", "ssq2", s % 2)
                self.act(junk[:], x_[:], AF.Square, [kx], [("n", "xn"), k2], accum_out=sm[:, 4 + s % 2:5 + s % 2])
                self.rstd(sm[:, 6 + s % 2:7 + s % 2], sm[:, 4 + s % 2:5 + s % 2], DM, k2)
                self.ts(xn[:], x_[:], sm[:, 6 + s % 2:7 + s % 2], None, ALU.mult, None, [kx, k2 + ("r",)], [("n", "xn")])
                for g in range(4):
                    b = (s % 2) * 4 + g
                    pb = self.bankb(b)
                    for jj in range(8):
                        kc = g * 8 + jj
                        self.tr(pb[:, jj * 128:(jj + 1) * 128], xn[:, kc * 128:(kc + 1) * 128], self.ident_b[:],
                                [("n", "xn"), ("c", "ident")], [("ps", b)])
                    for jj in range(8):
                        kc = g * 8 + jj
                        o = self.big[:, kc, s * 128:(s + 1) * 128]
                        i_ = pb[:, jj * 128:(jj + 1) * 128]
                        if g % 2 == 0:
                            self.act(o, i_, AF.Identity, [("ps", b), ("n", "mc")], [("big", kc, s)],
                                     scale=mc[:, pi + 1, kc:kc + 1], bias=mc[:, pi, kc:kc + 1])
                        else:
                            self.ts(o, i_, mc[:, pi + 1, kc:kc + 1], mc[:, pi, kc:kc + 1], ALU.mult, ALU.add,
                                    [("ps", b), ("n", "mc")], [("big", kc, s)])
        self.barrier()

    def bigkey(self, kc):
        return ("bigall",)

    def stage_inproj(self, l):
        self.local_reset()
        P = self.P
        w_in = self.dram(f"w_in{l}", [DM, 10240], F32)
        wsT_d = self.dram(f"wsT{l}", [128, 8, 128], F32)
        tril_d = self.dram("trilT", [128, 128], F32)
        bs_d = self.dram(f"bs{l}", [1024], F32)
        vn_d = self.dram(f"vncol{l}", [128, 2, 16], F32)
        qT_d = self.dram(f"qT{l}", [2048, T], BF16)
        kT_d = self.dram(f"kT{l}", [2048, T], BF16)
        v_d = self.dram(f"v{l}", [T, 2048], BF16)
        gmT_d = self.dram(f"gmT{l}", [2048, T], BF16)

        vt = self.lsb([128, 8, 2048], BF16)
        uT = self.lsb([128, 4, 1024], BF16)
        stg = [self.lsb([128, 4096], BF16) for _ in range(2)]
        wsf = self.sb([128, 8, 128], F32, self.last_off)
        bsb = self.sb([128, 1024], F32, self.last_off + 4096)
        E = self.lsb([128, 16, 128], F32)
        wsb = self.lsb([128, 8, 128], BF16)
        tril = self.lsb([128, 128], F32)
        vn = self.lsb([128, 2, 16], F32)
        tmp = self.lsb([128, 512], F32)
        junk = self.lsb([128, 512], BF16)
        st1 = self.lsb([128, 32], F32)
        st2 = self.lsb([128, 32], F32)
        sv = self.lsb([128, 48], F32)
        bk = lambda kc: ("bigall",)

        self.ld(wsf[:], wsT_d, "i_ws", [("i", "wsf")])
        self.ld(tril[:], tril_d, "i_tr", [("i", "tril")])
        self.ld(bsb[:], bs_d.partition_broadcast(128), "i_bs", [("i", "bsb")])
        self.ld(vn[:], vn_d, "i_vn", [("i", "vn")])
        for g in range(8):
            self.tt(wsb[:, g, :], wsf[:, g, :], tril[:], ALU.mult, [("i", "wsf"), ("i", "tril")], [("i", "wsb", g)])
        for g in range(8):
            self.mm(self.bank(g // 4, 128, (g % 4) * 128), self.ones_b[:], wsb[:, g, :], True, True,
                    [("c", "ones"), ("i", "wsb", g)], [("ps", g // 4)])
        for cg in range(16):
            g = cg // 2
            self.stt(E[:, cg, :], self.bank(g // 4, 128, (g % 4) * 128), vn[:, 1, cg:cg + 1], bsb[:, g * 128:(g + 1) * 128],
                     ALU.mult, ALU.add, [("ps", g // 4), ("i", "vn"), ("i", "bsb")], [("i", "E", cg)])

        for j in range(4):
            slab, skey = self.slab_load(w_in, 0, 8192 + j * 512)
            self.gemm_tm(slab, skey, self.big, bk)
            for tb in range(8):
                self.act(vt[:, tb, j * 512:(j + 1) * 512], self.bank(tb), AF.Gelu, [("ps", tb)], [("i", "vt", tb, j), ("i", "st1", tb * 4 + j)],
                         accum_out=st1[:, tb * 4 + j:tb * 4 + j + 1])
                self.act(junk[:], vt[:, tb, j * 512:(j + 1) * 512], AF.Square, [("i", "vt", tb, j)], [("i", "junk"), ("i", "st2", tb * 4 + j)],
                         accum_out=st2[:, tb * 4 + j:tb * 4 + j + 1])
        k1 = [("i", "st1", i) for i in range(32)]
        k2 = [("i", "st2", i) for i in range(32)]
        ks = ("i", "sv")
        self.red(sv[:, 0:8], st1[:].rearrange("p (a b) -> p a b", b=4), ALU.add, k1, [ks])
        self.red(sv[:, 8:16], st2[:].rearrange("p (a b) -> p a b", b=4), ALU.add, k2, [ks])
        self.ts(sv[:, 0:8], sv[:, 0:8], 1.0 / 2048, None, ALU.mult, None, [ks], [ks])
        self.tt(sv[:, 16:24], sv[:, 0:8], sv[:, 0:8], ALU.mult, [ks], [ks])
        self.stt(sv[:, 8:16], sv[:, 8:16], 1.0 / 2048, sv[:, 16:24], ALU.mult, ALU.subtract, [ks], [ks])
        self.act(sv[:, 8:16], sv[:, 8:16], AF.Sqrt, [ks, ("c", "eps")], [ks], scale=1.0, bias=self.epsc[:, 0:1])
        self.P.op("dve", lambda e, o=sv[:, 8:16]: e.reciprocal(out=o, in_=o), reads=[ks], writes=[ks])
        for tb in range(8):
            self.ts(vt[:, tb, :], vt[:, tb, :], sv[:, tb:tb + 1], sv[:, 8 + tb:9 + tb], ALU.subtract, ALU.mult,
                    [ks] + [("i", "vt", tb, j) for j in range(4)], [("i", "vtn", tb)] + [("i", "vt", tb, j) for j in range(4)])

        nst = 0
        for j in range(4):
            slab, skey = self.slab_load(w_in, 0, 6144 + j * 512)
            self.gemm_fm(slab, skey, self.big, bk)
            for cc in range(4):
                for th in range(2):
                    b = cc * 2 + th
                    self.act(uT[:, cc, th * 512:(th + 1) * 512], self.bank(b), AF.Gelu, [("ps", b)], [("i", "uT", cc, th)])
            sg_ = stg[nst % 2]; ksg = ("i", "stg", nst % 2); nst += 1
            for cc in range(4):
                cg = j * 4 + cc
                g = cg // 2
                for tb in range(8):
                    b = cc * 2 + tb // 4
                    self.mm(self.bank(b, 128, (tb % 4) * 128), vt[:, tb, cg * 128:(cg + 1) * 128], wsb[:, g, :], True, True,
                            [("i", "vtn", tb), ("i", "wsb", g)], [("ps", b)])
                for tb in range(8):
                    b = cc * 2 + tb // 4
                    self.stt(tmp[:, 0:128], self.bank(b, 128, (tb % 4) * 128), vn[:, 0, cg:cg + 1], E[:, cg, :], ALU.mult, ALU.add,
                             [("ps", b), ("i", "vn"), ("i", "E", cg)], [("i", "tmp")])
                    self.tt(sg_[:, cc * 1024 + tb * 128:cc * 1024 + (tb + 1) * 128], tmp[:, 0:128], uT[:, cc, tb * 128:(tb + 1) * 128], ALU.mult,
                            [("i", "tmp"), ("i", "uT", cc, tb // 4)], [ksg])
            self.st(gmT_d[j * 512:(j + 1) * 512, :].rearrange("(c p) t -> p c t", p=128), sg_[:].rearrange("p (c t) -> p c t", c=4),
                    f"i_st{(nst - 1) % 2}", [ksg], writes=[(f"gmT{l}", j)])
        for nm, c0, dst in (("qT", 0, qT_d), ("kT", 2048, kT_d)):
            for j in range(4):
                slab, skey = self.slab_load(w_in, 0, c0 + j * 512)
                self.gemm_fm(slab, skey, self.big, bk)
                sg_ = stg[nst % 2]; ksg = ("i", "stg", nst % 2); nst += 1
                for cc in range(4):
                    for th in range(2):
                        b = cc * 2 + th
                        self.cp(sg_[:, cc * 1024 + th * 512:cc * 1024 + (th + 1) * 512], self.bank(b), [("ps", b)], [ksg],
                                eng="act" if b % 2 == 0 else "dve")
                self.st(dst[j * 512:(j + 1) * 512, :].rearrange("(c p) t -> p c t", p=128), sg_[:].rearrange("p (c t) -> p c t", c=4),
                        f"i_st{(nst - 1) % 2}", [ksg], writes=[(f"{nm}{l}", j)])
        for j in range(4):
            slab, skey = self.slab_load(w_in, 0, 4096 + j * 512)
            self.gemm_tm(slab, skey, self.big, bk)
            sg_ = stg[nst % 2]; ksg = ("i", "stg", nst % 2); nst += 1
            for tb in range(8):
                self.cp(sg_[:, tb * 512:(tb + 1) * 512], self.bank(tb), [("ps", tb)], [ksg], eng="act" if tb % 2 == 0 else "dve")
            self.st(v_d[:, j * 512:(j + 1) * 512].rearrange("(b p) n -> p b n", p=128), sg_[:].rearrange("p (b n) -> p b n", b=8),
                    f"i_st{(nst - 1) % 2}", [ksg], writes=[(f"v{l}", j)])
        self.barrier()

    def stage_attn(self, l):
        self.local_reset()
        lam_init = 0.8 - 0.6 * math.exp(-0.3 * l)
        qT_d = self.dram(f"qT{l}", [2048, T], BF16)
        kT_f = self.dram(f"kTf{l}", [2048, 4096], BF16)
        v_f = self.dram(f"vf{l}", [4096, 2048], BF16)
        braw_d = self.dram("braw", [8, NS, 128, 640], F32)
        mask_d = self.dram("maskc", [NS, 128, 640], F32)
        b31_d = self.dram("b31", [8], F32)
        rbT_d = self.dram("rbT", [256], F32)
        lamv_d = self.dram(f"lamv{l}", [4, 128], F32)
        sg_d = self.dram(f"subg{l}", [256], F32)

        KT = [self.sb([128, 2, 4096], BF16, BIG_OFF + 32768), self.lsb([128, 2, 4096], BF16)]
        V = self.sb([128, 32, 256], BF16, BIG_OFF + 32768 + 16384)
        Q = [self.lsb([128, 2, 1024], BF16) for _ in range(2)]
        p = [[self.lsb([128, 4096], BF16) for _ in range(2)] for _ in range(2)]
        ksq = [self.lsb([128, 512], BF16) for _ in range(2)]
        qsq = self.lsb([128, 2, 128], BF16)
        ach = [self.lsb([128, 512], BF16) for _ in range(2)]
        aT = [self.lsb([128, 512], BF16) for _ in range(2)]
        braw = [self.lsb([128, 640], F32)] * 2
        mask = [self.lsb([128, 640], F32)] * 2
        bp32 = braw[0]
        lv = self.sb([128, 512], F32, self.last_off - 2560)
        bpb = [self.lsb([128, 640], BF16) for _ in range(2)]
        sgr = self.lsb([128, 256], F32)
        b31 = self.lsb([128, 8], F32)
        on = self.lsb([128, 256], BF16)
        junk = on
        osb = self.lsb([128, 256], F32)
        smc = self.lsb([128, 16], F32)
        nbm = self.lsb([128, 8], F32)
        kmx = [self.lsb([128, 24], F32) for _ in range(2)]
        sms = [self.lsb([128, 64], F32) for _ in range(2)]
        self.ld(lv[:], lamv_d.rearrange("a n -> (a n)").partition_broadcast(128), "a_lv", [("a", "lv")])
        self.ld(sgr[:], sg_d.partition_broadcast(128), "a_sg", [("a", "sgr")])
        self.ld(b31[:], b31_d.partition_broadcast(128), "a_b31", [("a", "b31")])
        kl = ("a", "lam")
        self.tt(lv[:, 0:128], lv[:, 0:128], lv[:, 128:256], ALU.mult, [("a", "lv")], [("a", "lv")])
        self.tt(lv[:, 256:384], lv[:, 256:384], lv[:, 384:512], ALU.mult, [("a", "lv")], [("a", "lv")])
        self.red(smc[:, 0:1], lv[:, 0:128], ALU.add, [("a", "lv")], [kl])
        self.red(smc[:, 1:2], lv[:, 256:384], ALU.add, [("a", "lv")], [kl])
        self.act(smc[:, 2:4], smc[:, 0:2], AF.Exp, [kl], [kl])
        self.tt(smc[:, 4:5], smc[:, 3:4], smc[:, 2:3], ALU.subtract, [kl], [kl])
        self.ts(smc[:, 4:5], smc[:, 4:5], -lam_init, None, ALU.add, None, [kl], [kl])
        self.ts(sgr[:], sgr[:], 1.0 - lam_init, None, ALU.mult, None, [("a", "sgr")], [("a", "sgr")])
        neglam = smc[:, 4:5]
        self.ld(mask[0][:, 0:256], rbT_d.partition_broadcast(128), "a_rb", [("a", "mask", 0)])
        self.red(nbm[:], mask[0][:, 0:256].rearrange("p (h b) -> p h b", b=32), ALU.max, [("a", "mask", 0)], [("a", "nbm")])
        self.tt(nbm[:], nbm[:], b31[:], ALU.subtract, [("a", "nbm"), ("a", "b31")], [("a", "nbm")])
        self.ts(nbm[:], nbm[:], 0.0, -1.0, ALU.max, ALU.mult, [("a", "nbm")], [("a", "nbm")])
        cnt = {"sb": 0, "tb": 0, "at": 0, "ch": 0, "ks": 0}
        kmg = [None]

        tasks = [(h, s) for h in range(8) for s in range(NS)]

        def load_qk(h):
            hp = h % 2
            self.ld(KT[hp][:], kT_f[h * 256:(h + 1) * 256, :].rearrange("(c p) k -> p c k", p=128), f"a_k{hp}", [("a", "KT", hp)])
            self.ld(Q[hp][:], qT_d[h * 256:(h + 1) * 256, :].rearrange("(c p) t -> p c t", p=128), f"a_q{hp}", [("a", "Q", hp)],
                    reads=[(f"qT{l}", j) for j in range(4)])

        def kmaxprep(h):
            hp = h % 2
            km = kmx[hp]
            for c in range(2):
                for ch in range(8):
                    i = cnt["ks"] % 2
                    cnt["ks"] += 1
                    self.tt(ksq[i][:], KT[hp][:, c, ch * 512:(ch + 1) * 512], KT[hp][:, c, ch * 512:(ch + 1) * 512], ALU.mult,
                            [("a", "KT", hp)], [("a", "ksq", i)])
                    self.mm(self.bank(3), self.ones_b[:], ksq[i][:], True, True, [("c", "ones"), ("a", "ksq", i)], [("ps", 3)])
                    self.red(km[:, c * 8 + ch:c * 8 + ch + 1], self.bank(3), ALU.max, [("ps", 3)], [("a", "kmp", hp, c, ch)])
                    yield
                self.red(km[:, 16 + c:17 + c], km[:, c * 8:c * 8 + 8], ALU.max, [("a", "kmp", hp, c, ch) for ch in range(8)], [("a", "kmax", hp)])

        def prep(ti, h, s):
            par = ti % 2
            hp = h % 2
            sm = sms[par]
            km = kmx[hp]
            self.ld(braw[par][:], braw_d[h, s], "a_br", [("a", "braw", 0)])
            self.ld(mask[par][:], mask_d[s], "a_mk", [("a", "mask", 0)])
            self.ts(braw[par][:], braw[par][:], b31[:, h:h + 1], 1.0 / SCALE, ALU.subtract, ALU.mult, [("a", "braw", 0), ("a", "b31")], [("a", "braw", 0)])
            self.tt(bpb[par][:], braw[par][:], mask[par][:], ALU.add, [("a", "braw", 0), ("a", "mask", 0)], [("a", "bpb", par)])
            kb_ = ("a", "bnd", par)
            yield
            self.tt(qsq[:], Q[hp][:, :, s * 128:(s + 1) * 128], Q[hp][:, :, s * 128:(s + 1) * 128], ALU.mult, [("a", "Q", hp)], [("a", "qsq")])
            pb3 = self.bankb(3)
            for c in range(2):
                self.tr(pb3[:, c * 128:(c + 1) * 128], qsq[:, c, :], self.ident_b[:], [("a", "qsq"), ("c", "ident")], [("ps", 3)])
            self.red(sm[:, 16:18], pb3[:, 0:256].rearrange("p (c d) -> p c d", c=2), ALU.add, [("ps", 3)], [kb_])
            self.tt(sm[:, 16:18], sm[:, 16:18], km[:, 16:18], ALU.mult, [kb_, ("a", "kmax", hp)], [kb_])
            yield
            self.act(sm[:, 16:18], sm[:, 16:18], AF.Ln, [kb_], [kb_])
            self.act(sm[:, 16:18], sm[:, 16:18], AF.Exp, [kb_], [kb_], scale=0.5)
            yield
            self.ts(sm[:, 18:20], sm[:, 16:18], -SCALE * 1.02, nbm[:, h:h + 1], ALU.mult, ALU.add, [kb_, ("a", "nbm")], [("a", "negm", par, 0), ("a", "negm", par, 1)])

        def gen_ab(ti, h, s):
            par = ti % 2
            hp = h % 2
            sm = sms[par]
            if ti == 0:
                load_qk(0)
                for _ in kmaxprep(0):
                    pass
                for _ in prep(0, h, s):
                    pass
            if s == 4 and h + 1 < 8:
                load_qk(h + 1)
            if s == 5 and h + 1 < 8:
                kmg[0] = kmaxprep(h + 1)
            if ti + 1 < len(tasks):
                h2, s2 = tasks[ti + 1]
                if s2 == 0 and kmg[0] is not None:
                    for _ in kmg[0]:
                        pass
                    kmg[0] = None
                pg = prep(ti + 1, h2, s2)
            else:
                pg = iter(())
            nk = nkeys(s); w = min(5, nk); w0 = nk - w; nch = nk // 4
            yield
            for c in range(2):
                for ch in range(nch):
                    b = cnt["sb"] % 3
                    cnt["sb"] += 1
                    lo, hi = max(ch * 512, w0 * 128), (ch + 1) * 512
                    hasb = hi > lo
                    self.mm(self.bank(b), Q[hp][:, c, s * 128:(s + 1) * 128], KT[hp][:, c, ch * 512:(ch + 1) * 512], True, not hasb,
                            [("a", "Q", hp), ("a", "KT", hp)], [("ps", b)])
                    if hasb:
                        self.mm(self.bank(b, hi - lo, lo - ch * 512), self.ident_b[:], bpb[par][:, lo - w0 * 128:hi - w0 * 128], False, True,
                                [("c", "ident"), ("a", "bpb", par)], [("ps", b)])
                    self.act(p[par][c][:, ch * 512:(ch + 1) * 512], self.bank(b), AF.Exp, [("ps", b), ("a", "negm", par, c)],
                             [("a", "p", par, c, ch), ("a", "lp", par, c, ch)], scale=SCALE, bias=sm[:, 18 + c:19 + c],
                             accum_out=sm[:, 20 + c * 8 + ch:21 + c * 8 + ch])
                    if kmg[0] is not None:
                        next(kmg[0], None)
                    next(pg, None)
                    yield
                self.red(sm[:, 36 + c:37 + c], sm[:, 20 + c * 8:20 + c * 8 + nch], ALU.add, [("a", "lp", par, c, ch) for ch in range(nch)], [("a", "l", par, c)])
            for _ in pg:
                pass
            kr = ("a", "r", par)
            self.P.op("dve", lambda e, o=sm[:, 38:40], i=sm[:, 36:38]: e.reciprocal(out=o, in_=i), reads=[("a", "l", par, 0), ("a", "l", par, 1)], writes=[kr])
            self.stt(sm[:, 39:40], sm[:, 39:40], neglam, sm[:, 36:37], ALU.mult, ALU.mult, [kr, kl, ("a", "l", par, 0)], [kr])
            yield

        def gen_c(ti, h, s):
            par = ti % 2
            sm = sms[par]
            kr = ("a", "r", par)
            if s == 0:
                self.ld(V[:], v_f[:, h * 256:(h + 1) * 256].rearrange("(b p) d -> p b d", p=128), "a_v", [("a", "V")])
            nk = nkeys(s); nch = nk // 4
            ob = 6 + ti % 2
            pend = None

            def pv(g, ai):
                for jj in range(4):
                    kb = g * 4 + jj
                    self.mm(self.bank(ob, 256), aT[ai][:, jj * 128:(jj + 1) * 128], V[:, kb, :], kb == 0, kb == nk - 1,
                            [("a", "aT", ai), ("a", "V")], [("ps", ob)])

            for g in range(nch):
                i = cnt["ch"] % 2
                cnt["ch"] += 1
                sl = slice(g * 512, (g + 1) * 512)
                self.stt(ach[i][:], p[par][1][:, sl], sm[:, 39:40], p[par][0][:, sl], ALU.mult, ALU.add,
                         [kr, ("a", "p", par, 0, g), ("a", "p", par, 1, g)], [("a", "ach", i)])
                tb = 4 + cnt["tb"] % 2
                cnt["tb"] += 1
                pb = self.bankb(tb)
                for jj in range(4):
                    self.tr(pb[:, jj * 128:(jj + 1) * 128], ach[i][:, jj * 128:(jj + 1) * 128], self.ident_b[:], [("a", "ach", i), ("c", "ident")], [("ps", tb)])
                ai = cnt["at"] % 2
                cnt["at"] += 1
                self.cp(aT[ai][:, 0:512], pb[:, 0:512], [("ps", tb)], [("a", "aT", ai)], eng="dve")
                if pend is not None:
                    pv(*pend)
                pend = (g, ai)
                yield
            pv(*pend)
            yield
            ko = ("a", "o", par)
            self.ts(osb[:], self.bank(ob, 256), sm[:, 38:39], None, ALU.mult, None, [("ps", ob), kr], [("a", "osb")])
            self.P.op("dve", lambda e, o=junk[:], a_=osb[:], acc=sm[:, 48:49]: e.scalar_tensor_tensor(out=o, in0=a_, scalar=1.0, in1=a_, op0=ALU.mult, op1=ALU.mult, accum_out=acc),
                      reads=[("a", "osb")], writes=[("a", "on"), ko])
            yield
            self.act(sm[:, 49:50], sm[:, 48:49], AF.Ln, [ko, ("c", "eps")], [ko + ("r",)], scale=1.0 / 256, bias=self.epsc[:, 0:1])
            self.act(sm[:, 49:50], sm[:, 49:50], AF.Exp, [ko + ("r",)], [ko + ("r",)], scale=-0.5)
            yield
            self.stt(on[:], osb[:], sm[:, 49:50], sgr[:], ALU.mult, ALU.mult, [("a", "osb"), ko + ("r",), ("a", "sgr")], [("a", "on")])
            tb = 4 + cnt["tb"] % 2
            cnt["tb"] += 1
            pb = self.bankb(tb)
            for jj in range(2):
                self.tr(pb[:, jj * 128:(jj + 1) * 128], on[:, jj * 128:(jj + 1) * 128], self.ident_b[:], [("a", "on"), ("c", "ident")], [("ps", tb)])
            yield
            for jj in range(2):
                self.cp(self.big[:, h * 2 + jj, s * 128:(s + 1) * 128], pb[:, jj * 128:(jj + 1) * 128], [("ps", tb)], [("bigw", h, jj, s)], eng="act")
            yield

        for _ in gen_ab(0, *tasks[0]):
            pass
        for ti, (h, s) in enumerate(tasks):
            gc = gen_c(ti, h, s)
            n_c = nkeys(s) // 4 + 5
            if ti + 1 < len(tasks):
                ga = gen_ab(ti + 1, *tasks[ti + 1])
                n_a = 2 * (nkeys(tasks[ti + 1][1]) // 4) + 2
            else:
                ga, n_a = iter(()), 0
            steps = max(n_a, n_c)
            ia = ic = 0
            for k in range(1, steps + 1):
                while ia < (k * n_a) // steps:
                    next(ga, None); ia += 1
                while ic < (k * n_c) // steps:
                    next(gc, None); ic += 1
            for _ in ga:
                pass
            for _ in gc:
                pass
        self.barrier()

    def evac_y(self, j, nst):
        y_d = self.dram("y_d", [T, DM], F32)
        sg_ = self.ystg[nst % 2]; ksg = ("o", "ystg", nst % 2)
        for tb in range(8):
            self.cp(sg_[:, tb * 512:(tb + 1) * 512], self.bank(tb), [("ps", tb)], [ksg], eng="act" if tb % 2 == 0 else "dve")
        self.st(y_d[:, j * 512:(j + 1) * 512].rearrange("(b p) n -> p b n", p=128), sg_[:].rearrange("p (b n) -> p b n", b=8),
                f"o_st{nst % 2}", [ksg], writes=[("y_d", s, j) for s in range(8)])

    def stage_outproj(self, l):
        self.local_reset()
        w_out = self.dram(f"w_out{l}", [DM, DM], F32)
        gmT_d = self.dram(f"gmT{l}", [2048, T], BF16)
        self.ystg = [self.lsb([128, 4096], F32) for _ in range(2)]
        for hf in range(2):
            self.ld(self.big[:, 16 + hf * 8:24 + hf * 8, :], gmT_d[hf * 1024:(hf + 1) * 1024, :].rearrange("(c p) t -> p c t", p=128),
                    f"o_gm{hf}", [("o", "gm", hf)], reads=[(f"gmT{l}", j) for j in range(4)])
        ak = lambda kc: ("bigall",) if kc < 16 else ("o", "gm", (kc - 16) // 8)
        for j in range(8):
            slab, skey = self.slab_load(w_out, 0, j * 512)
            self.gemm_tm(slab, skey, self.big, ak)
            self.evac_y(j, j)
        self.barrier()

    def stage_mlp1(self, l):
        self.local_reset()
        w1 = self.dram(f"w_1{l}", [DM, DFF], F32)
        hid_d = self.dram("hid_d", [DFF, T], BF16)
        stg = [self.lsb([128, 4096], BF16) for _ in range(2)]
        r = [self.lsb([128, 512], BF16) for _ in range(2)]
        bk = lambda kc: ("bigall",)
        ri = 0
        for j in range(32):
            slab, skey = self.slab_load(w1, 0, j * 512)
            self.gemm_fm(slab, skey, self.big, bk)
            sg_ = stg[j % 2]; ksg = ("h", "stg", j % 2)
            for cc in range(4):
                for th in range(2):
                    b = cc * 2 + th
                    r_ = r[ri % 2]; kr = ("h", "r", ri % 2); ri += 1
                    self.act(r_[:], self.bank(b), AF.Relu, [("ps", b)], [kr])
                    self.tt(sg_[:, cc * 1024 + th * 512:cc * 1024 + (th + 1) * 512], self.bank(b), r_[:], ALU.mult, [("ps", b), kr], [ksg])
            self.st(hid_d[j * 512:(j + 1) * 512, :].rearrange("(c p) t -> p c t", p=128), sg_[:].rearrange("p (c t) -> p c t", c=4),
                    f"h_st{j % 2}", [ksg], writes=[("hid", j)])
        self.barrier()

    def stage_mlp2(self, l):
        self.local_reset()
        w2 = self.dram(f"w_2{l}", [DFF, DM], F32)
        hid_d = self.dram("hid_d", [DFF, T], BF16)
        self.ystg = [self.lsb([128, 4096], F32) for _ in range(2)]
        for cg in range(8):
            for ks in range(4):
                slab, skey = self.slab_load(w2, ks * 4096, cg * 512)
                for hf in range(2):
                    r0 = ks * 4096 + hf * 2048
                    self.ld(self.big[:, hf * 16:(hf + 1) * 16, :], hid_d[r0:r0 + 2048, :].rearrange("(c p) t -> p c t", p=128),
                            f"g_h{hf}", [("g", "hid", hf)], reads=[("hid", r0 // 512 + i) for i in range(4)])
                    self.gemm_tm(slab, skey, self.big, lambda kc, hf=hf: ("g", "hid", hf), kcs=range(hf * 16, hf * 16 + 16),
                                 first=(ks == 0 and hf == 0), last=(ks == 3 and hf == 1))
            self.evac_y(cg, cg)
        self.barrier()

    def finish(self):
        self.P.barrier(skip_queues=())
        self.P.emit()
        return self.nc


def _col(v, n):
    return np.ascontiguousarray(v.reshape(n, 128).T)


_LAUNCHES = [
    dict(stages=[("modshard",)], ins=["ident_bf", "c_pc2", "w_adaS", "browS"], outs=["modS"]),
    dict(stages=[("norm", 0, "pre"), ("inproj", 0)],
         ins=["ident_bf", "modraw0", "grow0", "x_c", "w_in0", "wsT0", "trilT", "bs0", "vncol0"],
         outs=["qT0", "kT0", "v0", "gmT0"]),
    dict(stages=[("attn", 0), ("outproj", 0), ("norm", 0, "mid"), ("mlp1", 0), ("mlp2", 0), ("norm", 0, "end"), ("inproj", 1)],
         ins=["ident_bf", "qT0", "kTf0", "vf0", "braw", "maskc", "b31", "rbT", "lamv0", "subg0", "w_out0", "gmT0", "modraw0", "grow0", "x_c",
              "w_10", "w_20", "modraw1", "grow1", "w_in1", "wsT1", "trilT", "bs1", "vncol1"],
         outs=["xb0", "qT1", "kT1", "v1", "gmT1"]),
    dict(stages=[("attn", 1), ("outproj", 1), ("norm", 1, "mid"), ("mlp1", 1), ("mlp2", 1), ("norm", 1, "end")],
         ins=["ident_bf", "qT1", "kTf1", "vf1", "braw", "maskc", "b31", "rbT", "lamv1", "subg1", "w_out1", "gmT1", "modraw1", "grow1", "xb0",
              "w_11", "w_21"],
         outs=["out"]),
]


def _build(cfg):
    B = Builder(cfg["ins"], cfg["outs"])
    for stg in cfg["stages"]:
        getattr(B, "stage_" + stg[0])(*stg[1:])
    nc = B.finish()
    return nc, B.used_in, B.used_out


def kernel(x, c, rel_bias, w_ada, b_ada, pre_mix_g, w_in, lambda_q1, lambda_k1, lambda_q2, lambda_k2,
           subln_g, v_norm_g, v_norm_b, w_s, b_s, w_out, post_mix_g, pre_mlp_g, w_1, w_2, post_mlp_g):
    import ml_dtypes
    f = lambda a: np.asarray(a, dtype=np.float32)
    x, c, rel_bias, w_ada, b_ada, w_in, w_out, w_1, w_2 = map(f, (x, c, rel_bias, w_ada, b_ada, w_in, w_out, w_1, w_2))
    pre_mix_g, post_mix_g, pre_mlp_g, post_mlp_g = map(f, (pre_mix_g, post_mix_g, pre_mlp_g, post_mlp_g))
    lambda_q1, lambda_k1, lambda_q2, lambda_k2, subln_g, v_norm_g, v_norm_b, w_s, b_s = map(
        f, (lambda_q1, lambda_k1, lambda_q2, lambda_k2, subln_g, v_norm_g, v_norm_b, w_s, b_s))

    cores = list(range(NCORE))
    bj = [(cid // 4, cid % 4) for cid in cores]
    shared = {"ident_bf": np.eye(128, dtype=np.float32).astype(ml_dtypes.bfloat16),
              "trilT": np.ascontiguousarray(np.triu(np.ones((128, 128), np.float32))),
              "b31": np.ascontiguousarray(rel_bias[31, :]), "rbT": np.ascontiguousarray(rel_bias.T.reshape(-1))}
    for l in range(DEPTH):
        shared[f"grow{l}"] = np.ascontiguousarray(np.stack([pre_mix_g[l], post_mix_g[l], pre_mlp_g[l], post_mlp_g[l]], axis=0))
        shared[f"w_in{l}"] = w_in[l]
        shared[f"wsT{l}"] = np.ascontiguousarray(w_s[l].transpose(2, 0, 1))
        shared[f"bs{l}"] = np.ascontiguousarray(b_s[l].reshape(-1))
        shared[f"vncol{l}"] = np.ascontiguousarray(np.stack([_col(v_norm_g[l], 16), _col(v_norm_b[l], 16)], axis=1))
        shared[f"lamv{l}"] = np.ascontiguousarray(np.stack([lambda_q1[l], lambda_k1[l], lambda_q2[l], lambda_k2[l]], axis=0))
        shared[f"subg{l}"] = subln_g[l]
        shared[f"w_out{l}"] = w_out[l]
        shared[f"w_1{l}"] = w_1[l]
        shared[f"w_2{l}"] = w_2[l]
    percore = [dict() for _ in cores]
    for cid, (b, j) in zip(cores, bj):
        d = percore[cid]
        d["x_c"] = np.ascontiguousarray(np.concatenate([x[b, qblock(j, s) * 128:(qblock(j, s) + 1) * 128, :] for s in range(NS)], axis=0))
        d["c_pc2"] = np.ascontiguousarray(np.stack([_col(c[0], 32), _col(c[1], 32)], axis=2).reshape(128, 64))
        sl_ = slice(cid * 3072, (cid + 1) * 3072)
        d["w_adaS"] = np.ascontiguousarray(np.concatenate([w_ada[0][:, sl_], w_ada[1][:, sl_]], axis=1))
        d["browS"] = np.ascontiguousarray(np.concatenate([b_ada[0][sl_], b_ada[1][sl_]]))
        braw = np.zeros((8, NS, 128, 640), np.float32)
        mask = np.zeros((NS, 128, 640), np.float32)
        for s in range(NS):
            nk = nkeys(s); w = min(5, nk); w0 = nk - w
            qpos = qblock(j, s) * 128 + np.arange(128)[:, None]
            kpos = w0 * 128 + np.arange(w * 128)[None, :]
            n = qpos - kpos
            bucket = t5_bucket_np(n)
            mask[s, :, :w * 128] = np.where(n < 0, np.float32(MASKV), np.float32(0))
            braw[:, s, :, :w * 128] = rel_bias[bucket, :].transpose(2, 0, 1)
        d["braw"] = braw
        d["maskc"] = mask

    def run(cfg, extra):
        nc, used_in, used_out = _build(cfg)
        in_maps = []
        for cid in cores:
            m = {}
            for name in used_in:
                if name in extra[cid]:
                    m[name] = extra[cid][name]
                elif name in percore[cid]:
                    m[name] = percore[cid][name]
                else:
                    m[name] = shared[name]
            in_maps.append(m)
        res = run_bass_kernel_spmd(nc, in_maps, core_ids=cores)
        return [r for r in res.results]

    def gather_kv(res, l):
        ext = [dict() for _ in cores]
        for b in range(2):
            kT = np.zeros((2048, 4096), ml_dtypes.bfloat16)
            vv = np.zeros((4096, 2048), ml_dtypes.bfloat16)
            for j in range(4):
                r = res[b * 4 + j]
                for s in range(NS):
                    qb = qblock(j, s)
                    kT[:, qb * 128:(qb + 1) * 128] = r[f"kT{l}"][:, s * 128:(s + 1) * 128]
                    vv[qb * 128:(qb + 1) * 128, :] = r[f"v{l}"][s * 128:(s + 1) * 128, :]
            for j in range(4):
                ext[b * 4 + j][f"kTf{l}"] = kT
                ext[b * 4 + j][f"vf{l}"] = vv
        return ext

    r0 = run(_LAUNCHES[0], [dict() for _ in cores])
    modraw = np.zeros((2, DEPTH, 6 * DM), np.float32)
    for cid in cores:
        for b in range(2):
            for l in range(DEPTH):
                modraw[b, l, cid * 3072:(cid + 1) * 3072] = r0[cid]["modS"][b, l * 3072:(l + 1) * 3072]
    base = [dict() for _ in cores]
    for cid, (b, j) in zip(cores, bj):
        for l in range(DEPTH):
            base[cid][f"modraw{l}"] = np.ascontiguousarray(modraw[b, l].reshape(6, DM))
    r1 = run(_LAUNCHES[1], base)
    ext = gather_kv(r1, 0)
    for cid in cores:
        ext[cid].update(base[cid])
        for k in ("qT0", "gmT0"):
            ext[cid][k] = r1[cid][k]
    r2 = run(_LAUNCHES[2], ext)
    ext = gather_kv(r2, 1)
    for cid in cores:
        ext[cid].update(base[cid])
        for k in ("qT1", "gmT1", "xb0"):
            ext[cid][k] = r2[cid][k]
    r3 = run(_LAUNCHES[3], ext)
    out = np.zeros((2, 4096, DM), np.float32)
    for cid, (b, j) in zip(cores, bj):
        for s in range(NS):
            qb = qblock(j, s)
            out[b, qb * 128:(qb + 1) * 128, :] = r3[cid]["out"][s * 128:(s + 1) * 128, :]
    return out
```

```python
import math
from contextlib import ExitStack

import numpy as np
import concourse.bass as bass
import concourse.mybir as mybir
from concourse.bass_utils import run_bass_kernel_spmd

F32 = mybir.dt.float32
BF16 = mybir.dt.bfloat16
ALU = mybir.AluOpType
AF = mybir.ActivationFunctionType
AX = mybir.AxisListType

DM = 4096
DEPTH = 2
NCORE = 8
T = 1024
NS = 8
DFF = 16384
EPS = 1e-6
SCALE = 128 ** -0.5
MASKV = -30000.0 / SCALE
ENGS = ("pe", "act", "dve", "pool", "sp")


class Prog:
    def __init__(self, nc, cap=30000, same_engine_sync=("act", "dve", "pool")):
        self.nc = nc
        self.cap = cap
        self.q = {e: [] for e in ENGS}
        self.bufs = {}
        self.dcount = {}
        self.ginc = {}
        self.ses = set(same_engine_sync)

    def _b(self, k):
        b = self.bufs.get(k)
        if b is None:
            b = self.bufs[k] = {"w": {}, "r": {}}
        return b

    def _collect(self, eng, reads, writes):
        deps = {}

        def add(ev):
            dom = ev[:2]
            if dom == ("e", eng) and eng not in self.ses:
                return
            if dom not in deps or deps[dom][2] < ev[2]:
                deps[dom] = ev

        for k in reads:
            for ev in self._b(k)["w"].values():
                add(ev)
        for k in writes:
            b = self._b(k)
            for ev in b["w"].values():
                add(ev)
            for ev in b["r"].values():
                add(ev)
        return list(deps.values())

    def _record(self, ev, reads, writes):
        dom = ev[:2]
        for k in reads:
            self._b(k)["r"][dom] = ev
        for k in writes:
            b = self._b(k)
            b["w"] = {dom: ev}
            b["r"] = {}

    def op(self, eng, emit, reads=(), writes=()):
        reads = tuple(reads)
        writes = tuple(writes)
        deps = self._collect(eng, reads, writes)
        idx = len(self.q[eng])
        self.q[eng].append({"waits": deps, "emit": emit, "kind": "op", "ms": False})
        self._record(("e", eng, idx), reads, writes)

    def dma(self, eng, group, emit, reads=(), writes=(), inc=16):
        reads = tuple(reads)
        writes = tuple(writes)
        deps = self._collect(eng, reads, writes)
        n = self.dcount.get(group, 0) + 1
        self.dcount[group] = n
        self.ginc[group] = inc
        self.q[eng].append({"waits": deps, "emit": emit, "kind": "dma", "group": group})
        self._record(("d", group, n), reads, writes)

    def barrier(self, skip_queues=("pool",), skip_groups=()):
        evs = []
        for e in ENGS:
            for i in range(len(self.q[e]) - 1, -1, -1):
                if self.q[e][i]["kind"] == "op":
                    evs.append(("e", e, i))
                    break
        for g, n in self.dcount.items():
            if g not in skip_groups:
                evs.append(("d", g, n))
        for e in ENGS:
            if e in skip_queues:
                continue
            w = [ev for ev in evs if not (ev[:2] == ("e", e) and e not in self.ses)]
            self.q[e].append({"waits": w, "emit": None, "kind": "nop"})

    def emit(self):
        nc = self.nc
        for e in ENGS:
            for ent in self.q[e]:
                for ev in ent["waits"]:
                    if ev[0] == "e":
                        self.q[ev[1]][ev[2]]["ms"] = True
        msnum = {}
        nms = {}
        for e in ENGS:
            c = 0
            for i, ent in enumerate(self.q[e]):
                if ent["kind"] == "op" and ent["ms"]:
                    msnum[(e, i)] = c
                    c += 1
            nms[e] = c
        with ExitStack() as st:
            esems = {}
            for e in ENGS:
                k = (nms[e] + self.cap - 1) // self.cap
                esems[e] = [st.enter_context(nc.semaphore(f"s_{e}{j}")) for j in range(k)]
            dsems = {g: st.enter_context(nc.semaphore(f"d_{g}")) for g in self.dcount}
            block = st.enter_context(nc.Block())

            def resolve(ev):
                if ev[0] == "e":
                    m = msnum[(ev[1], ev[2])]
                    return esems[ev[1]][m // self.cap], (m % self.cap) + 1
                return dsems[ev[1]], self.ginc[ev[1]] * ev[2]

            def run(ename, eobj):
                waited = {}
                for i, ent in enumerate(self.q[ename]):
                    for ev in ent["waits"]:
                        dom = ev[:2]
                        if waited.get(dom, -1) >= ev[2]:
                            continue
                        waited[dom] = ev[2]
                        s, v = resolve(ev)
                        eobj.wait_ge(s, v)
                    if ent["kind"] == "nop":
                        continue
                    ins = ent["emit"](eobj)
                    if ent["kind"] == "dma":
                        ins.then_inc(dsems[ent["group"]], self.ginc[ent["group"]])
                    elif ent["ms"]:
                        m = msnum[(ename, i)]
                        ins.then_inc(esems[ename][m // self.cap], 1)

            if self.q["pe"]:
                block.tensor(lambda e: run("pe", e))
            if self.q["act"]:
                block.scalar(lambda e: run("act", e))
            if self.q["dve"]:
                block.vector(lambda e: run("dve", e))
            if self.q["pool"]:
                block.gpsimd(lambda e: run("pool", e))
            if self.q["sp"]:
                block.sync(lambda e: run("sp", e))


def qblock(j, s):
    m = s // 2
    return 8 * m + j if s % 2 == 0 else 8 * m + 7 - j


def nkeys(s):
    m = s // 2
    return 8 * m + 4 if s % 2 == 0 else 8 * m + 8


def t5_bucket_np(n):
    n = np.maximum(n, 0)
    nf = np.maximum(n, 1).astype(np.float32)
    large = 16 + (np.log(nf / np.float32(16)) / np.float32(math.log(8.0)) * np.float32(16)).astype(np.int32)
    large = np.minimum(large, 31)
    return np.where(n < 16, n, large)


SB0 = 16512
SLAB_OFF = SB0
BIG_OFF = SB0 + 65536
CONST_OFF = BIG_OFF + 65536
LOCAL_OFF = CONST_OFF + 4096
LOCAL_END = 229344


class Builder:
    def __init__(self, ext_in, ext_out):
        self.nc = bass.Bass("TRN2", target_bir_lowering=False)
        self.P = Prog(self.nc)
        self.ext_in = set(ext_in)
        self.ext_out = set(ext_out)
        self.drams = {}
        self.used_in = []
        self.used_out = []
        self.nsb = 0
        self.slab_i = 0
        nc = self.nc
        self.slab = [self.sb([128, 32, 512], BF16, SLAB_OFF + i * 32768) for i in range(2)]
        self.big = self.sb([128, 32, 1024], BF16, BIG_OFF)
        self.ident_b = self.sb([128, 128], BF16, CONST_OFF)
        self.ones_b = self.sb([128, 128], BF16, CONST_OFF + 256)
        self.ident_f = self.sb([128, 128], F32, CONST_OFF + 512)
        self.small = self.sb([128, 768], F32, CONST_OFF + 1024)
        self.ps = nc.alloc_psum_tensor("ps", [128, 4096], F32)
        self.loc = LOCAL_OFF
        self.dmai = 0
        idb = self.dram("ident_bf", [128, 128], BF16)
        self.ld(self.ident_b[:], idb, "c_ident", [("c", "ident")])
        self.P.op("dve", lambda e: e.memset(self.ones_b[:], 1.0), writes=[("c", "ones")])
        self.epsc = self.sb([128, 8], F32, CONST_OFF + 512)
        self.P.op("dve", lambda e: e.memset(self.epsc[:], EPS), writes=[("c", "eps")])

    def sb(self, shape, dtype, off):
        self.nsb += 1
        return self.nc.alloc_sbuf_tensor_at(f"sb{self.nsb}", list(shape), dtype, offset=off)

    def local_reset(self):
        self.loc = LOCAL_OFF

    def lsb(self, shape, dtype):
        n = int(np.prod(shape[1:])) * (4 if dtype == F32 else 2)
        n = (n + 63) // 64 * 64
        off = self.loc
        self.loc += n
        assert self.loc <= LOCAL_END, ("local sbuf overflow", self.loc - LOCAL_END)
        self.last_off = off
        return self.sb(shape, dtype, off)

    def dram(self, name, shape, dtype):
        if name in self.drams:
            return self.drams[name]
        if name in self.ext_in:
            t = self.nc.dram_tensor(name, list(shape), dtype, kind="ExternalInput")
            self.used_in.append(name)
        elif name in self.ext_out:
            t = self.nc.dram_tensor(name, list(shape), dtype, kind="ExternalOutput")
            self.used_out.append(name)
        else:
            t = self.nc.dram_tensor(name, list(shape), dtype)
        self.drams[name] = t.ap()
        return self.drams[name]

    def ld(self, out, in_, group, writes, reads=(), eng="sp"):
        self.P.dma(eng, group, lambda e, o=out, i=in_: e.dma_start(out=o, in_=i), reads=reads, writes=writes)

    def st(self, out, in_, group, reads, writes=(), eng="sp"):
        self.P.dma(eng, group, lambda e, o=out, i=in_: e.dma_start(out=o, in_=i), reads=reads, writes=writes)

    def mm(self, out, lhsT, rhs, start, stop, reads, writes):
        self.P.op("pe", lambda e, o=out, l=lhsT, r=rhs, a=start, b=stop: e.matmul(o, l, r, start=a, stop=b),
                  reads=reads, writes=writes)

    def tr(self, out, in_, ident, reads, writes):
        self.P.op("pe", lambda e, o=out, i=in_, d=ident: e.transpose(o, i, d), reads=reads, writes=writes)

    def act(self, out, in_, func, reads, writes, **kw):
        self.P.op("act", lambda e, o=out, i=in_, f=func, kw=kw: e.activation(out=o, in_=i, func=f, **kw),
                  reads=reads, writes=writes)

    def ts(self, out, in0, s1, s2, op0, op1, reads, writes, eng="dve"):
        if op1 is None:
            self.P.op(eng, lambda e, o=out, i=in0, a=s1, p=op0: e.tensor_scalar(out=o, in0=i, scalar1=a, scalar2=None, op0=p),
                      reads=reads, writes=writes)
        else:
            self.P.op(eng, lambda e, o=out, i=in0, a=s1, b=s2, p=op0, q=op1: e.tensor_scalar(out=o, in0=i, scalar1=a, scalar2=b, op0=p, op1=q),
                      reads=reads, writes=writes)

    def tt(self, out, in0, in1, op, reads, writes, eng="dve"):
        self.P.op(eng, lambda e, o=out, a=in0, b=in1, p=op: e.tensor_tensor(out=o, in0=a, in1=b, op=p), reads=reads, writes=writes)

    def stt(self, out, in0, scalar, in1, op0, op1, reads, writes, eng="dve"):
        self.P.op(eng, lambda e, o=out, a=in0, s=scalar, b=in1, p=op0, q=op1: e.scalar_tensor_tensor(out=o, in0=a, scalar=s, in1=b, op0=p, op1=q),
                  reads=reads, writes=writes)

    def red(self, out, in_, op, reads, writes):
        self.P.op("dve", lambda e, o=out, i=in_, p=op: e.tensor_reduce(out=o, in_=i, axis=AX.X, op=p), reads=reads, writes=writes)

    def cp(self, out, in_, reads, writes, eng="dve"):
        if eng == "act":
            self.act(out, in_, AF.Copy, reads, writes)
        else:
            self.P.op(eng, lambda e, o=out, i=in_: e.tensor_copy(out=o, in_=i), reads=reads, writes=writes)

    def rstd(self, out, ssq, n, key):
        self.act(out, ssq, AF.Sqrt, [key, ("c", "eps")], [key + ("r",)], scale=1.0 / n, bias=self.epsc[:, 0:1])
        self.P.op("dve", lambda e, o=out: e.reciprocal(out=o, in_=o), reads=[key + ("r",)], writes=[key + ("r",)])

    def bank(self, b, n=512, off=0):
        return self.ps[:, b * 512 + off:b * 512 + off + n]

    def bankb(self, b):
        return self.ps[:, b * 512:(b + 1) * 512].bitcast(BF16)

    def barrier(self):
        self.P.barrier(skip_groups=tuple(f"slab{i}_{q}" for i in range(2) for q in range(4)))

    def slab_load(self, W, r0, c0):
        i = self.slab_i % 2
        self.slab_i += 1
        buf = self.slab[i]
        src = W[r0:r0 + 4096, c0:c0 + 512].rearrange("(c p) n -> p c n", p=128)
        for q in range(4):
            self.P.dma("pool", f"slab{i}_{q}",
                       lambda e, o=buf[:, 8 * q:8 * q + 8, :], s=src[:, 8 * q:8 * q + 8, :]: e.dma_start(out=o, in_=s),
                       writes=[("slab", i, q)])
        return buf, (lambda kc, i=i: ("slab", i, kc // 8))

    def gemm_fm(self, slab, skey, actT, akey):
        for cc in range(4):
            for th in range(2):
                b = cc * 2 + th
                for kc in range(32):
                    self.mm(self.bank(b), slab[:, kc, cc * 128:(cc + 1) * 128], actT[:, kc, th * 512:(th + 1) * 512],
                            kc == 0, kc == 31, [skey(kc), akey(kc)], [("ps", b)])

    def gemm_tm(self, slab, skey, actT, akey, kcs=range(32), first=True, last=True):
        kcs = list(kcs)
        for kc in kcs:
            for tb in range(8):
                self.mm(self.bank(tb), actT[:, kc, tb * 128:(tb + 1) * 128], slab[:, kc, :],
                        first and kc == kcs[0], last and kc == kcs[-1], [skey(kc), akey(kc)], [("ps", tb)])

    def stage_modshard(self):
        self.local_reset()
        c2_d = self.dram("c_pc2", [128, 64], F32)
        wS = self.dram("w_adaS", [DM, 6144], F32)
        bS = self.dram("browS", [6144], F32)
        modS = self.dram("modS", [2, 6144], F32)
        csb = self.lsb([128, 64], F32)
        cact = self.lsb([128, 64], F32)
        rep = self.lsb([128, 32, 128], BF16)
        brow = self.lsb([128, 6144], F32)
        trow = self.lsb([128, 6144], F32)
        self.ld(csb[:], c2_d, "m_c", [("m", "csb")])
        self.ld(brow[:], bS.partition_broadcast(128), "m_br", [("m", "brow")])
        self.act(cact[:], csb[:], AF.Silu, [("m", "csb")], [("m", "cact")])
        for kc in range(32):
            for b in range(2):
                self.ts(rep[:, kc, b * 64:(b + 1) * 64], self.ones_b[:, 0:64], cact[:, kc * 2 + b:kc * 2 + b + 1], None, ALU.mult, None,
                        [("m", "cact"), ("c", "ones")], [("m", "rep", kc)])
        for j in range(12):
            slab, skey = self.slab_load(wS, 0, j * 512)
            for kc in range(32):
                self.mm(self.bank(j % 8), rep[:, kc, :], slab[:, kc, :], kc == 0, kc == 31, [skey(kc), ("m", "rep", kc)], [("ps", j % 8)])
            sl = slice(j * 512, (j + 1) * 512)
            self.tt(trow[:, sl], self.bank(j % 8), brow[:, sl], ALU.add, [("ps", j % 8), ("m", "brow")], [("m", "trow", j)])
        for b in range(2):
            self.st(modS[b:b + 1, :], trow[b * 64:b * 64 + 1, :], f"m_st{b}", [("m", "trow", j) for j in range(12)], writes=[("modS", b)])
        self.barrier()

    def stage_norm(self, l, kind):
        self.local_reset()
        has_y = kind != "pre"
        final = kind == "end" and l == DEPTH - 1
        has_next = not final
        if kind == "pre":
            xsrc = self.dram("x_c", [T, DM], F32); xsrc_key = None
        elif kind == "mid":
            xsrc = self.dram("x_c", [T, DM], F32) if l == 0 else self.dram(f"xb{l - 1}", [T, DM], F32)
            xsrc_key = None if l == 0 else f"xb{l - 1}"
            xdst = self.dram(f"xa{l}", [T, DM], F32); xdst_key = f"xa{l}"
        else:
            xsrc = self.dram(f"xa{l}", [T, DM], F32); xsrc_key = f"xa{l}"
            xdst = self.dram("out", [T, DM], F32) if final else self.dram(f"xb{l}", [T, DM], F32)
            xdst_key = "out" if final else f"xb{l}"
        if has_y:
            y_d = self.dram("y_d", [T, DM], F32)
            mr_d = self.dram(f"modraw{l}", [6, DM], F32)
            gr_d = self.dram(f"grow{l}", [4, DM], F32)
            gi = 2 if kind == "mid" else 5
            gate = self.lsb([128, DM], F32)
            yt = [self.lsb([128, DM], F32) for _ in range(2)]
            self.ld(gate[:], mr_d[gi, :].partition_broadcast(128), "n_g", [("n", "gate")])
            self.ld(yt[0][:], gr_d[1 if kind == "mid" else 3, :].partition_broadcast(128), "n_y0", [("n", "y", 0)])
            self.tt(gate[:], gate[:], yt[0][:], ALU.mult, [("n", "gate"), ("n", "y", 0)], [("n", "gate")])
        xt = [self.lsb([128, DM], F32)] * 2 if has_y else [self.lsb([128, DM], F32) for _ in range(2)]
        xk = (lambda s_: 0) if has_y else (lambda s_: s_ % 2)
        if has_next:
            nl, pi = (l, 0) if kind == "pre" else ((l, 2) if kind == "mid" else (l + 1, 0))
            mrn_d = self.dram(f"modraw{nl}", [6, DM], F32)
            grn_d = self.dram(f"grow{nl}", [4, DM], F32)
            mc = self.lsb([128, 4, 32], F32)
            rv = 0 if pi == 0 else 3
            srcs = [(pi, mrn_d[rv, :]), ((pi + 2) % 4, mrn_d[rv + 1, :]), ((pi + 3) % 4, grn_d[0 if pi == 0 else 2, :])]
            for i2, (slot, src) in enumerate(srcs):
                self.P.dma("sp", f"n_mc{i2}", lambda e, o=mc[:, slot, :], i=src.rearrange("(c p) -> p c", p=128):
                           e.dma_start(out=o, in_=i, allow_slow_non_contiguous=True), writes=[("n", "mcl", i2)])
            self.stt(mc[:, pi + 1, :], mc[:, (pi + 2) % 4, :], 1.0, mc[:, (pi + 3) % 4, :], ALU.add, ALU.mult,
                     [("n", "mcl", 1), ("n", "mcl", 2)], [("n", "mc")])
            self.ts(mc[:, pi, :], mc[:, pi, :], 1.0, None, ALU.mult, None, [("n", "mcl", 0), ("n", "mc")], [("n", "mc")])
            xn = self.lsb([128, DM], BF16)
        junk = self.lsb([128, 1024], BF16)
        nsm = self.lsb([128, 64], F32)

        def load_x(s):
            self.ld(xt[s % 2][:], xsrc[s * 128:(s + 1) * 128, :], f"n_x{xk(s)}", [("n", "x", xk(s))],
                    reads=[(xsrc_key, s)] if xsrc_key else [])

        def load_y(s):
            self.ld(yt[s % 2][:], y_d[s * 128:(s + 1) * 128, :], f"n_y{s % 2}", [("n", "y", s % 2)], reads=[("y_d", s, j) for j in range(8)])

        def ssq_rstd(src, skey, base, key):
            for q in range(4):
                self.act(junk[:], src[:, q * 1024:(q + 1) * 1024], AF.Square, [skey], [("n", "junk"), key + ("p", q)],
                         accum_out=nsm[:, base + q:base + q + 1])
            self.red(nsm[:, base + 4:base + 5], nsm[:, base:base + 4], ALU.add, [key + ("p", q) for q in range(4)], [key])
            self.rstd(nsm[:, base + 5:base + 6], nsm[:, base + 4:base + 5], DM, key)

        def a_y(s):
            x_ = xt[s % 2]; kx = ("n", "x", xk(s))
            if has_y:
                if s + 1 < NS:
                    load_y(s + 1)
                if s > 0:
                    load_x(s)
                y_ = yt[s % 2]; ky = ("n", "y", s % 2)
                base = (s % 2) * 16
                k1 = ("n", "ssq1", s % 2)
                ssq_rstd(y_, ky, base, k1)
                self.stt(y_[:], y_[:], nsm[:, base + 5:base + 6], gate[:], ALU.mult, ALU.mult, [ky, k1 + ("r",), ("n", "gate")], [ky])
                self.tt(x_[:], x_[:], y_[:], ALU.add, [kx, ky], [kx])
                self.st(xdst[s * 128:(s + 1) * 128, :], x_[:], f"n_xs{xk(s)}", [kx], writes=[(xdst_key, s)])
            elif s + 1 < NS:
                load_x(s + 1)

        def a_x(s):
            x_ = xt[s % 2]; kx = ("n", "x", xk(s))
            base = (s % 2) * 16 + 8
            k2 = ("n", "ssq2", s % 2)
            ssq_rstd(x_, kx, base, k2)
            self.ts(xn[:], x_[:], nsm[:, base + 5:base + 6], None, ALU.mult, None, [kx, k2 + ("r",)], [("n", "xn")])

        def b_t(s):
            for g in range(4):
                b = (s % 2) * 4 + g
                pb = self.bankb(b)
                for jj in range(8):
                    kc = g * 8 + jj
                    self.tr(pb[:, jj * 128:(jj + 1) * 128], xn[:, kc * 128:(kc + 1) * 128], self.ident_b[:],
                            [("n", "xn"), ("c", "ident")], [("ps", b)])
                for jj in range(8):
                    kc = g * 8 + jj
                    o = self.big[:, kc, s * 128:(s + 1) * 128]
                    i_ = pb[:, jj * 128:(jj + 1) * 128]
                    if g % 2 == 0:
                        self.act(o, i_, AF.Identity, [("ps", b), ("n", "mc")], [("big", kc, s)],
                                 scale=mc[:, pi + 1, kc:kc + 1], bias=mc[:, pi, kc:kc + 1])
                    else:
                        self.ts(o, i_, mc[:, pi + 1, kc:kc + 1], mc[:, pi, kc:kc + 1], ALU.mult, ALU.add,
                                [("ps", b), ("n", "mc")], [("big", kc, s)])

        load_x(0)
        if has_y:
            load_y(0)
        a_y(0)
        if has_next:
            a_x(0)
        for s in range(NS):
            if s + 1 < NS:
                a_y(s + 1)
            if has_next:
                b_t(s)
                if s + 1 < NS:
                    a_x(s + 1)
        self.barrier()

    def bigkey(self, kc):
        return ("bigall",)

    def stage_inproj(self, l):
        self.local_reset()
        P = self.P
        w_in = self.dram(f"w_in{l}", [DM, 10240], F32)
        wsT_d = self.dram(f"wsT{l}", [128, 8, 128], F32)
        tril_d = self.dram("trilT", [128, 128], F32)
        bs_d = self.dram(f"bs{l}", [1024], F32)
        vn_d = self.dram(f"vncol{l}", [128, 2, 16], F32)
        qT_d = self.dram(f"qT{l}", [2048, T], BF16)
        kT_d = self.dram(f"kT{l}", [2048, T], BF16)
        v_d = self.dram(f"v{l}", [T, 2048], BF16)
        gmT_d = self.dram(f"gmT{l}", [2048, T], BF16)

        vt = self.lsb([128, 8, 2048], BF16)
        uT = self.lsb([128, 4, 1024], BF16)
        stg = [self.lsb([128, 4096], BF16) for _ in range(2)]
        wsf = self.sb([128, 8, 128], F32, self.last_off)
        bsb = self.sb([128, 1024], F32, self.last_off + 4096)
        E = self.lsb([128, 16, 128], F32)
        wsb = self.lsb([128, 8, 128], BF16)
        tril = self.lsb([128, 128], F32)
        vn = self.lsb([128, 2, 16], F32)
        tmp = self.lsb([128, 512], F32)
        junk = self.lsb([128, 512], BF16)
        st1 = self.lsb([128, 32], F32)
        st2 = self.lsb([128, 32], F32)
        sv = self.lsb([128, 48], F32)
        bk = lambda kc: ("bigall",)

        self.ld(wsf[:], wsT_d, "i_ws", [("i", "wsf")])
        self.ld(tril[:], tril_d, "i_tr", [("i", "tril")])
        self.ld(bsb[:], bs_d.partition_broadcast(128), "i_bs", [("i", "bsb")])
        self.ld(vn[:], vn_d, "i_vn", [("i", "vn")])
        for g in range(8):
            self.tt(wsb[:, g, :], wsf[:, g, :], tril[:], ALU.mult, [("i", "wsf"), ("i", "tril")], [("i", "wsb", g)])
        for g in range(8):
            self.mm(self.bank(g // 4, 128, (g % 4) * 128), self.ones_b[:], wsb[:, g, :], True, True,
                    [("c", "ones"), ("i", "wsb", g)], [("ps", g // 4)])
        for cg in range(16):
            g = cg // 2
            self.stt(E[:, cg, :], self.bank(g // 4, 128, (g % 4) * 128), vn[:, 1, cg:cg + 1], bsb[:, g * 128:(g + 1) * 128],
                     ALU.mult, ALU.add, [("ps", g // 4), ("i", "vn"), ("i", "bsb")], [("i", "E", cg)])

        for j in range(4):
            slab, skey = self.slab_load(w_in, 0, 8192 + j * 512)
            self.gemm_tm(slab, skey, self.big, bk)
            for tb in range(8):
                self.act(vt[:, tb, j * 512:(j + 1) * 512], self.bank(tb), AF.Gelu, [("ps", tb)], [("i", "vt", tb, j), ("i", "st1", tb * 4 + j)],
                         accum_out=st1[:, tb * 4 + j:tb * 4 + j + 1])
                self.act(junk[:], vt[:, tb, j * 512:(j + 1) * 512], AF.Square, [("i", "vt", tb, j)], [("i", "junk"), ("i", "st2", tb * 4 + j)],
                         accum_out=st2[:, tb * 4 + j:tb * 4 + j + 1])
        k1 = [("i", "st1", i) for i in range(32)]
        k2 = [("i", "st2", i) for i in range(32)]
        ks = ("i", "sv")
        self.red(sv[:, 0:8], st1[:].rearrange("p (a b) -> p a b", b=4), ALU.add, k1, [ks])
        self.red(sv[:, 8:16], st2[:].rearrange("p (a b) -> p a b", b=4), ALU.add, k2, [ks])
        self.ts(sv[:, 0:8], sv[:, 0:8], 1.0 / 2048, None, ALU.mult, None, [ks], [ks])
        self.tt(sv[:, 16:24], sv[:, 0:8], sv[:, 0:8], ALU.mult, [ks], [ks])
        self.stt(sv[:, 8:16], sv[:, 8:16], 1.0 / 2048, sv[:, 16:24], ALU.mult, ALU.subtract, [ks], [ks])
        self.act(sv[:, 8:16], sv[:, 8:16], AF.Sqrt, [ks, ("c", "eps")], [ks], scale=1.0, bias=self.epsc[:, 0:1])
        self.P.op("dve", lambda e, o=sv[:, 8:16]: e.reciprocal(out=o, in_=o), reads=[ks], writes=[ks])
        for tb in range(8):
            self.ts(vt[:, tb, :], vt[:, tb, :], sv[:, tb:tb + 1], sv[:, 8 + tb:9 + tb], ALU.subtract, ALU.mult,
                    [ks] + [("i", "vt", tb, j) for j in range(4)], [("i", "vtn", tb)] + [("i", "vt", tb, j) for j in range(4)])

        nst = 0
        for j in range(4):
            slab, skey = self.slab_load(w_in, 0, 6144 + j * 512)
            self.gemm_fm(slab, skey, self.big, bk)
            for cc in range(4):
                for th in range(2):
                    b = cc * 2 + th
                    self.act(uT[:, cc, th * 512:(th + 1) * 512], self.bank(b), AF.Gelu, [("ps", b)], [("i", "uT", cc, th)])
            sg_ = stg[nst % 2]; ksg = ("i", "stg", nst % 2); nst += 1
            for cc in range(4):
                cg = j * 4 + cc
                g = cg // 2
                for tb in range(8):
                    b = cc * 2 + tb // 4
                    self.mm(self.bank(b, 128, (tb % 4) * 128), vt[:, tb, cg * 128:(cg + 1) * 128], wsb[:, g, :], True, True,
                            [("i", "vtn", tb), ("i", "wsb", g)], [("ps", b)])
                for tb in range(8):
                    b = cc * 2 + tb // 4
                    self.stt(tmp[:, 0:128], self.bank(b, 128, (tb % 4) * 128), vn[:, 0, cg:cg + 1], E[:, cg, :], ALU.mult, ALU.add,
                             [("ps", b), ("i", "vn"), ("i", "E", cg)], [("i", "tmp")])
                    self.tt(sg_[:, cc * 1024 + tb * 128:cc * 1024 + (tb + 1) * 128], tmp[:, 0:128], uT[:, cc, tb * 128:(tb + 1) * 128], ALU.mult,
                            [("i", "tmp"), ("i", "uT", cc, tb // 4)], [ksg])
            self.st(gmT_d[j * 512:(j + 1) * 512, :].rearrange("(c p) t -> p c t", p=128), sg_[:].rearrange("p (c t) -> p c t", c=4),
                    f"i_st{(nst - 1) % 2}", [ksg], writes=[(f"gmT{l}", j)])
        for nm, c0, dst in (("qT", 0, qT_d), ("kT", 2048, kT_d)):
            for j in range(4):
                slab, skey = self.slab_load(w_in, 0, c0 + j * 512)
                self.gemm_fm(slab, skey, self.big, bk)
                sg_ = stg[nst % 2]; ksg = ("i", "stg", nst % 2); nst += 1
                for cc in range(4):
                    for th in range(2):
                        b = cc * 2 + th
                        self.cp(sg_[:, cc * 1024 + th * 512:cc * 1024 + (th + 1) * 512], self.bank(b), [("ps", b)], [ksg],
                                eng="act" if b % 2 == 0 else "dve")
                self.st(dst[j * 512:(j + 1) * 512, :].rearrange("(c p) t -> p c t", p=128), sg_[:].rearrange("p (c t) -> p c t", c=4),
                        f"i_st{(nst - 1) % 2}", [ksg], writes=[(f"{nm}{l}", j)])
        for j in range(4):
            slab, skey = self.slab_load(w_in, 0, 4096 + j * 512)
            self.gemm_tm(slab, skey, self.big, bk)
            sg_ = stg[nst % 2]; ksg = ("i", "stg", nst % 2); nst += 1
            for tb in range(8):
                self.cp(sg_[:, tb * 512:(tb + 1) * 512], self.bank(tb), [("ps", tb)], [ksg], eng="act" if tb % 2 == 0 else "dve")
            self.st(v_d[:, j * 512:(j + 1) * 512].rearrange("(b p) n -> p b n", p=128), sg_[:].rearrange("p (b n) -> p b n", b=8),
                    f"i_st{(nst - 1) % 2}", [ksg], writes=[(f"v{l}", j)])
        self.barrier()

    def stage_attn(self, l):
        self.local_reset()
        lam_init = 0.8 - 0.6 * math.exp(-0.3 * l)
        qT_d = self.dram(f"qT{l}", [2048, T], BF16)
        kT_f = self.dram(f"kTf{l}", [2048, 4096], BF16)
        v_f = self.dram(f"vf{l}", [4096, 2048], BF16)
        braw_d = self.dram("braw", [8, NS, 128, 640], F32)
        mask_d = self.dram("maskc", [NS, 128, 640], F32)
        b31_d = self.dram("b31", [8], F32)
        rbT_d = self.dram("rbT", [256], F32)
        lamv_d = self.dram(f"lamv{l}", [4, 128], F32)
        sg_d = self.dram(f"subg{l}", [256], F32)

        KT = [self.sb([128, 2, 4096], BF16, BIG_OFF + 32768), self.lsb([128, 2, 4096], BF16)]
        V = self.sb([128, 32, 256], BF16, BIG_OFF + 32768 + 16384)
        Q = [self.lsb([128, 2, 1024], BF16) for _ in range(2)]
        p = [[self.lsb([128, 4096], BF16) for _ in range(2)] for _ in range(2)]
        ksq = [self.lsb([128, 512], BF16) for _ in range(2)]
        qsq = self.lsb([128, 2, 128], BF16)
        ach = [self.lsb([128, 512], BF16) for _ in range(2)]
        aT = [self.lsb([128, 512], BF16) for _ in range(2)]
        braw = [self.lsb([128, 640], F32)] * 2
        mask = [self.lsb([128, 640], F32)] * 2
        bp32 = braw[0]
        lv = self.sb([128, 512], F32, self.last_off - 2560)
        bpb = [self.lsb([128, 640], BF16) for _ in range(2)]
        sgr = self.lsb([128, 256], F32)
        b31 = self.lsb([128, 8], F32)
        on = self.lsb([128, 256], BF16)
        junk = on
        osb = self.lsb([128, 256], F32)
        smc = self.lsb([128, 16], F32)
        nbm = self.lsb([128, 8], F32)
        kmx = [self.lsb([128, 24], F32) for _ in range(2)]
        sms = [self.lsb([128, 64], F32) for _ in range(2)]
        self.ld(lv[:], lamv_d.rearrange("a n -> (a n)").partition_broadcast(128), "a_lv", [("a", "lv")])
        self.ld(sgr[:], sg_d.partition_broadcast(128), "a_sg", [("a", "sgr")])
        self.ld(b31[:], b31_d.partition_broadcast(128), "a_b31", [("a", "b31")])
        kl = ("a", "lam")
        self.tt(lv[:, 0:128], lv[:, 0:128], lv[:, 128:256], ALU.mult, [("a", "lv")], [("a", "lv")])
        self.tt(lv[:, 256:384], lv[:, 256:384], lv[:, 384:512], ALU.mult, [("a", "lv")], [("a", "lv")])
        self.red(smc[:, 0:1], lv[:, 0:128], ALU.add, [("a", "lv")], [kl])
        self.red(smc[:, 1:2], lv[:, 256:384], ALU.add, [("a", "lv")], [kl])
        self.act(smc[:, 2:4], smc[:, 0:2], AF.Exp, [kl], [kl])
        self.tt(smc[:, 4:5], smc[:, 3:4], smc[:, 2:3], ALU.subtract, [kl], [kl])
        self.ts(smc[:, 4:5], smc[:, 4:5], -lam_init, None, ALU.add, None, [kl], [kl])
        self.ts(sgr[:], sgr[:], 1.0 - lam_init, None, ALU.mult, None, [("a", "sgr")], [("a", "sgr")])
        neglam = smc[:, 4:5]
        self.ld(mask[0][:, 0:256], rbT_d.partition_broadcast(128), "a_rb", [("a", "mask", 0)])
        self.red(nbm[:], mask[0][:, 0:256].rearrange("p (h b) -> p h b", b=32), ALU.max, [("a", "mask", 0)], [("a", "nbm")])
        self.tt(nbm[:], nbm[:], b31[:], ALU.subtract, [("a", "nbm"), ("a", "b31")], [("a", "nbm")])
        self.ts(nbm[:], nbm[:], 0.0, -1.0, ALU.max, ALU.mult, [("a", "nbm")], [("a", "nbm")])
        cnt = {"sb": 0, "tb": 0, "at": 0, "ch": 0, "ks": 0}
        kmg = [None]

        tasks = [(h, s) for h in range(8) for s in range(NS)]

        def load_qk(h):
            hp = h % 2
            self.ld(KT[hp][:], kT_f[h * 256:(h + 1) * 256, :].rearrange("(c p) k -> p c k", p=128), f"a_k{hp}", [("a", "KT", hp)])
            self.ld(Q[hp][:], qT_d[h * 256:(h + 1) * 256, :].rearrange("(c p) t -> p c t", p=128), f"a_q{hp}", [("a", "Q", hp)],
                    reads=[(f"qT{l}", j) for j in range(4)])

        def kmaxprep(h):
            hp = h % 2
            km = kmx[hp]
            for c in range(2):
                for ch in range(8):
                    i = cnt["ks"] % 2
                    cnt["ks"] += 1
                    self.tt(ksq[i][:], KT[hp][:, c, ch * 512:(ch + 1) * 512], KT[hp][:, c, ch * 512:(ch + 1) * 512], ALU.mult,
                            [("a", "KT", hp)], [("a", "ksq", i)])
                    self.mm(self.bank(3), self.ones_b[:], ksq[i][:], True, True, [("c", "ones"), ("a", "ksq", i)], [("ps", 3)])
                    self.red(km[:, c * 8 + ch:c * 8 + ch + 1], self.bank(3), ALU.max, [("ps", 3)], [("a", "kmp", hp, c, ch)])
                    yield
                self.red(km[:, 16 + c:17 + c], km[:, c * 8:c * 8 + 8], ALU.max, [("a", "kmp", hp, c, ch) for ch in range(8)], [("a", "kmax", hp)])

        def prep(ti, h, s):
            par = ti % 2
            hp = h % 2
            sm = sms[par]
            km = kmx[hp]
            self.ld(braw[par][:], braw_d[h, s], "a_br", [("a", "braw", 0)])
            self.ld(mask[par][:], mask_d[s], "a_mk", [("a", "mask", 0)])
            self.ts(braw[par][:], braw[par][:], b31[:, h:h + 1], 1.0 / SCALE, ALU.subtract, ALU.mult, [("a", "braw", 0), ("a", "b31")], [("a", "braw", 0)])
            self.tt(bpb[par][:], braw[par][:], mask[par][:], ALU.add, [("a", "braw", 0), ("a", "mask", 0)], [("a", "bpb", par)])
            kb_ = ("a", "bnd", par)
            yield
            self.tt(qsq[:], Q[hp][:, :, s * 128:(s + 1) * 128], Q[hp][:, :, s * 128:(s + 1) * 128], ALU.mult, [("a", "Q", hp)], [("a", "qsq")])
            pb3 = self.bankb(3)
            for c in range(2):
                self.tr(pb3[:, c * 128:(c + 1) * 128], qsq[:, c, :], self.ident_b[:], [("a", "qsq"), ("c", "ident")], [("ps", 3)])
            self.red(sm[:, 16:18], pb3[:, 0:256].rearrange("p (c d) -> p c d", c=2), ALU.add, [("ps", 3)], [kb_])
            self.tt(sm[:, 16:18], sm[:, 16:18], km[:, 16:18], ALU.mult, [kb_, ("a", "kmax", hp)], [kb_])
            yield
            self.act(sm[:, 16:18], sm[:, 16:18], AF.Ln, [kb_], [kb_])
            self.act(sm[:, 16:18], sm[:, 16:18], AF.Exp, [kb_], [kb_], scale=0.5)
            yield
            self.ts(sm[:, 18:20], sm[:, 16:18], -SCALE * 1.02, nbm[:, h:h + 1], ALU.mult, ALU.add, [kb_, ("a", "nbm")], [("a", "negm", par, 0), ("a", "negm", par, 1)])

        def gen_ab(ti, h, s):
            par = ti % 2
            hp = h % 2
            sm = sms[par]
            if ti == 0:
                load_qk(0)
                for _ in kmaxprep(0):
                    pass
                for _ in prep(0, h, s):
                    pass
            if s == 4 and h + 1 < 8:
                load_qk(h + 1)
            if s == 5 and h + 1 < 8:
                kmg[0] = kmaxprep(h + 1)
            if ti + 1 < len(tasks):
                h2, s2 = tasks[ti + 1]
                if s2 == 0 and kmg[0] is not None:
                    for _ in kmg[0]:
                        pass
                    kmg[0] = None
                pg = prep(ti + 1, h2, s2)
            else:
                pg = iter(())
            nk = nkeys(s); w = min(5, nk); w0 = nk - w; nch = nk // 4
            yield
            for c in range(2):
                for ch in range(nch):
                    b = cnt["sb"] % 3
                    cnt["sb"] += 1
                    lo, hi = max(ch * 512, w0 * 128), (ch + 1) * 512
                    hasb = hi > lo
                    self.mm(self.bank(b), Q[hp][:, c, s * 128:(s + 1) * 128], KT[hp][:, c, ch * 512:(ch + 1) * 512], True, not hasb,
                            [("a", "Q", hp), ("a", "KT", hp)], [("ps", b)])
                    if hasb:
                        self.mm(self.bank(b, hi - lo, lo - ch * 512), self.ident_b[:], bpb[par][:, lo - w0 * 128:hi - w0 * 128], False, True,
                                [("c", "ident"), ("a", "bpb", par)], [("ps", b)])
                    self.act(p[par][c][:, ch * 512:(ch + 1) * 512], self.bank(b), AF.Exp, [("ps", b), ("a", "negm", par, c)],
                             [("a", "p", par, c, ch), ("a", "lp", par, c, ch)], scale=SCALE, bias=sm[:, 18 + c:19 + c],
                             accum_out=sm[:, 20 + c * 8 + ch:21 + c * 8 + ch])
                    if kmg[0] is not None:
                        next(kmg[0], None)
                    next(pg, None)
                    yield
                self.red(sm[:, 36 + c:37 + c], sm[:, 20 + c * 8:20 + c * 8 + nch], ALU.add, [("a", "lp", par, c, ch) for ch in range(nch)], [("a", "l", par, c)])
            for _ in pg:
                pass
            kr = ("a", "r", par)
            self.P.op("dve", lambda e, o=sm[:, 38:40], i=sm[:, 36:38]: e.reciprocal(out=o, in_=i), reads=[("a", "l", par, 0), ("a", "l", par, 1)], writes=[kr])
            self.stt(sm[:, 39:40], sm[:, 39:40], neglam, sm[:, 36:37], ALU.mult, ALU.mult, [kr, kl, ("a", "l", par, 0)], [kr])
            yield

        def gen_c(ti, h, s):
            par = ti % 2
            sm = sms[par]
            kr = ("a", "r", par)
            if s == 0:
                self.ld(V[:], v_f[:, h * 256:(h + 1) * 256].rearrange("(b p) d -> p b d", p=128), "a_v", [("a", "V")])
            nk = nkeys(s); nch = nk // 4
            ob = 6 + ti % 2
            pend = None

            def pv(g, ai):
                for jj in range(4):
                    kb = g * 4 + jj
                    self.mm(self.bank(ob, 256), aT[ai][:, jj * 128:(jj + 1) * 128], V[:, kb, :], kb == 0, kb == nk - 1,
                            [("a", "aT", ai), ("a", "V")], [("ps", ob)])

            for g in range(nch):
                i = cnt["ch"] % 2
                cnt["ch"] += 1
                sl = slice(g * 512, (g + 1) * 512)
                self.stt(ach[i][:], p[par][1][:, sl], sm[:, 39:40], p[par][0][:, sl], ALU.mult, ALU.add,
                         [kr, ("a", "p", par, 0, g), ("a", "p", par, 1, g)], [("a", "ach", i)])
                tb = 4 + cnt["tb"] % 2
                cnt["tb"] += 1
                pb = self.bankb(tb)
                for jj in range(4):
                    self.tr(pb[:, jj * 128:(jj + 1) * 128], ach[i][:, jj * 128:(jj + 1) * 128], self.ident_b[:], [("a", "ach", i), ("c", "ident")], [("ps", tb)])
                ai = cnt["at"] % 2
                cnt["at"] += 1
                self.cp(aT[ai][:, 0:512], pb[:, 0:512], [("ps", tb)], [("a", "aT", ai)], eng="dve")
                if pend is not None:
                    pv(*pend)
                pend = (g, ai)
                yield
            pv(*pend)
            yield
            ko = ("a", "o", par)
            self.ts(osb[:], self.bank(ob, 256), sm[:, 38:39], None, ALU.mult, None, [("ps", ob), kr], [("a", "osb")])
            self.P.op("dve", lambda e, o=junk[:], a_=osb[:], acc=sm[:, 48:49]: e.scalar_tensor_tensor(out=o, in0=a_, scalar=1.0, in1=a_, op0=ALU.mult, op1=ALU.mult, accum_out=acc),
                      reads=[("a", "osb")], writes=[("a", "on"), ko])
            yield
            self.act(sm[:, 49:50], sm[:, 48:49], AF.Ln, [ko, ("c", "eps")], [ko + ("r",)], scale=1.0 / 256, bias=self.epsc[:, 0:1])
            self.act(sm[:, 49:50], sm[:, 49:50], AF.Exp, [ko + ("r",)], [ko + ("r",)], scale=-0.5)
            yield
            self.stt(on[:], osb[:], sm[:, 49:50], sgr[:], ALU.mult, ALU.mult, [("a", "osb"), ko + ("r",), ("a", "sgr")], [("a", "on")])
            tb = 4 + cnt["tb"] % 2
            cnt["tb"] += 1
            pb = self.bankb(tb)
            for jj in range(2):
                self.tr(pb[:, jj * 128:(jj + 1) * 128], on[:, jj * 128:(jj + 1) * 128], self.ident_b[:], [("a", "on"), ("c", "ident")], [("ps", tb)])
            yield
            for jj in range(2):
                self.cp(self.big[:, h * 2 + jj, s * 128:(s + 1) * 128], pb[:, jj * 128:(jj + 1) * 128], [("ps", tb)], [("bigw", h, jj, s)], eng="act")
            yield

        for _ in gen_ab(0, *tasks[0]):
            pass
        for ti, (h, s) in enumerate(tasks):
            gc = gen_c(ti, h, s)
            n_c = nkeys(s) // 4 + 5
            if ti + 1 < len(tasks):
                ga = gen_ab(ti + 1, *tasks[ti + 1])
                n_a = 2 * (nkeys(tasks[ti + 1][1]) // 4) + 2
            else:
                ga, n_a = iter(()), 0
            steps = max(n_a, n_c)
            ia = ic = 0
            for k in range(1, steps + 1):
                while ia < (k * n_a) // steps:
                    next(ga, None); ia += 1
                while ic < (k * n_c) // steps:
                    next(gc, None); ic += 1
            for _ in ga:
                pass
            for _ in gc:
                pass
        self.barrier()

    def evac_y(self, j, nst):
        y_d = self.dram("y_d", [T, DM], F32)
        sg_ = self.ystg[nst % 2]; ksg = ("o", "ystg", nst % 2)
        for tb in range(8):
            self.cp(sg_[:, tb * 512:(tb + 1) * 512], self.bank(tb), [("ps", tb)], [ksg], eng="act" if tb % 2 == 0 else "dve")
        self.st(y_d[:, j * 512:(j + 1) * 512].rearrange("(b p) n -> p b n", p=128), sg_[:].rearrange("p (b n) -> p b n", b=8),
                f"o_st{nst % 2}", [ksg], writes=[("y_d", s, j) for s in range(8)], eng="act")

    def stage_outproj(self, l):
        self.local_reset()
        w_out = self.dram(f"w_out{l}", [DM, DM], F32)
        gmT_d = self.dram(f"gmT{l}", [2048, T], BF16)
        self.ystg = [self.lsb([128, 4096], F32) for _ in range(2)]
        for hf in range(2):
            self.ld(self.big[:, 16 + hf * 8:24 + hf * 8, :], gmT_d[hf * 1024:(hf + 1) * 1024, :].rearrange("(c p) t -> p c t", p=128),
                    f"o_gm{hf}", [("o", "gm", hf)], reads=[(f"gmT{l}", j) for j in range(4)])
        ak = lambda kc: ("bigall",) if kc < 16 else ("o", "gm", (kc - 16) // 8)
        for j in range(8):
            slab, skey = self.slab_load(w_out, 0, j * 512)
            self.gemm_tm(slab, skey, self.big, ak)
            self.evac_y(j, j)
        self.barrier()

    def stage_mlp1(self, l):
        self.local_reset()
        w1 = self.dram(f"w_1{l}", [DM, DFF], F32)
        hid_d = self.dram("hid_d", [DFF, T], BF16)
        stg = [self.lsb([128, 4096], BF16) for _ in range(2)]
        r = [self.lsb([128, 512], BF16) for _ in range(2)]
        bk = lambda kc: ("bigall",)
        ri = 0
        for j in range(32):
            slab, skey = self.slab_load(w1, 0, j * 512)
            self.gemm_fm(slab, skey, self.big, bk)
            sg_ = stg[j % 2]; ksg = ("h", "stg", j % 2)
            for cc in range(4):
                for th in range(2):
                    b = cc * 2 + th
                    r_ = r[ri % 2]; kr = ("h", "r", ri % 2); ri += 1
                    self.act(r_[:], self.bank(b), AF.Relu, [("ps", b)], [kr])
                    self.tt(sg_[:, cc * 1024 + th * 512:cc * 1024 + (th + 1) * 512], self.bank(b), r_[:], ALU.mult, [("ps", b), kr], [ksg])
            self.st(hid_d[j * 512:(j + 1) * 512, :].rearrange("(c p) t -> p c t", p=128), sg_[:].rearrange("p (c t) -> p c t", c=4),
                    f"h_st{j % 2}", [ksg], writes=[("hid", j)])
        self.barrier()

    def stage_mlp2(self, l):
        self.local_reset()
        w2 = self.dram(f"w_2{l}", [DFF, DM], F32)
        hid_d = self.dram("hid_d", [DFF, T], BF16)
        self.ystg = [self.lsb([128, 4096], F32) for _ in range(2)]
        for cg in range(8):
            for ks in range(4):
                slab, skey = self.slab_load(w2, ks * 4096, cg * 512)
                for hf in range(2):
                    r0 = ks * 4096 + hf * 2048
                    self.ld(self.big[:, hf * 16:(hf + 1) * 16, :], hid_d[r0:r0 + 2048, :].rearrange("(c p) t -> p c t", p=128),
                            f"g_h{hf}", [("g", "hid", hf)], reads=[("hid", r0 // 512 + i) for i in range(4)])
                    self.gemm_tm(slab, skey, self.big, lambda kc, hf=hf: ("g", "hid", hf), kcs=range(hf * 16, hf * 16 + 16),
                                 first=(ks == 0 and hf == 0), last=(ks == 3 and hf == 1))
            self.evac_y(cg, cg)
        self.barrier()

    def finish(self):
        self.P.barrier(skip_queues=())
        self.P.emit()
        return self.nc


def _col(v, n):
    return np.ascontiguousarray(v.reshape(n, 128).T)


_LAUNCHES = [
    dict(stages=[("modshard",)], ins=["ident_bf", "c_pc2", "w_adaS", "browS"], outs=["modS"]),
    dict(stages=[("norm", 0, "pre"), ("inproj", 0)],
         ins=["ident_bf", "modraw0", "grow0", "x_c", "w_in0", "wsT0", "trilT", "bs0", "vncol0"],
         outs=["qT0", "kT0", "v0", "gmT0"]),
    dict(stages=[("attn", 0), ("outproj", 0), ("norm", 0, "mid"), ("mlp1", 0), ("mlp2", 0), ("norm", 0, "end"), ("inproj", 1)],
         ins=["ident_bf", "qT0", "kTf0", "vf0", "braw", "maskc", "b31", "rbT", "lamv0", "subg0", "w_out0", "gmT0", "modraw0", "grow0", "x_c",
              "w_10", "w_20", "modraw1", "grow1", "w_in1", "wsT1", "trilT", "bs1", "vncol1"],
         outs=["xb0", "qT1", "kT1", "v1", "gmT1"]),
    dict(stages=[("attn", 1), ("outproj", 1), ("norm", 1, "mid"), ("mlp1", 1), ("mlp2", 1), ("norm", 1, "end")],
         ins=["ident_bf", "qT1", "kTf1", "vf1", "braw", "maskc", "b31", "rbT", "lamv1", "subg1", "w_out1", "gmT1", "modraw1", "grow1", "xb0",
              "w_11", "w_21"],
         outs=["out"]),
]


def _build(cfg):
    B = Builder(cfg["ins"], cfg["outs"])
    for stg in cfg["stages"]:
        getattr(B, "stage_" + stg[0])(*stg[1:])
    nc = B.finish()
    return nc, B.used_in, B.used_out


def kernel(x, c, rel_bias, w_ada, b_ada, pre_mix_g, w_in, lambda_q1, lambda_k1, lambda_q2, lambda_k2,
           subln_g, v_norm_g, v_norm_b, w_s, b_s, w_out, post_mix_g, pre_mlp_g, w_1, w_2, post_mlp_g):
    import ml_dtypes
    f = lambda a: np.asarray(a, dtype=np.float32)
    x, c, rel_bias, w_ada, b_ada, w_in, w_out, w_1, w_2 = map(f, (x, c, rel_bias, w_ada, b_ada, w_in, w_out, w_1, w_2))
    pre_mix_g, post_mix_g, pre_mlp_g, post_mlp_g = map(f, (pre_mix_g, post_mix_g, pre_mlp_g, post_mlp_g))
    lambda_q1, lambda_k1, lambda_q2, lambda_k2, subln_g, v_norm_g, v_norm_b, w_s, b_s = map(
        f, (lambda_q1, lambda_k1, lambda_q2, lambda_k2, subln_g, v_norm_g, v_norm_b, w_s, b_s))

    cores = list(range(NCORE))
    bj = [(cid // 4, cid % 4) for cid in cores]
    shared = {"ident_bf": np.eye(128, dtype=np.float32).astype(ml_dtypes.bfloat16),
              "trilT": np.ascontiguousarray(np.triu(np.ones((128, 128), np.float32))),
              "b31": np.ascontiguousarray(rel_bias[31, :]), "rbT": np.ascontiguousarray(rel_bias.T.reshape(-1))}
    for l in range(DEPTH):
        shared[f"grow{l}"] = np.ascontiguousarray(np.stack([pre_mix_g[l], post_mix_g[l], pre_mlp_g[l], post_mlp_g[l]], axis=0))
        shared[f"w_in{l}"] = w_in[l]
        shared[f"wsT{l}"] = np.ascontiguousarray(w_s[l].transpose(2, 0, 1))
        shared[f"bs{l}"] = np.ascontiguousarray(b_s[l].reshape(-1))
        shared[f"vncol{l}"] = np.ascontiguousarray(np.stack([_col(v_norm_g[l], 16), _col(v_norm_b[l], 16)], axis=1))
        shared[f"lamv{l}"] = np.ascontiguousarray(np.stack([lambda_q1[l], lambda_k1[l], lambda_q2[l], lambda_k2[l]], axis=0))
        shared[f"subg{l}"] = subln_g[l]
        shared[f"w_out{l}"] = w_out[l]
        shared[f"w_1{l}"] = w_1[l]
        shared[f"w_2{l}"] = w_2[l]
    percore = [dict() for _ in cores]
    for cid, (b, j) in zip(cores, bj):
        d = percore[cid]
        d["x_c"] = np.ascontiguousarray(np.concatenate([x[b, qblock(j, s) * 128:(qblock(j, s) + 1) * 128, :] for s in range(NS)], axis=0))
        d["c_pc2"] = np.ascontiguousarray(np.stack([_col(c[0], 32), _col(c[1], 32)], axis=2).reshape(128, 64))
        sl_ = slice(cid * 3072, (cid + 1) * 3072)
        d["w_adaS"] = np.ascontiguousarray(np.concatenate([w_ada[0][:, sl_], w_ada[1][:, sl_]], axis=1))
        d["browS"] = np.ascontiguousarray(np.concatenate([b_ada[0][sl_], b_ada[1][sl_]]))
        braw = np.zeros((8, NS, 128, 640), np.float32)
        mask = np.zeros((NS, 128, 640), np.float32)
        for s in range(NS):
            nk = nkeys(s); w = min(5, nk); w0 = nk - w
            qpos = qblock(j, s) * 128 + np.arange(128)[:, None]
            kpos = w0 * 128 + np.arange(w * 128)[None, :]
            n = qpos - kpos
            bucket = t5_bucket_np(n)
            mask[s, :, :w * 128] = np.where(n < 0, np.float32(MASKV), np.float32(0))
            braw[:, s, :, :w * 128] = rel_bias[bucket, :].transpose(2, 0, 1)
        d["braw"] = braw
        d["maskc"] = mask

    def run(cfg, extra):
        nc, used_in, used_out = _build(cfg)
        in_maps = []
        for cid in cores:
            m = {}
            for name in used_in:
                if name in extra[cid]:
                    m[name] = extra[cid][name]
                elif name in percore[cid]:
                    m[name] = percore[cid][name]
                else:
                    m[name] = shared[name]
            in_maps.append(m)
        res = run_bass_kernel_spmd(nc, in_maps, core_ids=cores)
        return [r for r in res.results]

    def gather_kv(res, l):
        ext = [dict() for _ in cores]
        for b in range(2):
            kT = np.zeros((2048, 4096), ml_dtypes.bfloat16)
            vv = np.zeros((4096, 2048), ml_dtypes.bfloat16)
            for j in range(4):
                r = res[b * 4 + j]
                for s in range(NS):
                    qb = qblock(j, s)
                    kT[:, qb * 128:(qb + 1) * 128] = r[f"kT{l}"][:, s * 128:(s + 1) * 128]
                    vv[qb * 128:(qb + 1) * 128, :] = r[f"v{l}"][s * 128:(s + 1) * 128, :]
            for j in range(4):
                ext[b * 4 + j][f"kTf{l}"] = kT
                ext[b * 4 + j][f"vf{l}"] = vv
        return ext

    r0 = run(_LAUNCHES[0], [dict() for _ in cores])
    modraw = np.zeros((2, DEPTH, 6 * DM), np.float32)
    for cid in cores:
        for b in range(2):
            for l in range(DEPTH):
                modraw[b, l, cid * 3072:(cid + 1) * 3072] = r0[cid]["modS"][b, l * 3072:(l + 1) * 3072]
    base = [dict() for _ in cores]
    for cid, (b, j) in zip(cores, bj):
        for l in range(DEPTH):
            base[cid][f"modraw{l}"] = np.ascontiguousarray(modraw[b, l].reshape(6, DM))
    r1 = run(_LAUNCHES[1], base)
    ext = gather_kv(r1, 0)
    for cid in cores:
        ext[cid].update(base[cid])
        for k in ("qT0", "gmT0"):
            ext[cid][k] = r1[cid][k]
    r2 = run(_LAUNCHES[2], ext)
    ext = gather_kv(r2, 1)
    for cid in cores:
        ext[cid].update(base[cid])
        for k in ("qT1", "gmT1", "xb0"):
            ext[cid][k] = r2[cid][k]
    r3 = run(_LAUNCHES[3], ext)
    out = np.zeros((2, 4096, DM), np.float32)
    for cid, (b, j) in zip(cores, bj):
        for s in range(NS):
            qb = qblock(j, s)
            out[b, qb * 128:(qb + 1) * 128, :] = r3[cid]["out"][s * 128:(s + 1) * 128, :]
    return out
```

```python
import math
from contextlib import ExitStack

import numpy as np
import concourse.bass as bass
import concourse.mybir as mybir
from concourse.bass_utils import run_bass_kernel_spmd

F32 = mybir.dt.float32
BF16 = mybir.dt.bfloat16
ALU = mybir.AluOpType
AF = mybir.ActivationFunctionType
AX = mybir.AxisListType

DM = 4096
DEPTH = 2
NCORE = 8
T = 1024
NS = 8
DFF = 16384
EPS = 1e-6
SCALE = 128 ** -0.5
MASKV = -30000.0 / SCALE
ENGS = ("pe", "act", "dve", "pool", "sp")


class Prog:
    def __init__(self, nc, cap=30000, same_engine_sync=("act", "dve", "pool")):
        self.nc = nc
        self.cap = cap
        self.q = {e: [] for e in ENGS}
        self.bufs = {}
        self.dcount = {}
        self.ginc = {}
        self.ses = set(same_engine_sync)

    def _b(self, k):
        b = self.bufs.get(k)
        if b is None:
            b = self.bufs[k] = {"w": {}, "r": {}}
        return b

    def _collect(self, eng, reads, writes):
        deps = {}

        def add(ev):
            dom = ev[:2]
            if dom == ("e", eng) and eng not in self.ses:
                return
            if dom not in deps or deps[dom][2] < ev[2]:
                deps[dom] = ev

        for k in reads:
            for ev in self._b(k)["w"].values():
                add(ev)
        for k in writes:
            b = self._b(k)
            for ev in b["w"].values():
                add(ev)
            for ev in b["r"].values():
                add(ev)
        return list(deps.values())

    def _record(self, ev, reads, writes):
        dom = ev[:2]
        for k in reads:
            self._b(k)["r"][dom] = ev
        for k in writes:
            b = self._b(k)
            b["w"] = {dom: ev}
            b["r"] = {}

    def op(self, eng, emit, reads=(), writes=()):
        reads = tuple(reads)
        writes = tuple(writes)
        deps = self._collect(eng, reads, writes)
        idx = len(self.q[eng])
        self.q[eng].append({"waits": deps, "emit": emit, "kind": "op", "ms": False})
        self._record(("e", eng, idx), reads, writes)

    def dma(self, eng, group, emit, reads=(), writes=(), inc=16):
        reads = tuple(reads)
        writes = tuple(writes)
        deps = self._collect(eng, reads, writes)
        n = self.dcount.get(group, 0) + 1
        self.dcount[group] = n
        self.ginc[group] = inc
        self.q[eng].append({"waits": deps, "emit": emit, "kind": "dma", "group": group})
        self._record(("d", group, n), reads, writes)

    def barrier(self, skip_queues=("pool",), skip_groups=()):
        evs = []
        for e in ENGS:
            for i in range(len(self.q[e]) - 1, -1, -1):
                if self.q[e][i]["kind"] == "op":
                    evs.append(("e", e, i))
                    break
        for g, n in self.dcount.items():
            if g not in skip_groups:
                evs.append(("d", g, n))
        for e in ENGS:
            if e in skip_queues:
                continue
            w = [ev for ev in evs if not (ev[:2] == ("e", e) and e not in self.ses)]
            self.q[e].append({"waits": w, "emit": None, "kind": "nop"})

    def emit(self):
        nc = self.nc
        for e in ENGS:
            for ent in self.q[e]:
                for ev in ent["waits"]:
                    if ev[0] == "e":
                        self.q[ev[1]][ev[2]]["ms"] = True
        msnum = {}
        nms = {}
        for e in ENGS:
            c = 0
            for i, ent in enumerate(self.q[e]):
                if ent["kind"] == "op" and ent["ms"]:
                    msnum[(e, i)] = c
                    c += 1
            nms[e] = c
        with ExitStack() as st:
            esems = {}
            for e in ENGS:
                k = (nms[e] + self.cap - 1) // self.cap
                esems[e] = [st.enter_context(nc.semaphore(f"s_{e}{j}")) for j in range(k)]
            dsems = {g: st.enter_context(nc.semaphore(f"d_{g}")) for g in self.dcount}
            block = st.enter_context(nc.Block())

            def resolve(ev):
                if ev[0] == "e":
                    m = msnum[(ev[1], ev[2])]
                    return esems[ev[1]][m // self.cap], (m % self.cap) + 1
                return dsems[ev[1]], self.ginc[ev[1]] * ev[2]

            def run(ename, eobj):
                waited = {}
                for i, ent in enumerate(self.q[ename]):
                    for ev in ent["waits"]:
                        dom = ev[:2]
                        if waited.get(dom, -1) >= ev[2]:
                            continue
                        waited[dom] = ev[2]
                        s, v = resolve(ev)
                        eobj.wait_ge(s, v)
                    if ent["kind"] == "nop":
                        continue
                    ins = ent["emit"](eobj)
                    if ent["kind"] == "dma":
                        ins.then_inc(dsems[ent["group"]], self.ginc[ent["group"]])
                    elif ent["ms"]:
                        m = msnum[(ename, i)]
                        ins.then_inc(esems[ename][m // self.cap], 1)

            if self.q["pe"]:
                block.tensor(lambda e: run("pe", e))
            if self.q["act"]:
                block.scalar(lambda e: run("act", e))
            if self.q["dve"]:
                block.vector(lambda e: run("dve", e))
            if self.q["pool"]:
                block.gpsimd(lambda e: run("pool", e))
            if self.q["sp"]:
                block.sync(lambda e: run("sp", e))


def qblock(j, s):
    m = s // 2
    return 8 * m + j if s % 2 == 0 else 8 * m + 7 - j


def nkeys(s):
    m = s // 2
    return 8 * m + 4 if s % 2 == 0 else 8 * m + 8


def t5_bucket_np(n):
    n = np.maximum(n, 0)
    nf = np.maximum(n, 1).astype(np.float32)
    large = 16 + (np.log(nf / np.float32(16)) / np.float32(math.log(8.0)) * np.float32(16)).astype(np.int32)
    large = np.minimum(large, 31)
    return np.where(n < 16, n, large)


SB0 = 16512
SLAB_OFF = SB0
BIG_OFF = SB0 + 65536
CONST_OFF = BIG_OFF + 65536
LOCAL_OFF = CONST_OFF + 4096
LOCAL_END = 229344


class Builder:
    def __init__(self, ext_in, ext_out):
        self.nc = bass.Bass("TRN2", target_bir_lowering=False)
        self.P = Prog(self.nc)
        self.ext_in = set(ext_in)
        self.ext_out = set(ext_out)
        self.drams = {}
        self.used_in = []
        self.used_out = []
        self.nsb = 0
        self.slab_i = 0
        nc = self.nc
        self.slab = [self.sb([128, 32, 512], BF16, SLAB_OFF + i * 32768) for i in range(2)]
        self.big = self.sb([128, 32, 1024], BF16, BIG_OFF)
        self.ident_b = self.sb([128, 128], BF16, CONST_OFF)
        self.ones_b = self.sb([128, 128], BF16, CONST_OFF + 256)
        self.ident_f = self.sb([128, 128], F32, CONST_OFF + 512)
        self.small = self.sb([128, 768], F32, CONST_OFF + 1024)
        self.ps = nc.alloc_psum_tensor("ps", [128, 4096], F32)
        self.loc = LOCAL_OFF
        self.dmai = 0
        idb = self.dram("ident_bf", [128, 128], BF16)
        self.ld(self.ident_b[:], idb, "c_ident", [("c", "ident")])
        self.P.op("dve", lambda e: e.memset(self.ones_b[:], 1.0), writes=[("c", "ones")])
        self.epsc = self.sb([128, 8], F32, CONST_OFF + 512)
        self.P.op("dve", lambda e: e.memset(self.epsc[:], EPS), writes=[("c", "eps")])

    def sb(self, shape, dtype, off):
        self.nsb += 1
        return self.nc.alloc_sbuf_tensor_at(f"sb{self.nsb}", list(shape), dtype, offset=off)

    def local_reset(self):
        self.loc = LOCAL_OFF

    def lsb(self, shape, dtype):
        n = int(np.prod(shape[1:])) * (4 if dtype == F32 else 2)
        n = (n + 63) // 64 * 64
        off = self.loc
        self.loc += n
        assert self.loc <= LOCAL_END, ("local sbuf overflow", self.loc - LOCAL_END)
        self.last_off = off
        return self.sb(shape, dtype, off)

    def dram(self, name, shape, dtype):
        if name in self.drams:
            return self.drams[name]
        if name in self.ext_in:
            t = self.nc.dram_tensor(name, list(shape), dtype, kind="ExternalInput")
            self.used_in.append(name)
        elif name in self.ext_out:
            t = self.nc.dram_tensor(name, list(shape), dtype, kind="ExternalOutput")
            self.used_out.append(name)
        else:
            t = self.nc.dram_tensor(name, list(shape), dtype)
        self.drams[name] = t.ap()
        return self.drams[name]

    def ld(self, out, in_, group, writes, reads=(), eng="sp"):
        self.P.dma(eng, group, lambda e, o=out, i=in_: e.dma_start(out=o, in_=i), reads=reads, writes=writes)

    def st(self, out, in_, group, reads, writes=(), eng="sp"):
        self.P.dma(eng, group, lambda e, o=out, i=in_: e.dma_start(out=o, in_=i), reads=reads, writes=writes)

    def mm(self, out, lhsT, rhs, start, stop, reads, writes):
        self.P.op("pe", lambda e, o=out, l=lhsT, r=rhs, a=start, b=stop: e.matmul(o, l, r, start=a, stop=b),
                  reads=reads, writes=writes)

    def tr(self, out, in_, ident, reads, writes):
        self.P.op("pe", lambda e, o=out, i=in_, d=ident: e.transpose(o, i, d), reads=reads, writes=writes)

    def act(self, out, in_, func, reads, writes, **kw):
        self.P.op("act", lambda e, o=out, i=in_, f=func, kw=kw: e.activation(out=o, in_=i, func=f, **kw),
                  reads=reads, writes=writes)

    def ts(self, out, in0, s1, s2, op0, op1, reads, writes, eng="dve"):
        if op1 is None:
            self.P.op(eng, lambda e, o=out, i=in0, a=s1, p=op0: e.tensor_scalar(out=o, in0=i, scalar1=a, scalar2=None, op0=p),
                      reads=reads, writes=writes)
        else:
            self.P.op(eng, lambda e, o=out, i=in0, a=s1, b=s2, p=op0, q=op1: e.tensor_scalar(out=o, in0=i, scalar1=a, scalar2=b, op0=p, op1=q),
                      reads=reads, writes=writes)

    def tt(self, out, in0, in1, op, reads, writes, eng="dve"):
        self.P.op(eng, lambda e, o=out, a=in0, b=in1, p=op: e.tensor_tensor(out=o, in0=a, in1=b, op=p), reads=reads, writes=writes)

    def stt(self, out, in0, scalar, in1, op0, op1, reads, writes, eng="dve"):
        self.P.op(eng, lambda e, o=out, a=in0, s=scalar, b=in1, p=op0, q=op1: e.scalar_tensor_tensor(out=o, in0=a, scalar=s, in1=b, op0=p, op1=q),
                  reads=reads, writes=writes)

    def red(self, out, in_, op, reads, writes):
        self.P.op("dve", lambda e, o=out, i=in_, p=op: e.tensor_reduce(out=o, in_=i, axis=AX.X, op=p), reads=reads, writes=writes)

    def cp(self, out, in_, reads, writes, eng="dve"):
        if eng == "act":
            self.act(out, in_, AF.Copy, reads, writes)
        else:
            self.P.op(eng, lambda e, o=out, i=in_: e.tensor_copy(out=o, in_=i), reads=reads, writes=writes)

    def rstd(self, out, ssq, n, key):
        self.act(out, ssq, AF.Sqrt, [key, ("c", "eps")], [key + ("r",)], scale=1.0 / n, bias=self.epsc[:, 0:1])
        self.P.op("dve", lambda e, o=out: e.reciprocal(out=o, in_=o), reads=[key + ("r",)], writes=[key + ("r",)])

    def bank(self, b, n=512, off=0):
        return self.ps[:, b * 512 + off:b * 512 + off + n]

    def bankb(self, b):
        return self.ps[:, b * 512:(b + 1) * 512].bitcast(BF16)

    def barrier(self):
        self.P.barrier(skip_groups=tuple(f"slab{i}_{q}" for i in range(2) for q in range(4)))

    def slab_load(self, W, r0, c0):
        i = self.slab_i % 2
        self.slab_i += 1
        buf = self.slab[i]
        src = W[r0:r0 + 4096, c0:c0 + 512].rearrange("(c p) n -> p c n", p=128)
        for q in range(4):
            self.P.dma("pool", f"slab{i}_{q}",
                       lambda e, o=buf[:, 8 * q:8 * q + 8, :], s=src[:, 8 * q:8 * q + 8, :]: e.dma_start(out=o, in_=s),
                       writes=[("slab", i, q)])
        return buf, (lambda kc, i=i: ("slab", i, kc // 8))

    def gemm_fm(self, slab, skey, actT, akey):
        for cc in range(4):
            for th in range(2):
                b = cc * 2 + th
                for kc in range(32):
                    self.mm(self.bank(b), slab[:, kc, cc * 128:(cc + 1) * 128], actT[:, kc, th * 512:(th + 1) * 512],
                            kc == 0, kc == 31, [skey(kc), akey(kc)], [("ps", b)])

    def gemm_tm(self, slab, skey, actT, akey, kcs=range(32), first=True, last=True):
        kcs = list(kcs)
        for kc in kcs:
            for tb in range(8):
                self.mm(self.bank(tb), actT[:, kc, tb * 128:(tb + 1) * 128], slab[:, kc, :],
                        first and kc == kcs[0], last and kc == kcs[-1], [skey(kc), akey(kc)], [("ps", tb)])

    def stage_modshard(self):
        self.local_reset()
        c2_d = self.dram("c_pc2", [128, 64], F32)
        wS = self.dram("w_adaS", [DM, 6144], F32)
        bS = self.dram("browS", [6144], F32)
        modS = self.dram("modS", [2, 6144], F32)
        csb = self.lsb([128, 64], F32)
        cact = self.lsb([128, 64], F32)
        rep = self.lsb([128, 32, 128], BF16)
        brow = self.lsb([128, 6144], F32)
        trow = self.lsb([128, 6144], F32)
        self.ld(csb[:], c2_d, "m_c", [("m", "csb")])
        self.ld(brow[:], bS.partition_broadcast(128), "m_br", [("m", "brow")])
        self.act(cact[:], csb[:], AF.Silu, [("m", "csb")], [("m", "cact")])
        for kc in range(32):
            for b in range(2):
                self.ts(rep[:, kc, b * 64:(b + 1) * 64], self.ones_b[:, 0:64], cact[:, kc * 2 + b:kc * 2 + b + 1], None, ALU.mult, None,
                        [("m", "cact"), ("c", "ones")], [("m", "rep", kc)])
        for j in range(12):
            slab, skey = self.slab_load(wS, 0, j * 512)
            for kc in range(32):
                self.mm(self.bank(j % 8), rep[:, kc, :], slab[:, kc, :], kc == 0, kc == 31, [skey(kc), ("m", "rep", kc)], [("ps", j % 8)])
            sl = slice(j * 512, (j + 1) * 512)
            self.tt(trow[:, sl], self.bank(j % 8), brow[:, sl], ALU.add, [("ps", j % 8), ("m", "brow")], [("m", "trow", j)])
        for b in range(2):
            self.st(modS[b:b + 1, :], trow[b * 64:b * 64 + 1, :], f"m_st{b}", [("m", "trow", j) for j in range(12)], writes=[("modS", b)])
        self.barrier()

    def stage_norm(self, l, kind):
        self.local_reset()
        has_y = kind != "pre"
        final = kind == "end" and l == DEPTH - 1
        has_next = not final
        if kind == "pre":
            xsrc = self.dram("x_c", [T, DM], F32); xsrc_key = None
        elif kind == "mid":
            xsrc = self.dram("x_c", [T, DM], F32) if l == 0 else self.dram(f"xb{l - 1}", [T, DM], F32)
            xsrc_key = None if l == 0 else f"xb{l - 1}"
            xdst = self.dram(f"xa{l}", [T, DM], F32); xdst_key = f"xa{l}"
        else:
            xsrc = self.dram(f"xa{l}", [T, DM], F32); xsrc_key = f"xa{l}"
            xdst = self.dram("out", [T, DM], F32) if final else self.dram(f"xb{l}", [T, DM], F32)
            xdst_key = "out" if final else f"xb{l}"
        if has_y:
            y_d = self.dram("y_d", [T, DM], F32)
            mr_d = self.dram(f"modraw{l}", [6, DM], F32)
            gr_d = self.dram(f"grow{l}", [4, DM], F32)
            gi = 2 if kind == "mid" else 5
            gate = self.lsb([128, DM], F32)
            yt = [self.lsb([128, DM], F32) for _ in range(2)]
            self.ld(gate[:], mr_d[gi, :].partition_broadcast(128), "n_g", [("n", "gate")])
            self.ld(yt[0][:], gr_d[1 if kind == "mid" else 3, :].partition_broadcast(128), "n_y0", [("n", "y", 0)])
            self.tt(gate[:], gate[:], yt[0][:], ALU.mult, [("n", "gate"), ("n", "y", 0)], [("n", "gate")])
        xt = [self.lsb([128, DM], F32)] * 2 if has_y else [self.lsb([128, DM], F32) for _ in range(2)]
        xk = (lambda s_: 0) if has_y else (lambda s_: s_ % 2)
        if has_next:
            nl, pi = (l, 0) if kind == "pre" else ((l, 2) if kind == "mid" else (l + 1, 0))
            mrn_d = self.dram(f"modraw{nl}", [6, DM], F32)
            grn_d = self.dram(f"grow{nl}", [4, DM], F32)
            mc = self.lsb([128, 4, 32], F32)
            rv = 0 if pi == 0 else 3
            srcs = [(pi, mrn_d[rv, :]), ((pi + 2) % 4, mrn_d[rv + 1, :]), ((pi + 3) % 4, grn_d[0 if pi == 0 else 2, :])]
            for i2, (slot, src) in enumerate(srcs):
                self.P.dma("sp", f"n_mc{i2}", lambda e, o=mc[:, slot, :], i=src.rearrange("(c p) -> p c", p=128):
                           e.dma_start(out=o, in_=i, allow_slow_non_contiguous=True), writes=[("n", "mcl", i2)])
            self.stt(mc[:, pi + 1, :], mc[:, (pi + 2) % 4, :], 1.0, mc[:, (pi + 3) % 4, :], ALU.add, ALU.mult,
                     [("n", "mcl", 1), ("n", "mcl", 2)], [("n", "mc")])
            self.ts(mc[:, pi, :], mc[:, pi, :], 1.0, None, ALU.mult, None, [("n", "mcl", 0), ("n", "mc")], [("n", "mc")])
            xn = self.lsb([128, DM], BF16)
        junk = self.lsb([128, 1024], BF16)
        nsm = self.lsb([128, 64], F32)

        def load_x(s):
            self.ld(xt[s % 2][:], xsrc[s * 128:(s + 1) * 128, :], f"n_x{xk(s)}", [("n", "x", xk(s))],
                    reads=[(xsrc_key, s)] if xsrc_key else [])

        def load_y(s):
            self.ld(yt[s % 2][:], y_d[s * 128:(s + 1) * 128, :], f"n_y{s % 2}", [("n", "y", s % 2)], reads=[("y_d", s, j) for j in range(8)])

        def ssq_rstd(src, skey, base, key):
            for q in range(4):
                self.act(junk[:], src[:, q * 1024:(q + 1) * 1024], AF.Square, [skey], [("n", "junk"), key + ("p", q)],
                         accum_out=nsm[:, base + q:base + q + 1])
            self.red(nsm[:, base + 4:base + 5], nsm[:, base:base + 4], ALU.add, [key + ("p", q) for q in range(4)], [key])
            self.rstd(nsm[:, base + 5:base + 6], nsm[:, base + 4:base + 5], DM, key)

        def a_y(s):
            x_ = xt[s % 2]; kx = ("n", "x", xk(s))
            if has_y:
                if s + 1 < NS:
                    load_y(s + 1)
                if s > 0:
                    load_x(s)
                y_ = yt[s % 2]; ky = ("n", "y", s % 2)
                base = (s % 2) * 16
                k1 = ("n", "ssq1", s % 2)
                ssq_rstd(y_, ky, base, k1)
                self.stt(y_[:], y_[:], nsm[:, base + 5:base + 6], gate[:], ALU.mult, ALU.mult, [ky, k1 + ("r",), ("n", "gate")], [ky])
                self.tt(x_[:], x_[:], y_[:], ALU.add, [kx, ky], [kx])
                self.st(xdst[s * 128:(s + 1) * 128, :], x_[:], f"n_xs{xk(s)}", [kx], writes=[(xdst_key, s)])
            elif s + 1 < NS:
                load_x(s + 1)

        def a_x(s):
            x_ = xt[s % 2]; kx = ("n", "x", xk(s))
            base = (s % 2) * 16 + 8
            k2 = ("n", "ssq2", s % 2)
            ssq_rstd(x_, kx, base, k2)
            self.ts(xn[:], x_[:], nsm[:, base + 5:base + 6], None, ALU.mult, None, [kx, k2 + ("r",)], [("n", "xn")])

        def b_t(s):
            for g in range(4):
                b = (s % 2) * 4 + g
                pb = self.bankb(b)
                for jj in range(8):
                    kc = g * 8 + jj
                    self.tr(pb[:, jj * 128:(jj + 1) * 128], xn[:, kc * 128:(kc + 1) * 128], self.ident_b[:],
                            [("n", "xn"), ("c", "ident")], [("ps", b)])
                for jj in range(8):
                    kc = g * 8 + jj
                    o = self.big[:, kc, s * 128:(s + 1) * 128]
                    i_ = pb[:, jj * 128:(jj + 1) * 128]
                    if g % 2 == 0:
                        self.act(o, i_, AF.Identity, [("ps", b), ("n", "mc")], [("big", kc, s)],
                                 scale=mc[:, pi + 1, kc:kc + 1], bias=mc[:, pi, kc:kc + 1])
                    else:
                        self.ts(o, i_, mc[:, pi + 1, kc:kc + 1], mc[:, pi, kc:kc + 1], ALU.mult, ALU.add,
                                [("ps", b), ("n", "mc")], [("big", kc, s)])

        load_x(0)
        if has_y:
            load_y(0)
        a_y(0)
        if has_next:
            a_x(0)
        for s in range(NS):
            if s + 1 < NS:
                a_y(s + 1)
            if has_next:
                b_t(s)
                if s + 1 < NS:
                    a_x(s + 1)
        self.barrier()

    def bigkey(self, kc):
        return ("bigall",)

    def stage_inproj(self, l):
        self.local_reset()
        P = self.P
        w_in = self.dram(f"w_in{l}", [DM, 10240], F32)
        wsT_d = self.dram(f"wsT{l}", [128, 8, 128], F32)
        tril_d = self.dram("trilT", [128, 128], F32)
        bs_d = self.dram(f"bs{l}", [1024], F32)
        vn_d = self.dram(f"vncol{l}", [128, 2, 16], F32)
        qT_d = self.dram(f"qT{l}", [2048, T], BF16)
        kT_d = self.dram(f"kT{l}", [2048, T], BF16)
        v_d = self.dram(f"v{l}", [T, 2048], BF16)
        gmT_d = self.dram(f"gmT{l}", [2048, T], BF16)

        vt = self.lsb([128, 8, 2048], BF16)
        uT = self.lsb([128, 4, 1024], BF16)
        stg = [self.lsb([128, 4096], BF16) for _ in range(2)]
        wsf = self.sb([128, 8, 128], F32, self.last_off)
        bsb = self.sb([128, 1024], F32, self.last_off + 4096)
        E = self.lsb([128, 16, 128], F32)
        wsb = self.lsb([128, 8, 128], BF16)
        tril = self.lsb([128, 128], F32)
        vn = self.lsb([128, 2, 16], F32)
        tmp = self.lsb([128, 512], F32)
        junk = self.lsb([128, 512], BF16)
        st1 = self.lsb([128, 32], F32)
        st2 = self.lsb([128, 32], F32)
        sv = self.lsb([128, 48], F32)
        bk = lambda kc: ("bigall",)

        self.ld(wsf[:], wsT_d, "i_ws", [("i", "wsf")])
        self.ld(tril[:], tril_d, "i_tr", [("i", "tril")])
        self.ld(bsb[:], bs_d.partition_broadcast(128), "i_bs", [("i", "bsb")])
        self.ld(vn[:], vn_d, "i_vn", [("i", "vn")])
        for g in range(8):
            self.tt(wsb[:, g, :], wsf[:, g, :], tril[:], ALU.mult, [("i", "wsf"), ("i", "tril")], [("i", "wsb", g)])
        for g in range(8):
            self.mm(self.bank(g // 4, 128, (g % 4) * 128), self.ones_b[:], wsb[:, g, :], True, True,
                    [("c", "ones"), ("i", "wsb", g)], [("ps", g // 4)])
        for cg in range(16):
            g = cg // 2
            self.stt(E[:, cg, :], self.bank(g // 4, 128, (g % 4) * 128), vn[:, 1, cg:cg + 1], bsb[:, g * 128:(g + 1) * 128],
                     ALU.mult, ALU.add, [("ps", g // 4), ("i", "vn"), ("i", "bsb")], [("i", "E", cg)])

        for j in range(4):
            slab, skey = self.slab_load(w_in, 0, 8192 + j * 512)
            self.gemm_tm(slab, skey, self.big, bk)
            for tb in range(8):
                self.act(vt[:, tb, j * 512:(j + 1) * 512], self.bank(tb), AF.Gelu, [("ps", tb)], [("i", "vt", tb, j), ("i", "st1", tb * 4 + j)],
                         accum_out=st1[:, tb * 4 + j:tb * 4 + j + 1])
                self.act(junk[:], vt[:, tb, j * 512:(j + 1) * 512], AF.Square, [("i", "vt", tb, j)], [("i", "junk"), ("i", "st2", tb * 4 + j)],
                         accum_out=st2[:, tb * 4 + j:tb * 4 + j + 1])
        k1 = [("i", "st1", i) for i in range(32)]
        k2 = [("i", "st2", i) for i in range(32)]
        ks = ("i", "sv")
        self.red(sv[:, 0:8], st1[:].rearrange("p (a b) -> p a b", b=4), ALU.add, k1, [ks])
        self.red(sv[:, 8:16], st2[:].rearrange("p (a b) -> p a b", b=4), ALU.add, k2, [ks])
        self.ts(sv[:, 0:8], sv[:, 0:8], 1.0 / 2048, None, ALU.mult, None, [ks], [ks])
        self.tt(sv[:, 16:24], sv[:, 0:8], sv[:, 0:8], ALU.mult, [ks], [ks])
        self.stt(sv[:, 8:16], sv[:, 8:16], 1.0 / 2048, sv[:, 16:24], ALU.mult, ALU.subtract, [ks], [ks])
        self.act(sv[:, 8:16], sv[:, 8:16], AF.Sqrt, [ks, ("c", "eps")], [ks], scale=1.0, bias=self.epsc[:, 0:1])
        self.P.op("dve", lambda e, o=sv[:, 8:16]: e.reciprocal(out=o, in_=o), reads=[ks], writes=[ks])
        for tb in range(8):
            self.ts(vt[:, tb, :], vt[:, tb, :], sv[:, tb:tb + 1], sv[:, 8 + tb:9 + tb], ALU.subtract, ALU.mult,
                    [ks] + [("i", "vt", tb, j) for j in range(4)], [("i", "vtn", tb)] + [("i", "vt", tb, j) for j in range(4)])

        nst = 0
        for j in range(4):
            slab, skey = self.slab_load(w_in, 0, 6144 + j * 512)
            self.gemm_fm(slab, skey, self.big, bk)
            for cc in range(4):
                for th in range(2):
                    b = cc * 2 + th
                    self.act(uT[:, cc, th * 512:(th + 1) * 512], self.bank(b), AF.Gelu, [("ps", b)], [("i", "uT", cc, th)])
            sg_ = stg[nst % 2]; ksg = ("i", "stg", nst % 2); nst += 1
            for cc in range(4):
                cg = j * 4 + cc
                g = cg // 2
                for tb in range(8):
                    b = cc * 2 + tb // 4
                    self.mm(self.bank(b, 128, (tb % 4) * 128), vt[:, tb, cg * 128:(cg + 1) * 128], wsb[:, g, :], True, True,
                            [("i", "vtn", tb), ("i", "wsb", g)], [("ps", b)])
                for tb in range(8):
                    b = cc * 2 + tb // 4
                    self.stt(tmp[:, 0:128], self.bank(b, 128, (tb % 4) * 128), vn[:, 0, cg:cg + 1], E[:, cg, :], ALU.mult, ALU.add,
                             [("ps", b), ("i", "vn"), ("i", "E", cg)], [("i", "tmp")])
                    self.tt(sg_[:, cc * 1024 + tb * 128:cc * 1024 + (tb + 1) * 128], tmp[:, 0:128], uT[:, cc, tb * 128:(tb + 1) * 128], ALU.mult,
                            [("i", "tmp"), ("i", "uT", cc, tb // 4)], [ksg])
            self.st(gmT_d[j * 512:(j + 1) * 512, :].rearrange("(c p) t -> p c t", p=128), sg_[:].rearrange("p (c t) -> p c t", c=4),
                    f"i_st{(nst - 1) % 2}", [ksg], writes=[(f"gmT{l}", j)])
        for nm, c0, dst in (("qT", 0, qT_d), ("kT", 2048, kT_d)):
            for j in range(4):
                slab, skey = self.slab_load(w_in, 0, c0 + j * 512)
                self.gemm_fm(slab, skey, self.big, bk)
                sg_ = stg[nst % 2]; ksg = ("i", "stg", nst % 2); nst += 1
                for cc in range(4):
                    for th in range(2):
                        b = cc * 2 + th
                        self.cp(sg_[:, cc * 1024 + th * 512:cc * 1024 + (th + 1) * 512], self.bank(b), [("ps", b)], [ksg],
                                eng="act" if b % 2 == 0 else "dve")
                self.st(dst[j * 512:(j + 1) * 512, :].rearrange("(c p) t -> p c t", p=128), sg_[:].rearrange("p (c t) -> p c t", c=4),
                        f"i_st{(nst - 1) % 2}", [ksg], writes=[(f"{nm}{l}", j)])
        for j in range(4):
            slab, skey = self.slab_load(w_in, 0, 4096 + j * 512)
            self.gemm_tm(slab, skey, self.big, bk)
            sg_ = stg[nst % 2]; ksg = ("i", "stg", nst % 2); nst += 1
            for tb in range(8):
                self.cp(sg_[:, tb * 512:(tb + 1) * 512], self.bank(tb), [("ps", tb)], [ksg], eng="act" if tb % 2 == 0 else "dve")
            self.st(v_d[:, j * 512:(j + 1) * 512].rearrange("(b p) n -> p b n", p=128), sg_[:].rearrange("p (b n) -> p b n", b=8),
                    f"i_st{(nst - 1) % 2}", [ksg], writes=[(f"v{l}", j)])
        self.barrier()

    def stage_attn(self, l):
        self.local_reset()
        lam_init = 0.8 - 0.6 * math.exp(-0.3 * l)
        qT_d = self.dram(f"qT{l}", [2048, T], BF16)
        kT_f = self.dram(f"kTf{l}", [2048, 4096], BF16)
        v_f = self.dram(f"vf{l}", [4096, 2048], BF16)
        braw_d = self.dram("braw", [8, NS, 128, 640], F32)
        mask_d = self.dram("maskc", [NS, 128, 640], F32)
        b31_d = self.dram("b31", [8], F32)
        rbT_d = self.dram("rbT", [256], F32)
        lamv_d = self.dram(f"lamv{l}", [4, 128], F32)
        sg_d = self.dram(f"subg{l}", [256], F32)

        KT = [self.sb([128, 2, 4096], BF16, BIG_OFF + 32768), self.lsb([128, 2, 4096], BF16)]
        V = self.sb([128, 32, 256], BF16, BIG_OFF + 32768 + 16384)
        Q = [self.lsb([128, 2, 1024], BF16) for _ in range(2)]
        p = [[self.lsb([128, 4096], BF16) for _ in range(2)] for _ in range(2)]
        ksq = [self.lsb([128, 512], BF16) for _ in range(2)]
        qsq = self.lsb([128, 2, 128], BF16)
        ach = [self.lsb([128, 512], BF16) for _ in range(2)]
        aT = [self.lsb([128, 512], BF16) for _ in range(2)]
        braw = [self.lsb([128, 640], F32)] * 2
        mask = [self.lsb([128, 640], F32)] * 2
        bp32 = braw[0]
        lv = self.sb([128, 512], F32, self.last_off - 2560)
        bpb = [self.lsb([128, 640], BF16) for _ in range(2)]
        sgr = self.lsb([128, 256], F32)
        b31 = self.lsb([128, 8], F32)
        on = self.lsb([128, 256], BF16)
        junk = on
        osb = self.lsb([128, 256], F32)
        smc = self.lsb([128, 16], F32)
        nbm = self.lsb([128, 8], F32)
        kmx = [self.lsb([128, 24], F32) for _ in range(2)]
        sms = [self.lsb([128, 64], F32) for _ in range(2)]
        self.ld(lv[:], lamv_d.rearrange("a n -> (a n)").partition_broadcast(128), "a_lv", [("a", "lv")])
        self.ld(sgr[:], sg_d.partition_broadcast(128), "a_sg", [("a", "sgr")])
        self.ld(b31[:], b31_d.partition_broadcast(128), "a_b31", [("a", "b31")])
        kl = ("a", "lam")
        self.tt(lv[:, 0:128], lv[:, 0:128], lv[:, 128:256], ALU.mult, [("a", "lv")], [("a", "lv")])
        self.tt(lv[:, 256:384], lv[:, 256:384], lv[:, 384:512], ALU.mult, [("a", "lv")], [("a", "lv")])
        self.red(smc[:, 0:1], lv[:, 0:128], ALU.add, [("a", "lv")], [kl])
        self.red(smc[:, 1:2], lv[:, 256:384], ALU.add, [("a", "lv")], [kl])
        self.act(smc[:, 2:4], smc[:, 0:2], AF.Exp, [kl], [kl])
        self.tt(smc[:, 4:5], smc[:, 3:4], smc[:, 2:3], ALU.subtract, [kl], [kl])
        self.ts(smc[:, 4:5], smc[:, 4:5], -lam_init, None, ALU.add, None, [kl], [kl])
        self.ts(sgr[:], sgr[:], 1.0 - lam_init, None, ALU.mult, None, [("a", "sgr")], [("a", "sgr")])
        neglam = smc[:, 4:5]
        self.ld(mask[0][:, 0:256], rbT_d.partition_broadcast(128), "a_rb", [("a", "mask", 0)])
        self.red(nbm[:], mask[0][:, 0:256].rearrange("p (h b) -> p h b", b=32), ALU.max, [("a", "mask", 0)], [("a", "nbm")])
        self.tt(nbm[:], nbm[:], b31[:], ALU.subtract, [("a", "nbm"), ("a", "b31")], [("a", "nbm")])
        self.ts(nbm[:], nbm[:], 0.0, -1.0, ALU.max, ALU.mult, [("a", "nbm")], [("a", "nbm")])
        cnt = {"sb": 0, "tb": 0, "at": 0, "ch": 0, "ks": 0}
        kmg = [None]

        tasks = [(h, s) for h in range(8) for s in range(NS)]

        def load_qk(h):
            hp = h % 2
            self.ld(KT[hp][:], kT_f[h * 256:(h + 1) * 256, :].rearrange("(c p) k -> p c k", p=128), f"a_k{hp}", [("a", "KT", hp)])
            self.ld(Q[hp][:], qT_d[h * 256:(h + 1) * 256, :].rearrange("(c p) t -> p c t", p=128), f"a_q{hp}", [("a", "Q", hp)],
                    reads=[(f"qT{l}", j) for j in range(4)])

        def kmaxprep(h):
            hp = h % 2
            km = kmx[hp]

            def sq(i):
                c, ch = divmod(i, 8)
                self.tt(ksq[i % 2][:], KT[hp][:, c, ch * 512:(ch + 1) * 512], KT[hp][:, c, ch * 512:(ch + 1) * 512], ALU.mult,
                        [("a", "KT", hp)], [("a", "ksq", i % 2)])

            def mx(i):
                c, ch = divmod(i, 8)
                self.red(km[:, c * 8 + ch:c * 8 + ch + 1], self.bank(3), ALU.max, [("ps", 3)], [("a", "kmp", hp, c, ch)])

            sq(0)
            for i in range(16):
                if i >= 1:
                    mx(i - 1)
                if i + 1 < 16:
                    sq(i + 1)
                self.mm(self.bank(3), self.ones_b[:], ksq[i % 2][:], True, True, [("c", "ones"), ("a", "ksq", i % 2)], [("ps", 3)])
                yield
            mx(15)
            for c in range(2):
                self.red(km[:, 16 + c:17 + c], km[:, c * 8:c * 8 + 8], ALU.max, [("a", "kmp", hp, c, ch) for ch in range(8)], [("a", "kmax", hp)])

        def prep(ti, h, s):
            par = ti % 2
            hp = h % 2
            sm = sms[par]
            km = kmx[hp]
            self.ld(braw[par][:], braw_d[h, s], "a_br", [("a", "braw", 0)])
            self.ld(mask[par][:], mask_d[s], "a_mk", [("a", "mask", 0)])
            self.ts(braw[par][:], braw[par][:], b31[:, h:h + 1], 1.0 / SCALE, ALU.subtract, ALU.mult, [("a", "braw", 0), ("a", "b31")], [("a", "braw", 0)])
            self.tt(bpb[par][:], braw[par][:], mask[par][:], ALU.add, [("a", "braw", 0), ("a", "mask", 0)], [("a", "bpb", par)])
            kb_ = ("a", "bnd", par)
            yield
            self.tt(qsq[:], Q[hp][:, :, s * 128:(s + 1) * 128], Q[hp][:, :, s * 128:(s + 1) * 128], ALU.mult, [("a", "Q", hp)], [("a", "qsq")])
            tbq = 4 + cnt["tb"] % 2
            cnt["tb"] += 1
            pb3 = self.bankb(tbq)
            for c in range(2):
                self.tr(pb3[:, c * 128:(c + 1) * 128], qsq[:, c, :], self.ident_b[:], [("a", "qsq"), ("c", "ident")], [("ps", tbq)])
            self.red(sm[:, 16:18], pb3[:, 0:256].rearrange("p (c d) -> p c d", c=2), ALU.add, [("ps", tbq)], [kb_])
            self.tt(sm[:, 16:18], sm[:, 16:18], km[:, 16:18], ALU.mult, [kb_, ("a", "kmax", hp)], [kb_])
            yield
            self.act(sm[:, 16:18], sm[:, 16:18], AF.Ln, [kb_], [kb_])
            self.act(sm[:, 16:18], sm[:, 16:18], AF.Exp, [kb_], [kb_], scale=0.5)
            yield
            self.ts(sm[:, 18:20], sm[:, 16:18], -SCALE * 1.02, nbm[:, h:h + 1], ALU.mult, ALU.add, [kb_, ("a", "nbm")], [("a", "negm", par, 0), ("a", "negm", par, 1)])

        def gen_ab(ti, h, s):
            par = ti % 2
            hp = h % 2
            sm = sms[par]
            if ti == 0:
                load_qk(0)
                for _ in kmaxprep(0):
                    pass
                for _ in prep(0, h, s):
                    pass
            if s == 4 and h + 1 < 8:
                load_qk(h + 1)
            if s == 5 and h + 1 < 8:
                kmg[0] = kmaxprep(h + 1)
            if ti + 1 < len(tasks):
                h2, s2 = tasks[ti + 1]
                if s2 == 0 and kmg[0] is not None:
                    for _ in kmg[0]:
                        pass
                    kmg[0] = None
                pg = prep(ti + 1, h2, s2)
            else:
                pg = iter(())
            nk = nkeys(s); w = min(5, nk); w0 = nk - w; nch = nk // 4
            yield
            for c in range(2):
                for ch in range(nch):
                    b = cnt["sb"] % 3
                    cnt["sb"] += 1
                    lo, hi = max(ch * 512, w0 * 128), (ch + 1) * 512
                    hasb = hi > lo
                    self.mm(self.bank(b), Q[hp][:, c, s * 128:(s + 1) * 128], KT[hp][:, c, ch * 512:(ch + 1) * 512], True, not hasb,
                            [("a", "Q", hp), ("a", "KT", hp)], [("ps", b)])
                    if hasb:
                        self.mm(self.bank(b, hi - lo, lo - ch * 512), self.ident_b[:], bpb[par][:, lo - w0 * 128:hi - w0 * 128], False, True,
                                [("c", "ident"), ("a", "bpb", par)], [("ps", b)])
                    self.act(p[par][c][:, ch * 512:(ch + 1) * 512], self.bank(b), AF.Exp, [("ps", b), ("a", "negm", par, c)],
                             [("a", "p", par, c, ch), ("a", "lp", par, c, ch)], scale=SCALE, bias=sm[:, 18 + c:19 + c],
                             accum_out=sm[:, 20 + c * 8 + ch:21 + c * 8 + ch])
                    if kmg[0] is not None:
                        next(kmg[0], None)
                    next(pg, None)
                    yield
                self.red(sm[:, 36 + c:37 + c], sm[:, 20 + c * 8:20 + c * 8 + nch], ALU.add, [("a", "lp", par, c, ch) for ch in range(nch)], [("a", "l", par, c)])
            for _ in pg:
                pass
            kr = ("a", "r", par)
            self.P.op("dve", lambda e, o=sm[:, 38:40], i=sm[:, 36:38]: e.reciprocal(out=o, in_=i), reads=[("a", "l", par, 0), ("a", "l", par, 1)], writes=[kr])
            self.stt(sm[:, 39:40], sm[:, 39:40], neglam, sm[:, 36:37], ALU.mult, ALU.mult, [kr, kl, ("a", "l", par, 0)], [kr])
            yield

        def gen_c(ti, h, s):
            par = ti % 2
            sm = sms[par]
            kr = ("a", "r", par)
            if s == 0:
                self.ld(V[:], v_f[:, h * 256:(h + 1) * 256].rearrange("(b p) d -> p b d", p=128), "a_v", [("a", "V")])
            nk = nkeys(s); nch = nk // 4
            ob = 6 + ti % 2
            pend = None

            def pv(g, ai):
                for jj in range(4):
                    kb = g * 4 + jj
                    self.mm(self.bank(ob, 256), aT[ai][:, jj * 128:(jj + 1) * 128], V[:, kb, :], kb == 0, kb == nk - 1,
                            [("a", "aT", ai), ("a", "V")], [("ps", ob)])

            for g in range(nch):
                i = cnt["ch"] % 2
                cnt["ch"] += 1
                sl = slice(g * 512, (g + 1) * 512)
                self.stt(ach[i][:], p[par][1][:, sl], sm[:, 39:40], p[par][0][:, sl], ALU.mult, ALU.add,
                         [kr, ("a", "p", par, 0, g), ("a", "p", par, 1, g)], [("a", "ach", i)])
                tb = 4 + cnt["tb"] % 2
                cnt["tb"] += 1
                pb = self.bankb(tb)
                for jj in range(4):
                    self.tr(pb[:, jj * 128:(jj + 1) * 128], ach[i][:, jj * 128:(jj + 1) * 128], self.ident_b[:], [("a", "ach", i), ("c", "ident")], [("ps", tb)])
                ai = cnt["at"] % 2
                cnt["at"] += 1
                self.cp(aT[ai][:, 0:512], pb[:, 0:512], [("ps", tb)], [("a", "aT", ai)], eng="dve")
                if pend is not None:
                    pv(*pend)
                pend = (g, ai)
                yield
            pv(*pend)
            yield
            ko = ("a", "o", par)
            self.ts(osb[:], self.bank(ob, 256), sm[:, 38:39], None, ALU.mult, None, [("ps", ob), kr], [("a", "osb")])
            self.P.op("dve", lambda e, o=junk[:], a_=osb[:], acc=sm[:, 48:49]: e.scalar_tensor_tensor(out=o, in0=a_, scalar=1.0, in1=a_, op0=ALU.mult, op1=ALU.mult, accum_out=acc),
                      reads=[("a", "osb")], writes=[("a", "on"), ko])
            yield
            self.act(sm[:, 49:50], sm[:, 48:49], AF.Ln, [ko, ("c", "eps")], [ko + ("r",)], scale=1.0 / 256, bias=self.epsc[:, 0:1])
            self.act(sm[:, 49:50], sm[:, 49:50], AF.Exp, [ko + ("r",)], [ko + ("r",)], scale=-0.5)
            yield
            self.stt(on[:], osb[:], sm[:, 49:50], sgr[:], ALU.mult, ALU.mult, [("a", "osb"), ko + ("r",), ("a", "sgr")], [("a", "on")])
            tb = 4 + cnt["tb"] % 2
            cnt["tb"] += 1
            pb = self.bankb(tb)
            for jj in range(2):
                self.tr(pb[:, jj * 128:(jj + 1) * 128], on[:, jj * 128:(jj + 1) * 128], self.ident_b[:], [("a", "on"), ("c", "ident")], [("ps", tb)])
            yield
            for jj in range(2):
                self.cp(self.big[:, h * 2 + jj, s * 128:(s + 1) * 128], pb[:, jj * 128:(jj + 1) * 128], [("ps", tb)], [("bigw", h, jj, s)], eng="act")
            yield

        for _ in gen_ab(0, *tasks[0]):
            pass
        for ti, (h, s) in enumerate(tasks):
            gc = gen_c(ti, h, s)
            n_c = nkeys(s) // 4 + 5
            if ti + 1 < len(tasks):
                ga = gen_ab(ti + 1, *tasks[ti + 1])
                n_a = 2 * (nkeys(tasks[ti + 1][1]) // 4) + 2
            else:
                ga, n_a = iter(()), 0
            steps = max(n_a, n_c)
            ia = ic = 0
            for k in range(1, steps + 1):
                while ia < (k * n_a) // steps:
                    next(ga, None); ia += 1
                while ic < (k * n_c) // steps:
                    next(gc, None); ic += 1
            for _ in ga:
                pass
            for _ in gc:
                pass
        self.barrier()

    def evac_y(self, j, nst):
        y_d = self.dram("y_d", [T, DM], F32)
        sg_ = self.ystg[nst % 2]; ksg = ("o", "ystg", nst % 2)
        for tb in range(8):
            self.cp(sg_[:, tb * 512:(tb + 1) * 512], self.bank(tb), [("ps", tb)], [ksg], eng="act" if tb % 2 == 0 else "dve")
        self.st(y_d[:, j * 512:(j + 1) * 512].rearrange("(b p) n -> p b n", p=128), sg_[:].rearrange("p (b n) -> p b n", b=8),
                f"o_st{nst % 2}", [ksg], writes=[("y_d", s, j) for s in range(8)], eng="act")

    def stage_outproj(self, l):
        self.local_reset()
        w_out = self.dram(f"w_out{l}", [DM, DM], F32)
        gmT_d = self.dram(f"gmT{l}", [2048, T], BF16)
        self.ystg = [self.lsb([128, 4096], F32) for _ in range(2)]
        for hf in range(2):
            self.ld(self.big[:, 16 + hf * 8:24 + hf * 8, :], gmT_d[hf * 1024:(hf + 1) * 1024, :].rearrange("(c p) t -> p c t", p=128),
                    f"o_gm{hf}", [("o", "gm", hf)], reads=[(f"gmT{l}", j) for j in range(4)])
        ak = lambda kc: ("bigall",) if kc < 16 else ("o", "gm", (kc - 16) // 8)
        for j in range(8):
            slab, skey = self.slab_load(w_out, 0, j * 512)
            self.gemm_tm(slab, skey, self.big, ak)
            self.evac_y(j, j)
        self.barrier()

    def stage_mlp1(self, l):
        self.local_reset()
        w1 = self.dram(f"w_1{l}", [DM, DFF], F32)
        hid_d = self.dram("hid_d", [DFF, T], BF16)
        stg = [self.lsb([128, 4096], BF16) for _ in range(2)]
        r = [self.lsb([128, 512], BF16) for _ in range(2)]
        bk = lambda kc: ("bigall",)
        ri = 0
        for j in range(32):
            slab, skey = self.slab_load(w1, 0, j * 512)
            self.gemm_fm(slab, skey, self.big, bk)
            sg_ = stg[j % 2]; ksg = ("h", "stg", j % 2)
            for cc in range(4):
                for th in range(2):
                    b = cc * 2 + th
                    r_ = r[ri % 2]; kr = ("h", "r", ri % 2); ri += 1
                    self.act(r_[:], self.bank(b), AF.Relu, [("ps", b)], [kr])
                    self.tt(sg_[:, cc * 1024 + th * 512:cc * 1024 + (th + 1) * 512], self.bank(b), r_[:], ALU.mult, [("ps", b), kr], [ksg])
            self.st(hid_d[j * 512:(j + 1) * 512, :].rearrange("(c p) t -> p c t", p=128), sg_[:].rearrange("p (c t) -> p c t", c=4),
                    f"h_st{j % 2}", [ksg], writes=[("hid", j)])
        self.barrier()

    def stage_mlp2(self, l):
        self.local_reset()
        w2 = self.dram(f"w_2{l}", [DFF, DM], F32)
        hid_d = self.dram("hid_d", [DFF, T], BF16)
        self.ystg = [self.lsb([128, 4096], F32) for _ in range(2)]
        for cg in range(8):
            for ks in range(4):
                slab, skey = self.slab_load(w2, ks * 4096, cg * 512)
                for hf in range(2):
                    r0 = ks * 4096 + hf * 2048
                    self.ld(self.big[:, hf * 16:(hf + 1) * 16, :], hid_d[r0:r0 + 2048, :].rearrange("(c p) t -> p c t", p=128),
                            f"g_h{hf}", [("g", "hid", hf)], reads=[("hid", r0 // 512 + i) for i in range(4)])
                    self.gemm_tm(slab, skey, self.big, lambda kc, hf=hf: ("g", "hid", hf), kcs=range(hf * 16, hf * 16 + 16),
                                 first=(ks == 0 and hf == 0), last=(ks == 3 and hf == 1))
            self.evac_y(cg, cg)
        self.barrier()

    def finish(self):
        self.P.barrier(skip_queues=())
        self.P.emit()
        return self.nc


def _col(v, n):
    return np.ascontiguousarray(v.reshape(n, 128).T)


_LAUNCHES = [
    dict(stages=[("modshard",)], ins=["ident_bf", "c_pc2", "w_adaS", "browS"], outs=["modS"]),
    dict(stages=[("norm", 0, "pre"), ("inproj", 0)],
         ins=["ident_bf", "modraw0", "grow0", "x_c", "w_in0", "wsT0", "trilT", "bs0", "vncol0"],
         outs=["qT0", "kT0", "v0", "gmT0"]),
    dict(stages=[("attn", 0), ("outproj", 0), ("norm", 0, "mid"), ("mlp1", 0), ("mlp2", 0), ("norm", 0, "end"), ("inproj", 1)],
         ins=["ident_bf", "qT0", "kTf0", "vf0", "braw", "maskc", "b31", "rbT", "lamv0", "subg0", "w_out0", "gmT0", "modraw0", "grow0", "x_c",
              "w_10", "w_20", "modraw1", "grow1", "w_in1", "wsT1", "trilT", "bs1", "vncol1"],
         outs=["xb0", "qT1", "kT1", "v1", "gmT1"]),
    dict(stages=[("attn", 1), ("outproj", 1), ("norm", 1, "mid"), ("mlp1", 1), ("mlp2", 1), ("norm", 1, "end")],
         ins=["ident_bf", "qT1", "kTf1", "vf1", "braw", "maskc", "b31", "rbT", "lamv1", "subg1", "w_out1", "gmT1", "modraw1", "grow1", "xb0",
              "w_11", "w_21"],
         outs=["out"]),
]


def _build(cfg):
    B = Builder(cfg["ins"], cfg["outs"])
    for stg in cfg["stages"]:
        getattr(B, "stage_" + stg[0])(*stg[1:])
    nc = B.finish()
    return nc, B.used_in, B.used_out


def kernel(x, c, rel_bias, w_ada, b_ada, pre_mix_g, w_in, lambda_q1, lambda_k1, lambda_q2, lambda_k2,
           subln_g, v_norm_g, v_norm_b, w_s, b_s, w_out, post_mix_g, pre_mlp_g, w_1, w_2, post_mlp_g):
    import ml_dtypes
    f = lambda a: np.asarray(a, dtype=np.float32)
    x, c, rel_bias, w_ada, b_ada, w_in, w_out, w_1, w_2 = map(f, (x, c, rel_bias, w_ada, b_ada, w_in, w_out, w_1, w_2))
    pre_mix_g, post_mix_g, pre_mlp_g, post_mlp_g = map(f, (pre_mix_g, post_mix_g, pre_mlp_g, post_mlp_g))
    lambda_q1, lambda_k1, lambda_q2, lambda_k2, subln_g, v_norm_g, v_norm_b, w_s, b_s = map(
        f, (lambda_q1, lambda_k1, lambda_q2, lambda_k2, subln_g, v_norm_g, v_norm_b, w_s, b_s))

    cores = list(range(NCORE))
    bj = [(cid // 4, cid % 4) for cid in cores]
    shared = {"ident_bf": np.eye(128, dtype=np.float32).astype(ml_dtypes.bfloat16),
              "trilT": np.ascontiguousarray(np.triu(np.ones((128, 128), np.float32))),
              "b31": np.ascontiguousarray(rel_bias[31, :]), "rbT": np.ascontiguousarray(rel_bias.T.reshape(-1))}
    for l in range(DEPTH):
        shared[f"grow{l}"] = np.ascontiguousarray(np.stack([pre_mix_g[l], post_mix_g[l], pre_mlp_g[l], post_mlp_g[l]], axis=0))
        shared[f"w_in{l}"] = w_in[l]
        shared[f"wsT{l}"] = np.ascontiguousarray(w_s[l].transpose(2, 0, 1))
        shared[f"bs{l}"] = np.ascontiguousarray(b_s[l].reshape(-1))
        shared[f"vncol{l}"] = np.ascontiguousarray(np.stack([_col(v_norm_g[l], 16), _col(v_norm_b[l], 16)], axis=1))
        shared[f"lamv{l}"] = np.ascontiguousarray(np.stack([lambda_q1[l], lambda_k1[l], lambda_q2[l], lambda_k2[l]], axis=0))
        shared[f"subg{l}"] = subln_g[l]
        shared[f"w_out{l}"] = w_out[l]
        shared[f"w_1{l}"] = w_1[l]
        shared[f"w_2{l}"] = w_2[l]
    percore = [dict() for _ in cores]
    for cid, (b, j) in zip(cores, bj):
        d = percore[cid]
        d["x_c"] = np.ascontiguousarray(np.concatenate([x[b, qblock(j, s) * 128:(qblock(j, s) + 1) * 128, :] for s in range(NS)], axis=0))
        d["c_pc2"] = np.ascontiguousarray(np.stack([_col(c[0], 32), _col(c[1], 32)], axis=2).reshape(128, 64))
        sl_ = slice(cid * 3072, (cid + 1) * 3072)
        d["w_adaS"] = np.ascontiguousarray(np.concatenate([w_ada[0][:, sl_], w_ada[1][:, sl_]], axis=1))
        d["browS"] = np.ascontiguousarray(np.concatenate([b_ada[0][sl_], b_ada[1][sl_]]))
        braw = np.zeros((8, NS, 128, 640), np.float32)
        mask = np.zeros((NS, 128, 640), np.float32)
        for s in range(NS):
            nk = nkeys(s); w = min(5, nk); w0 = nk - w
            qpos = qblock(j, s) * 128 + np.arange(128)[:, None]
            kpos = w0 * 128 + np.arange(w * 128)[None, :]
            n = qpos - kpos
            bucket = t5_bucket_np(n)
            mask[s, :, :w * 128] = np.where(n < 0, np.float32(MASKV), np.float32(0))
            braw[:, s, :, :w * 128] = rel_bias[bucket, :].transpose(2, 0, 1)
        d["braw"] = braw
        d["maskc"] = mask

    def run(cfg, extra):
        nc, used_in, used_out = _build(cfg)
        in_maps = []
        for cid in cores:
            m = {}
            for name in used_in:
                if name in extra[cid]:
                    m[name] = extra[cid][name]
                elif name in percore[cid]:
                    m[name] = percore[cid][name]
                else:
                    m[name] = shared[name]
            in_maps.append(m)
        res = run_bass_kernel_spmd(nc, in_maps, core_ids=cores)
        return [r for r in res.results]

    def gather_kv(res, l):
        ext = [dict() for _ in cores]
        for b in range(2):
            kT = np.zeros((2048, 4096), ml_dtypes.bfloat16)
            vv = np.zeros((4096, 2048), ml_dtypes.bfloat16)
            for j in range(4):
                r = res[b * 4 + j]
                for s in range(NS):
                    qb = qblock(j, s)
                    kT[:, qb * 128:(qb + 1) * 128] = r[f"kT{l}"][:, s * 128:(s + 1) * 128]
                    vv[qb * 128:(qb + 1) * 128, :] = r[f"v{l}"][s * 128:(s + 1) * 128, :]
            for j in range(4):
                ext[b * 4 + j][f"kTf{l}"] = kT
                ext[b * 4 + j][f"vf{l}"] = vv
        return ext

    r0 = run(_LAUNCHES[0], [dict() for _ in cores])
    modraw = np.zeros((2, DEPTH, 6 * DM), np.float32)
    for cid in cores:
        for b in range(2):
            for l in range(DEPTH):
                modraw[b, l, cid * 3072:(cid + 1) * 3072] = r0[cid]["modS"][b, l * 3072:(l + 1) * 3072]
    base = [dict() for _ in cores]
    for cid, (b, j) in zip(cores, bj):
        for l in range(DEPTH):
            base[cid][f"modraw{l}"] = np.ascontiguousarray(modraw[b, l].reshape(6, DM))
    r1 = run(_LAUNCHES[1], base)
    ext = gather_kv(r1, 0)
    for cid in cores:
        ext[cid].update(base[cid])
        for k in ("qT0", "gmT0"):
            ext[cid][k] = r1[cid][k]
    r2 = run(_LAUNCHES[2], ext)
    ext = gather_kv(r2, 1)
    for cid in cores:
        ext[cid].update(base[cid])
        for k in ("qT1", "gmT1", "xb0"):
            ext[cid][k] = r2[cid][k]
    r3 = run(_LAUNCHES[3], ext)
    out = np.zeros((2, 4096, DM), np.float32)
    for cid, (b, j) in zip(cores, bj):
        for s in range(NS):
            qb = qblock(j, s)
            out[b, qb * 128:(qb + 1) * 128, :] = r3[cid]["out"][s * 128:(s + 1) * 128, :]
    return out
```
